# Optimizing a Trainium2 kernel written in Bass

```python
import jax, jax.numpy as jnp
from jax import lax
import numpy as np

D_MODEL = 4096
BATCH = 8
SEQ = 2048
DEPTH = 2

HEAD_DIM = 128
N_HEADS = D_MODEL // HEAD_DIM
N_KV_HEADS_B = max(1, N_HEADS // 4)
GROUP_B = N_HEADS // N_KV_HEADS_B
D_FF = ((8 * D_MODEL // 3 + 255) // 256) * 256
CONV_WIDTH = 3
BLOCK_Q = 128
N_A_LAYERS = DEPTH // 2
N_B_LAYERS = DEPTH - N_A_LAYERS
RMS_EPS = 1e-6

kernel_name = "yoco_stickbreak_fox_convglu"


def rms_norm(x, g):
    xf = x.astype(jnp.float32)
    y = xf * lax.rsqrt(jnp.mean(xf * xf, axis=-1, keepdims=True) + RMS_EPS)
    return (y * g.astype(jnp.float32)).astype(x.dtype)


def to_heads(t, n_heads):
    b, s, _ = t.shape
    return t.reshape(b, s, n_heads, HEAD_DIM).transpose(0, 2, 1, 3)


def merge_heads(t):
    b, n, s, dh = t.shape
    return t.transpose(0, 2, 1, 3).reshape(b, s, n * dh)


def stick_breaking_attention(q, k, v):
    seq = q.shape[2]
    scale = HEAD_DIM ** -0.5
    outs = []
    for i in range(seq // BLOCK_Q):
        q0 = i * BLOCK_Q
        kend = q0 + BLOCK_Q
        z = jnp.einsum('bhqd,bhkd->bhqk', q[:, :, q0:kend], k[:, :, :kend]).astype(jnp.float32) * scale
        t_pos = q0 + jnp.arange(BLOCK_Q)[:, None]
        s_pos = jnp.arange(kend)[None, :]
        strict = s_pos < t_pos
        log_1mb = jnp.where(strict, jax.nn.log_sigmoid(-z), 0.0)
        between = lax.cumsum(log_1mb, axis=3, reverse=True) - log_1mb
        w = jnp.where(strict, jnp.exp(jax.nn.log_sigmoid(z) + between), 0.0)
        outs.append(jnp.einsum('bhqk,bhkd->bhqd', w.astype(v.dtype), v[:, :, :kend]))
    return jnp.concatenate(outs, axis=2)


def forgetting_attention(q, k, v, c):
    seq = q.shape[3]
    scale = HEAD_DIM ** -0.5
    outs = []
    for i in range(seq // BLOCK_Q):
        q0 = i * BLOCK_Q
        kend = q0 + BLOCK_Q
        logits = jnp.einsum('bhgqd,bhkd->bhgqk', q[:, :, :, q0:kend], k[:, :, :kend]).astype(jnp.float32) * scale
        logits = logits + c[:, :, :, q0:kend, None] - c[:, :, :, None, :kend]
        t_pos = q0 + jnp.arange(BLOCK_Q)[:, None]
        s_pos = jnp.arange(kend)[None, :]
        logits = jnp.where(s_pos <= t_pos, logits, -jnp.inf)
        p = jax.nn.softmax(logits, axis=-1)
        outs.append(jnp.einsum('bhgqk,bhkd->bhgqd', p.astype(v.dtype), v[:, :, :kend]))
    return jnp.concatenate(outs, axis=3)


def conv_glu_ffn(xn, w_up, w_conv, b_conv, w_down):
    seq = xn.shape[1]
    gate, val = jnp.split(xn @ w_up, 2, axis=-1)
    gp = jnp.pad(gate, ((0, 0), (CONV_WIDTH - 1, 0), (0, 0)))
    conv = b_conv + sum(w_conv[j] * gp[:, j:j + seq] for j in range(CONV_WIDTH))
    return (jax.nn.silu(conv) * val) @ w_down


def setup_inputs(seed: int = 0) -> dict:
    key = jax.random.key(seed)
    ks = jax.random.split(key, 24)
    D, F = D_MODEL, D_FF
    kv_width = 2 * N_KV_HEADS_B * HEAD_DIM

    def nrm(k, shape, scale):
        return jax.random.normal(k, shape, jnp.float32) * scale

    def gain(k, shape):
        return 1.0 + 0.02 * jax.random.normal(k, shape, jnp.float32)

    nA, nB = N_A_LAYERS, N_B_LAYERS
    return {
        "x": nrm(ks[0], (BATCH, SEQ, D), 1.0),
        "a_attn_norm": gain(ks[1], (nA, D)),
        "a_w_qkv": nrm(ks[2], (nA, D, 3 * D), D ** -0.5),
        "a_w_o": nrm(ks[3], (nA, D, D), D ** -0.5),
        "a_ffn_norm": gain(ks[4], (nA, D)),
        "a_w_up": nrm(ks[5], (nA, D, 2 * F), D ** -0.5),
        "a_w_conv": nrm(ks[6], (nA, CONV_WIDTH, F), CONV_WIDTH ** -0.5),
        "a_b_conv": nrm(ks[7], (nA, F), 0.02),
        "a_w_down": nrm(ks[8], (nA, F, D), F ** -0.5),
        "kv_norm": gain(ks[9], (D,)),
        "w_kv": nrm(ks[10], (D, kv_width), D ** -0.5),
        "w_f": nrm(ks[11], (D, N_HEADS), 0.5 * D ** -0.5),
        "b_f": 3.0 + 0.5 * jax.random.normal(ks[12], (N_HEADS,), jnp.float32),
        "b_attn_norm": gain(ks[13], (nB, D)),
        "b_w_q": nrm(ks[14], (nB, D, D), D ** -0.5),
        "b_w_o": nrm(ks[15], (nB, D, D), D ** -0.5),
        "b_ffn_norm": gain(ks[16], (nB, D)),
        "b_w_up": nrm(ks[17], (nB, D, 2 * F), D ** -0.5),
        "b_w_conv": nrm(ks[18], (nB, CONV_WIDTH, F), CONV_WIDTH ** -0.5),
        "b_b_conv": nrm(ks[19], (nB, F), 0.02),
        "b_w_down": nrm(ks[20], (nB, F, D), F ** -0.5),
        "final_norm": gain(ks[21], (D,)),
    }


def reference(x, a_attn_norm, a_w_qkv, a_w_o, a_ffn_norm, a_w_up, a_w_conv, a_b_conv, a_w_down,
              kv_norm, w_kv, w_f, b_f,
              b_attn_norm, b_w_q, b_w_o, b_ffn_norm, b_w_up, b_w_conv, b_b_conv, b_w_down,
              final_norm):
    bsz, seq, _ = x.shape
    h = x
    for l in range(DEPTH):
        if l < N_A_LAYERS:
            i = l
            xn = rms_norm(h, a_attn_norm[i])
            q, k, v = jnp.split(xn @ a_w_qkv[i], 3, axis=-1)
            o = stick_breaking_attention(to_heads(q, N_HEADS), to_heads(k, N_HEADS), to_heads(v, N_HEADS))
            h = h + merge_heads(o) @ a_w_o[i]
            h = h + conv_glu_ffn(rms_norm(h, a_ffn_norm[i]), a_w_up[i], a_w_conv[i], a_b_conv[i], a_w_down[i])
        else:
            j = l - N_A_LAYERS
            if l == N_A_LAYERS:
                kv_in = rms_norm(h, kv_norm)
                k_s, v_s = jnp.split(kv_in @ w_kv, 2, axis=-1)
                k_s = to_heads(k_s, N_KV_HEADS_B)
                v_s = to_heads(v_s, N_KV_HEADS_B)
                log_f = jax.nn.log_sigmoid((kv_in @ w_f + b_f).astype(jnp.float32))
                c_s = jnp.cumsum(log_f, axis=1).transpose(0, 2, 1).reshape(bsz, N_KV_HEADS_B, GROUP_B, seq)
            xn = rms_norm(h, b_attn_norm[j])
            q = to_heads(xn @ b_w_q[j], N_HEADS).reshape(bsz, N_KV_HEADS_B, GROUP_B, seq, HEAD_DIM)
            o = forgetting_attention(q, k_s, v_s, c_s).reshape(bsz, N_HEADS, seq, HEAD_DIM)
            h = h + merge_heads(o) @ b_w_o[j]
            h = h + conv_glu_ffn(rms_norm(h, b_ffn_norm[j]), b_w_up[j], b_w_conv[j], b_b_conv[j], b_w_down[j])
    return rms_norm(h, final_norm)
```

```python
from contextlib import ExitStack, contextmanager

import numpy as np
import concourse.bass as bass
import concourse.mybir as mybir
from concourse.bass_utils import run_bass_kernel_spmd

F32 = mybir.dt.float32
BF16 = mybir.dt.bfloat16
AF = mybir.ActivationFunctionType
ALU = mybir.AluOpType
AX = mybir.AxisListType

ENGS = ("pe", "act", "dve", "pool", "sp")
SB_MULT_ENG = "dve"
SB_COPY_ENG = "act"


class Cfg:
    def __init__(self, D=4096, S=2048, eps=1e-6):
        self.D, self.S, self.eps = D, S, eps
        self.H = D // 128
        self.KVH = max(1, self.H // 4)
        self.G = self.H // self.KVH
        self.F = ((8 * D // 3 + 255) // 256) * 256
        self.KC = D // 128
        self.FC = self.F // 128
        self.NTB = S // 128


class Sem:
    def __init__(self, h, name):
        self.h, self.name, self.count = h, name, 0


class Buf:
    def __init__(self, name):
        self.name = name
        self.writers = {}
        self.readers = {}
        self.dsem = None


def _merge(dst, src):
    for k, v in src.items():
        if dst.get(k, (None, 0))[1] < v[1]:
            dst[k] = v


class Prog:
    def __init__(self, nc, stack, n_dma_sems=48):
        self.nc = nc
        self.esem = {e: Sem(stack.enter_context(nc.semaphore("es_" + e)), e) for e in ENGS}
        self.dpool = [Sem(stack.enter_context(nc.semaphore("ds%d" % i)), "ds%d" % i)
                      for i in range(n_dma_sems)]
        self.waited = {e: {} for e in ENGS}
        self.lists = {e: [] for e in ENGS}
        self.stacks = [stack]
        self.bufstack = [[]]
        self.all_dsems = []
        self.n_inst = 0
        self.uid = 0
        self.dead = False

    def sb(self, name, shape, dtype):
        self.uid += 1
        return self.stacks[-1].enter_context(self.nc.sbuf_tensor("%s_%d" % (name, self.uid), list(shape), dtype))

    def ps(self, name, shape, dtype):
        self.uid += 1
        return self.stacks[-1].enter_context(self.nc.psum_tensor("%s_%d" % (name, self.uid), list(shape), dtype))

    def buf(self, name):
        b = Buf(name)
        self.bufstack[-1].append(b)
        return b

    def _deps(self, reads, writes):
        deps = {}
        for b in reads:
            _merge(deps, b.writers)
        for b in writes:
            _merge(deps, b.writers)
            _merge(deps, b.readers)
        return deps

    def _emit_waits(self, eng, deps):
        w = self.waited[eng]
        for k, (sem, v) in deps.items():
            if w.get(k, 0) < v:
                w[k] = v
                self.lists[eng].append(lambda e, h=sem.h, v=v: e.wait_ge(h, v))

    def _record(self, ev, reads, writes, partial):
        key = ev[0].name
        for b in writes:
            if partial:
                _merge(b.writers, {key: ev})
            else:
                b.writers = {key: ev}
                b.readers = {}
        for b in reads:
            if b not in writes:
                _merge(b.readers, {key: ev})

    def op(self, eng, fn, reads=(), writes=(), partial=False):
        if self.dead:
            return
        reads, writes = list(reads), list(writes)
        self._emit_waits(eng, self._deps(reads, writes))
        sem = self.esem[eng]
        sem.count += 1
        self.lists[eng].append(lambda e, fn=fn, h=sem.h: fn(e).then_inc(h, 1))
        self._record((sem, sem.count), reads, writes, partial)
        self.n_inst += 1

    def dma(self, q, out, in_, reads=(), writes=(), owner=None, partial=False, group_cont=False, **kw):
        if self.dead:
            return
        reads, writes = list(reads), list(writes)
        deps = self._deps(reads, writes)
        if owner.dsem is None:
            owner.dsem = self.dpool.pop()
            self.all_dsems.append(owner.dsem)
        sem = owner.dsem
        if not group_cont and sem.count > 0:
            _merge(deps, {sem.name: (sem, sem.count)})
        self._emit_waits(q, deps)
        sem.count += 16
        self.lists[q].append(
            lambda e, o=out, i=in_, h=sem.h, kw=kw: e.dma_start(out=o, in_=i, **kw).then_inc(h, 16))
        self._record((sem, sem.count), reads, writes, partial)
        self.n_inst += 1

    @contextmanager
    def scope(self, flush=False):
        st = ExitStack()
        bufs = []
        self.stacks.append(st)
        self.bufstack.append(bufs)
        try:
            with st:
                yield
                if flush:
                    for b in bufs:
                        sem = b.dsem
                        if sem is not None and sem.count > 0 and self.waited["sp"].get(sem.name, 0) < sem.count:
                            self.waited["sp"][sem.name] = sem.count
                            self.lists["sp"].append(lambda e, h=sem.h, v=sem.count: e.wait_ge(h, v))
                    self.flush()
        finally:
            self.stacks.pop()
            self.bufstack.pop()
            for b in bufs:
                if b.dsem is not None:
                    self.dpool.insert(0, b.dsem)
                    b.dsem = None

    def phase(self, name):
        return self.scope(flush=True)

    def flush(self):
        nc = self.nc
        lists = self.lists
        with nc.Block() as block:
            @block.tensor
            def _(e):
                for f in lists["pe"]:
                    f(e)

            @block.scalar
            def _(e):
                for f in lists["act"]:
                    f(e)

            @block.vector
            def _(e):
                for f in lists["dve"]:
                    f(e)

            @block.gpsimd
            def _(e):
                for f in lists["pool"]:
                    f(e)

            @block.sync
            def _(e):
                for f in lists["sp"]:
                    f(e)
        self.lists = {e: [] for e in ENGS}

    def dump(self, name, ap, shape, dtype, bufs):
        d = self.nc.dram_tensor(name, list(shape), dtype, kind="ExternalOutput").ap()
        b = self.buf("dump_" + name)
        self.dma("sp", d, ap, reads=bufs, owner=b)

    def finish(self):
        for sem in self.all_dsems:
            if sem.count > 0 and self.waited["sp"].get(sem.name, 0) < sem.count:
                self.waited["sp"][sem.name] = sem.count
                self.lists["sp"].append(lambda e, h=sem.h, v=sem.count: e.wait_ge(h, v))


class WStream:
    def __init__(self, P, name, KT, NC, nslots, extra=()):
        self.P, self.KT, self.NC = P, KT, NC
        self.tiles = [P.sb("%s_w%d" % (name, i), [128, KT, NC], BF16) for i in range(nslots)]
        self.bufs = [P.buf("%s_wb%d" % (name, i)) for i in range(nslots)]
        for i, t in enumerate(extra):
            self.tiles.append(t)
            self.bufs.append(P.buf("%s_wx%d" % (name, i)))
        self.n = 0

    def load(self, w2d, k0c, kt, col_ranges):
        P = self.P
        s = self.n % len(self.tiles)
        self.n += 1
        tile, buf = self.tiles[s], self.bufs[s]
        first = True
        off = 0
        for (c0, ncols) in col_ranges:
            src = w2d[k0c * 128:(k0c + kt) * 128, c0:c0 + ncols].rearrange("(kc p) n -> p kc n", p=128)
            step = max(1, min(16, 2048 // max(1, ncols // 64)))
            step = 16
            for a in range(0, kt, step):
                b = min(kt, a + step)
                P.dma("pool", tile[:, a:b, off:off + ncols], src[:, a:b, :], writes=[buf], owner=buf,
                      partial=not first, group_cont=not first)
                first = False
            off += ncols
        return tile, buf


def ws_load_bf16(ws, src3d, kt, srcbufs, q="act"):
    P = ws.P
    sl = ws.n % len(ws.tiles)
    ws.n += 1
    tile, buf = ws.tiles[sl], ws.bufs[sl]
    P.dma(q, tile[:, 0:kt, :], src3d, reads=srcbufs, writes=[buf], owner=buf)
    return tile, buf


def wd_convert_jobs(cfg, Wd, WC):
    FC, D = cfg.FC, cfg.D
    KH = (FC + 1) // 2
    jobs = []
    for cb in range(D // 256):
        for ki, (k0, kt) in enumerate(((0, KH), (KH, FC - KH))):
            tid = cb * 2 + ki
            src = Wd[k0 * 128:(k0 + kt) * 128, cb * 256:(cb + 1) * 256].rearrange("(kc p) n -> p kc n", p=128)
            dst = WC[tid].rearrange("p (k n) -> p k n", n=256)
            for a in range(0, kt, 16):
                b = min(kt, a + 16)
                jobs.append((dst[:, a:b, :], src[:, a:b, :]))
    return jobs


def mm_group(P, ps_ap, ps_bufs, wtile, wbuf, wc0, wn, kcs, at, atbufs, at_kc0, t0, tn, start, stop):
    def fn(e):
        inst = None
        n = len(kcs)
        for i, kc in enumerate(kcs):
            inst = e.matmul(ps_ap, wtile[:, kc, wc0:wc0 + wn], at[:, at_kc0 + kc, t0:t0 + tn],
                            start=(start and i == 0), stop=(stop and i == n - 1))
        return inst
    P.op("pe", fn, reads=[wbuf] + list(atbufs), writes=ps_bufs, partial=not start)


class Env:
    pass


def make_consts(P, env):
    env.identf = P.sb("identf", [128, 128], F32)
    env.identb = P.sb("identb", [128, 128], BF16)
    env.identf_b = P.buf("identf")
    env.identb_b = P.buf("identb")
    for t, b in ((env.identf, env.identf_b), (env.identb, env.identb_b)):
        P.op("pool", lambda e, t=t: e.memset(t[:, :], 1.0), writes=[b])
        P.op("pool", lambda e, t=t: e.affine_select(
            out=t[:, :], in_=t[:, :], pattern=[[-1, 128]], compare_op=ALU.is_equal, fill=0.0,
            base=0, channel_multiplier=1), reads=[b], writes=[b])


def load_cols(P, env, vec_ap, ncols, name, pu, pub):
    tmp = P.sb(name + "_r", [ncols, 128], F32)
    tmpb = P.buf(name + "_r")
    out = P.sb(name, [128, ncols], F32)
    outb = P.buf(name)
    P.dma("sp", tmp[:, :], vec_ap.rearrange("(c p) -> c p", p=128), writes=[tmpb], owner=tmpb)
    P.op("pe", lambda e: e.transpose(out=pu[:, 0:ncols], in_=tmp[:, :], identity=env.identf[0:ncols, 0:ncols]),
         reads=[tmpb, env.identf_b], writes=[pub])
    P.op("dve", lambda e: e.tensor_copy(out=out[:, :], in_=pu[:, 0:ncols]), reads=[pub], writes=[outb])
    return out, outb


def acc_squares(P, env, h_ap, hbuf, cols, first, tmp, tmpb):
    if first:
        P.op("pool", lambda e: e.tensor_tensor(out=env.acc[:, cols], in0=h_ap, in1=h_ap, op=ALU.mult),
             reads=[hbuf], writes=[env.acc_b], partial=True)
    else:
        n = cols.stop - cols.start
        P.op("pool", lambda e: e.tensor_tensor(out=tmp[:, 0:n], in0=h_ap, in1=h_ap, op=ALU.mult),
             reads=[hbuf], writes=[tmpb])
        P.op("pool", lambda e: e.tensor_tensor(out=env.acc[:, cols], in0=env.acc[:, cols], in1=tmp[:, 0:n], op=ALU.add),
             reads=[tmpb, env.acc_b], writes=[env.acc_b], partial=True)


def norm_stats_acc(P, env, cfg, D, pus, pubs):
    S = cfg.S
    onesf = P.sb("ns_ones", [128, 128], F32)
    onesb = P.buf("ns_ones")
    P.op("pool", lambda e: e.memset(onesf[:, :], 1.0), writes=[onesb])
    ntg = S // 512
    nu = (ntg + 1) // 2

    def fn(e):
        inst = None
        for tg in range(ntg):
            u = pus[tg // 2]
            inst = e.matmul(u[:, (tg % 2) * 512:(tg % 2) * 512 + 512], onesf[:, :],
                            env.acc[:, tg * 512:(tg + 1) * 512], start=True, stop=True)
        return inst
    P.op("pe", fn, reads=[env.acc_b, onesb], writes=pubs[:nu])
    for j in range(nu):
        w = min(1024, S - j * 1024)
        sl = slice(j * 1024, j * 1024 + w)
        P.op("dve", lambda e, j=j, w=w, sl=sl: e.tensor_scalar(
            out=env.rstd[:, sl], in0=pus[j][:, 0:w], scalar1=1.0 / D, scalar2=cfg.eps,
            op0=ALU.mult, op1=ALU.add), reads=[pubs[j]], writes=[env.rstd_b], partial=j > 0)
    P.op("act", lambda e: e.activation(out=env.rstd[:, :], in_=env.rstd[:, :], func=AF.Sqrt),
         reads=[env.rstd_b], writes=[env.rstd_b])
    P.op("dve", lambda e: e.reciprocal(out=env.rstd[:, :], in_=env.rstd[:, :]),
         reads=[env.rstd_b], writes=[env.rstd_b])


def norm_stats(P, env, cfg, Hap, Hb, KC, pus, pubs):
    S = cfg.S
    D = KC * 128
    hs = [P.sb("ns_h%d" % i, [128, S], F32) for i in range(2)]
    hsb = [P.buf("ns_h%d" % i) for i in range(2)]
    sq = [P.sb("ns_q%d" % i, [128, S], F32) for i in range(2)]
    sqb = [P.buf("ns_q%d" % i) for i in range(2)]
    onesf = P.sb("ns_ones", [128, 128], F32)
    onesb = P.buf("ns_ones")
    P.op("pool", lambda e: e.memset(onesf[:, :], 1.0), writes=[onesb])
    ntg = S // 512
    for kc in range(KC):
        i = kc % 2
        P.dma("sp", hs[i][:, :], Hap[kc * 128:(kc + 1) * 128, :], reads=[Hb], writes=[hsb[i]], owner=hsb[i])
        P.op("act", lambda e, i=i: e.activation(out=sq[i][:, :], in_=hs[i][:, :], func=AF.Square),
             reads=[hsb[i]], writes=[sqb[i]])

        def fn(e, i=i, kc=kc):
            inst = None
            for tg in range(ntg):
                u = pus[tg // 2]
                inst = e.matmul(u[:, (tg % 2) * 512:(tg % 2) * 512 + 512], onesf[:, :],
                                sq[i][:, tg * 512:(tg + 1) * 512], start=(kc == 0), stop=(kc == KC - 1))
            return inst
        nu = (ntg + 1) // 2
        P.op("pe", fn, reads=[sqb[i], onesb], writes=pubs[:nu], partial=kc > 0)
    for j in range((ntg + 1) // 2):
        w = min(1024, S - j * 1024)
        sl = slice(j * 1024, j * 1024 + w)
        P.op("dve", lambda e, j=j, w=w, sl=sl: e.tensor_scalar(
            out=env.rstd[:, sl], in0=pus[j][:, 0:w], scalar1=1.0 / D, scalar2=cfg.eps,
            op0=ALU.mult, op1=ALU.add), reads=[pubs[j]], writes=[env.rstd_b], partial=j > 0)
    P.op("act", lambda e: e.activation(out=env.rstd[:, :], in_=env.rstd[:, :], func=AF.Sqrt),
         reads=[env.rstd_b], writes=[env.rstd_b])
    P.op("dve", lambda e: e.reciprocal(out=env.rstd[:, :], in_=env.rstd[:, :]),
         reads=[env.rstd_b], writes=[env.rstd_b])


def norm_apply(P, env, cfg, Hap, Hb, g_ap, KC, xn, xnb, pu, pub):
    S = cfg.S
    gcol, gcolb = load_cols(P, env, g_ap, KC, "na_g", pu, pub)
    hs = [P.sb("na_h%d" % i, [128, S], F32) for i in range(3)]
    hsb = [P.buf("na_h%d" % i) for i in range(3)]
    for kc in range(KC):
        i = kc % 3
        P.dma("sp", hs[i][:, :], Hap[kc * 128:(kc + 1) * 128, :], reads=[Hb], writes=[hsb[i]], owner=hsb[i])
        P.op("dve", lambda e, i=i, kc=kc: e.scalar_tensor_tensor(
            out=xn[:, kc, :], in0=hs[i][:, :], scalar=gcol[:, kc:kc + 1], in1=env.rstd[:, :],
            op0=ALU.mult, op1=ALU.mult), reads=[hsb[i], gcolb, env.rstd_b], writes=[xnb], partial=kc > 0)


def gemm_plain(P, cfg, at, atb, KC, W2d, N, epilogue, pus, pubs, pre=None):
    S = cfg.S
    TU = min(1024, S)
    NCW = min(256, N)
    ws = WStream(P, "gp", KC, NCW, 2)
    ui = 0
    for c0 in range(0, N, NCW):
        wt, wb = ws.load(W2d, 0, KC, [(c0, NCW)])
        for sub in range(NCW // 128):
            n0 = c0 + sub * 128
            if pre is not None:
                pre(n0)
            for th in range(S // TU):
                u, ub = pus[ui % len(pus)], pubs[ui % len(pus)]
                ui += 1
                for tg in range(TU // 512):
                    mm_group(P, u[:, tg * 512:(tg + 1) * 512], [ub], wt, wb, sub * 128, 128,
                             list(range(KC)), at, [atb], 0, th * TU + tg * 512, 512, True, True)
                epilogue(n0, th, TU, u, ub, ui)


def store_epilogue(P, cfg, out_ap, outb, name):
    S = cfg.S
    stg = [P.sb("%s_st%d" % (name, i), [128, S], BF16) for i in range(2)]
    stgb = [P.buf("%s_st%d" % (name, i)) for i in range(2)]
    state = {"n": 0}

    def epi(n0, th, TU, u, ub, ui):
        i = state["n"] % 2
        st, sbf = stg[i], stgb[i]
        if ui % 2:
            P.op("act", lambda e: e.activation(out=st[:, th * TU:(th + 1) * TU], in_=u[:, 0:TU], func=AF.Copy),
                 reads=[ub], writes=[sbf], partial=True)
        else:
            P.op("dve", lambda e: e.tensor_copy(out=st[:, th * TU:(th + 1) * TU], in_=u[:, 0:TU]),
                 reads=[ub], writes=[sbf], partial=True)
        if (th + 1) * TU == S:
            P.dma("sp", out_ap[n0:n0 + 128, :], st[:, :], reads=[sbf], writes=[outb], owner=sbf, partial=True)
            state["n"] += 1
    return epi


def resid_epilogue(P, env, cfg, Hin, Hinb, Hout, Houtb, name):
    S = cfg.S
    hs = [P.sb("%s_h%d" % (name, i), [128, S], F32) for i in range(2)]
    hsb = [P.buf("%s_h%d" % (name, i)) for i in range(2)]
    sqt = P.sb("%s_sq" % name, [128, min(1024, S)], F32)
    sqtb = P.buf("%s_sq" % name)
    state = {"n": 0}

    def pre(n0):
        i = state["n"] % 2
        P.dma("sp", hs[i][:, :], Hin[n0:n0 + 128, :], reads=[Hinb], writes=[hsb[i]], owner=hsb[i])

    def epi(n0, th, TU, u, ub, ui):
        i = state["n"] % 2
        h, hb = hs[i], hsb[i]
        sl = slice(th * TU, (th + 1) * TU)
        P.op("dve", lambda e: e.tensor_tensor(out=h[:, sl], in0=u[:, 0:TU], in1=h[:, sl], op=ALU.add),
             reads=[ub, hb], writes=[hb], partial=True)
        if state["n"] == 0:
            P.op("act", lambda e: e.activation(out=env.acc[:, sl], in_=h[:, sl], func=AF.Square),
                 reads=[hb], writes=[env.acc_b], partial=True)
        else:
            P.op("act", lambda e: e.activation(out=sqt[:, 0:TU], in_=h[:, sl], func=AF.Square),
                 reads=[hb], writes=[sqtb])
            P.op("dve", lambda e: e.tensor_tensor(out=env.acc[:, sl], in0=env.acc[:, sl], in1=sqt[:, 0:TU], op=ALU.add),
                 reads=[sqtb, env.acc_b], writes=[env.acc_b], partial=True)
        if (th + 1) * TU == S:
            P.dma("sp", Hout[n0:n0 + 128, :], h[:, :], reads=[hb], writes=[Houtb], owner=hb, partial=True)
            state["n"] += 1
    return pre, epi


def load_at(P, at, atb, src_ap, KC, q="sp"):
    src = src_ap.rearrange("(kc p) t -> p kc t", p=128)
    for c in range(0, KC, 8):
        ce = min(KC, c + 8)
        P.dma(q, at[:, c:ce, :], src[:, c:ce, :], writes=[atb], owner=atb, partial=c > 0, group_cont=c > 0)


def up_proj(P, env, cfg, xn, xnb, Wup, wconv_ap, bconv_ap, ACTT, ACTTb, pus, pubs, conv_jobs=(), WCb=None):
    S, F, FC, KC = cfg.S, cfg.F, cfg.FC, cfg.KC
    TU = min(1024, S)
    wc = []
    for j in range(3):
        wc.append(load_cols(P, env, wconv_ap[j, :], FC, "wc%d" % j, pus[0], pubs[0]))
    bc, bcb = load_cols(P, env, bconv_ap, FC, "bc", pus[0], pubs[0])
    G = P.sb("up_G", [128, S + 2], F32)
    Gb = P.buf("up_G")
    P.op("pool", lambda e: e.memset(G[:, 0:2], 0.0), writes=[Gb])
    tmp = [P.sb("up_t%d" % i, [128, TU], F32) for i in range(2)]
    tmpb = [P.buf("up_t%d" % i) for i in range(2)]
    stg = [P.sb("up_s%d" % i, [128, S], BF16) for i in range(2)]
    stgb = [P.buf("up_s%d" % i) for i in range(2)]
    ws = WStream(P, "up", KC, 256, 2)
    ui = 0
    it = 0
    conv_jobs = list(conv_jobs)
    per = (len(conv_jobs) + FC - 1) // FC if conv_jobs else 0
    cvb = P.buf("up_cv")
    ncv = 0
    for f in range(FC):
        wt, wb = ws.load(Wup, 0, KC, [(f * 128, 128), (F + f * 128, 128)])
        for _ in range(per):
            if ncv < len(conv_jobs):
                dst, src = conv_jobs[ncv]
                P.dma("pool", dst, src, writes=[WCb], owner=cvb, partial=True, group_cont=ncv > 0)
                ncv += 1
        st, sbf = stg[f % 2], stgb[f % 2]
        for th in range(S // TU):
            ug, ugb = pus[ui % 4], pubs[ui % 4]
            uv, uvb = pus[(ui + 1) % 4], pubs[(ui + 1) % 4]
            ui += 2
            for tg in range(TU // 512):
                mm_group(P, ug[:, tg * 512:(tg + 1) * 512], [ugb], wt, wb, 0, 128, list(range(KC)),
                         xn, [xnb], 0, th * TU + tg * 512, 512, True, True)
            for tg in range(TU // 512):
                mm_group(P, uv[:, tg * 512:(tg + 1) * 512], [uvb], wt, wb, 128, 128, list(range(KC)),
                         xn, [xnb], 0, th * TU + tg * 512, 512, True, True)
            t0 = th * TU
            tm, tmb = tmp[it % 2], tmpb[it % 2]
            it += 1
            P.op("act", lambda e, ug=ug, t0=t0: e.activation(out=G[:, 2 + t0:2 + t0 + TU], in_=ug[:, 0:TU], func=AF.Copy),
                 reads=[ugb], writes=[Gb], partial=True)
            P.op("dve", lambda e, tm=tm, t0=t0, f=f: e.tensor_scalar(
                out=tm[:, :], in0=G[:, 2 + t0:2 + t0 + TU], scalar1=wc[2][0][:, f:f + 1], scalar2=bc[:, f:f + 1],
                op0=ALU.mult, op1=ALU.add), reads=[Gb, wc[2][1], bcb], writes=[tmb])
            P.op("dve", lambda e, tm=tm, t0=t0, f=f: e.scalar_tensor_tensor(
                out=tm[:, :], in0=G[:, 1 + t0:1 + t0 + TU], scalar=wc[1][0][:, f:f + 1], in1=tm[:, :],
                op0=ALU.mult, op1=ALU.add), reads=[Gb, wc[1][1], tmb], writes=[tmb])
            P.op("dve", lambda e, tm=tm, t0=t0, f=f: e.scalar_tensor_tensor(
                out=tm[:, :], in0=G[:, t0:t0 + TU], scalar=wc[0][0][:, f:f + 1], in1=tm[:, :],
                op0=ALU.mult, op1=ALU.add), reads=[Gb, wc[0][1], tmb], writes=[tmb])
            P.op("act", lambda e, tm=tm: e.activation(out=tm[:, :], in_=tm[:, :], func=AF.Silu),
                 reads=[tmb], writes=[tmb])
            P.op("dve", lambda e, tm=tm, st=st, uv=uv, t0=t0: e.tensor_tensor(
                out=st[:, t0:t0 + TU], in0=uv[:, 0:TU], in1=tm[:, :], op=ALU.mult),
                reads=[uvb, tmb], writes=[sbf], partial=True)
        P.dma("sp", ACTT[:, :, f * 512:(f + 1) * 512].rearrange("g p t -> p g t"),
              st[:, :].rearrange("p (g t) -> p g t", t=512), reads=[sbf], writes=[ACTTb], owner=sbf, partial=True)


def down_proj(P, env, cfg, atflat, atb, ACTT, ACTTb, Wd, Hin, Hinb, Hout, Houtb, pus, pubs, WC=None, WCb=None):
    S, D, FC = cfg.S, cfg.D, cfg.FC
    TG = 512
    KH = (FC + 1) // 2
    khs = [(0, KH), (KH, FC - KH)]
    at = atflat[:, 0:FC * TG].rearrange("p (kc t) -> p kc t", t=TG)
    athb = [P.buf("dn_at_lo"), P.buf("dn_at_hi")]
    extra = []
    used = FC * TG
    if atflat.shape[1] - used >= KH * 256:
        extra.append(atflat[:, used:used + KH * 256].rearrange("p (k n) -> p k n", n=256))
    ws = WStream(P, "dn", KH, 256, 2, extra=extra)
    hs = [P.sb("dn_h%d" % i, [128, 2, TG], F32) for i in range(3)]
    hsb = [P.buf("dn_h%d" % i) for i in range(3)]
    sqt = P.sb("dn_sq", [128, TG], F32)
    sqtb = P.buf("dn_sq")
    ui = 0
    hi = 0
    for g in range(S // TG):
        t0 = g * TG
        for (k0, kt), hb_ in zip(khs, athb):
            P.dma("sp", atflat[:, k0 * TG:(k0 + kt) * TG], ACTT[g][:, k0 * TG:(k0 + kt) * TG], reads=[ACTTb],
                  writes=[hb_], owner=hb_)
        for c0 in range(0, D, 256):
            h, hb = hs[hi % 3], hsb[hi % 3]
            hi += 1
            P.dma("sp", h[:, :, :], Hin[c0:c0 + 256, t0:t0 + TG].rearrange("(s p) t -> p s t", p=128),
                  reads=[Hinb], writes=[hb], owner=hb)
            u, ub = pus[ui % 4], pubs[ui % 4]
            ui += 1
            for ki, (k0, kt) in enumerate(khs):
                if WC is not None:
                    tid = (c0 // 256) * 2 + ki
                    wt, wb = ws_load_bf16(ws, WC[tid].rearrange("p (k n) -> p k n", n=256)[:, 0:kt, :], kt, [WCb])
                else:
                    wt, wb = ws.load(Wd, k0, kt, [(c0, 256)])
                for sub in range(2):
                    mm_group(P, u[:, sub * 512:sub * 512 + TG], [ub], wt, wb, sub * 128, 128, list(range(kt)),
                             at, [athb[ki]], k0, 0, TG, ki == 0, ki == 1)
            P.op("dve", lambda e, h=h, u=u: e.tensor_tensor(
                out=h[:, :, :], in0=u[:, 0:1024].rearrange("p (s t) -> p s t", s=2)[:, :, 0:TG], in1=h[:, :, :],
                op=ALU.add), reads=[ub, hb], writes=[hb])
            for sub in range(2):
                acc_squares(P, env, h[:, sub, :], hb, slice(t0, t0 + TG), c0 == 0 and sub == 0, sqt, sqtb)
            P.dma("sp", Hout[c0:c0 + 256, t0:t0 + TG].rearrange("(s p) t -> p s t", p=128), h[:, :, :],
                  reads=[hb], writes=[Houtb], owner=hb, partial=True)


def forget_gates(P, env, cfg, xn, xnb, wf_ap, bf_ap, NEGC, NEGCb, pu, pub):
    S, H, KC = cfg.S, cfg.H, cfg.KC
    wf = P.sb("fg_w", [128, KC, H], BF16)
    wfb = P.buf("fg_w")
    src = wf_ap.rearrange("(kc p) n -> p kc n", p=128)
    for a in range(0, KC, 16):
        b = min(KC, a + 16)
        P.dma("pool", wf[:, a:b, :], src[:, a:b, :], writes=[wfb], owner=wfb, partial=a > 0, group_cont=a > 0)
    bf = P.sb("fg_b", [H, 1], F32)
    bfb = P.buf("fg_b")
    P.dma("sp", bf[:, :], bf_ap.rearrange("(h o) -> h o", o=1), writes=[bfb], owner=bfb)
    P.op("dve", lambda e: e.tensor_scalar(out=bf[:, :], in0=bf[:, :], scalar1=-1.0, scalar2=None, op0=ALU.mult),
         reads=[bfb], writes=[bfb])
    l = P.sb("fg_l", [H, S], F32)
    lb = P.buf("fg_l")
    ng = P.sb("fg_n", [H, S], F32)
    ngb = P.buf("fg_n")
    TU = min(1024, S)
    for th in range(S // TU):
        for tg in range(TU // 512):
            def fn(e, th=th, tg=tg):
                inst = None
                for kc in range(KC):
                    inst = e.matmul(pu[0:H, tg * 512:(tg + 1) * 512], wf[:, kc, :],
                                    xn[:, kc, th * TU + tg * 512:th * TU + (tg + 1) * 512],
                                    start=(kc == 0), stop=(kc == KC - 1))
                return inst
            P.op("pe", fn, reads=[wfb, xnb], writes=[pub], partial=tg > 0)
        sl = slice(th * TU, (th + 1) * TU)
        P.op("act", lambda e, sl=sl: e.activation(out=l[:, sl], in_=pu[0:H, 0:TU], func=AF.Exp,
                                                   bias=bf[:, 0:1], scale=-1.0),
             reads=[pub, bfb], writes=[lb], partial=True)
    P.op("act", lambda e: e.activation(out=l[:, :], in_=l[:, :], func=AF.Ln, bias=1.0, scale=1.0),
         reads=[lb], writes=[lb])
    P.op("dve", lambda e: e.tensor_tensor_scan(out=ng[:, :], data0=l[:, :], data1=l[:, :], initial=0.0,
                                                op0=ALU.add, op1=ALU.max),
         reads=[lb], writes=[ngb])
    P.dma("sp", NEGC[:, :], ng[:, :], reads=[ngb], writes=[NEGCb], owner=ngb)


def attention(P, env, cfg, mode, q_ap, k_ap, v_ap, srcbufs, nkv, group, OT, OTb, NEGC=None, NEGCb=None):
    S, NTB = cfg.S, cfg.NTB
    scale = 128.0 ** -0.5
    fox = mode == "fox"
    zps = P.ps("at_z", [128, S], F32)
    zb = P.buf("at_z")
    tr = P.ps("at_tr", [128, S], BF16)
    trb = P.buf("at_tr")
    ops_ = P.ps("at_o", [128, 512], F32)
    opb = P.buf("at_o")
    tr2 = P.ps("at_tr2", [128, 1024], BF16)
    tr2b = P.buf("at_tr2")

    def mk(name, shape, dt, n=2):
        return ([P.sb("%s%d" % (name, i), shape, dt) for i in range(n)],
                [P.buf("%s%d" % (name, i)) for i in range(n)])
    kt, ktb = mk("at_k", [128, S], BF16)
    vt, vtb = mk("at_v", [128, S], BF16)
    qt, qtb = mk("at_q", [128, S], BF16)
    vsb, vsbb = mk("at_vs", [128, NTB, 128], BF16)
    NS = 2 if fox else 3
    E, Eb = mk("at_E", [128, S], F32, NS)
    W, Wb = mk("at_W", [128, S], BF16, NS)
    WT, WTb = mk("at_WT", [128, S], BF16, NS)
    oT, oTb = mk("at_oT", [128, S], BF16)
    sm, smb = mk("at_sm", [128, 4], F32, NS)
    sr, srb = mk("at_sr", [128, 4], F32, NS)
    maskadd = P.sb("at_mask", [128, 128], BF16)
    maskb = P.buf("at_mask")
    P.op("pool", lambda e: e.memset(maskadd[:, :], 0.0), writes=[maskb])
    P.op("pool", lambda e: e.affine_select(
        out=maskadd[:, :], in_=maskadd[:, :], pattern=[[-1, 128]],
        compare_op=(ALU.is_ge if fox else ALU.is_gt), fill=-1.0e30,
        base=0, channel_multiplier=1), reads=[maskb], writes=[maskb])
    if fox:
        ncb, ncbb = mk("at_nc", [128, S], F32)
        osb, osbb = mk("at_os", [128, 128], BF16)
    else:
        L, Lb = mk("at_L", [128, S], F32, NS)
        C, Cb = mk("at_C", [128, S + 1], F32, NS)
        ones = P.sb("at_ones", [128, S], F32)
        onesb = P.buf("at_ones")
        P.op("pool", lambda e: e.memset(ones[:, :], 1.0), writes=[onesb])
        for i in range(NS):
            P.op("pool", lambda e, i=i: e.memset(C[i][:, 0:1], 0.0), writes=[Cb[i]])

    def kv_setup(kh):
        ki = kh % 2
        P.dma("sp", kt[ki][:, :], k_ap[kh * 128:(kh + 1) * 128, :], reads=srcbufs, writes=[ktb[ki]], owner=ktb[ki])
        P.dma("sp", vt[ki][:, :], v_ap[kh * 128:(kh + 1) * 128, :], reads=srcbufs, writes=[vtb[ki]], owner=vtb[ki])
        for a0 in range(0, NTB, 8):
            a1 = min(NTB, a0 + 8)

            def fnv(e, a0=a0, a1=a1):
                inst = None
                for a in range(a0, a1):
                    inst = e.transpose(out=tr2[:, (a - a0) * 128:(a - a0 + 1) * 128],
                                       in_=vt[ki][:, a * 128:(a + 1) * 128], identity=env.identb[:, :])
                return inst
            P.op("pe", fnv, reads=[vtb[ki], env.identb_b], writes=[tr2b])
            P.op("act", lambda e, a0=a0, a1=a1: e.activation(
                out=vsb[ki][:, a0:a1, :], in_=tr2[:, 0:(a1 - a0) * 128].rearrange("p (a d) -> p a d", d=128),
                func=AF.Copy), reads=[tr2b], writes=[vsbb[ki]], partial=a0 > 0)

    def q_setup(h, qi):
        P.dma("sp", qt[qi][:, :], q_ap[h * 128:(h + 1) * 128, :], reads=srcbufs, writes=[qtb[qi]], owner=qtb[qi])
        if fox:
            P.dma("sp", ncb[qi][:, :], NEGC[h:h + 1, :].partition_broadcast(128), reads=[NEGCb],
                  writes=[ncbb[qi]], owner=ncbb[qi])

    def stA1(t):
        kh, g, h, qi, i, j = t
        if i == 0:
            if g == 0:
                kv_setup(kh)
            q_setup(h, qi)
        ki = kh % 2
        kend = 128 * (i + 1)

        def fnz(e):
            inst = None
            for c0 in range(0, kend, 512):
                cn = min(512, kend - c0)
                lastc = c0 + cn == kend
                inst = e.matmul(zps[:, c0:c0 + cn], qt[qi][:, i * 128:(i + 1) * 128], kt[ki][:, c0:c0 + cn],
                                start=True, stop=not lastc)
            inst = e.matmul(zps[:, kend - 128:kend], env.identb[:, :], maskadd[:, :], start=False, stop=True)
            return inst
        P.op("pe", fnz, reads=[qtb[qi], ktb[ki], env.identb_b, maskb], writes=[zb])

    def stA2(t, part=None):
        kh, g, h, qi, i, j = t
        kend = 128 * (i + 1)
        d0 = 128 * i
        if not fox and part == "b":
            P.op("dve", lambda e: e.tensor_tensor_scan(
                out=C[j][:, 1:kend + 1], data0=ones[:, 0:kend], data1=L[j][:, 0:kend], initial=0.0,
                op0=ALU.mult, op1=ALU.add), reads=[Lb[j], onesb], writes=[Cb[j]])
            P.op("dve", lambda e: e.tensor_scalar(
                out=sm[j][:, 0:1], in0=C[j][:, kend:kend + 1], scalar1=-1.0, scalar2=None, op0=ALU.mult),
                reads=[Cb[j]], writes=[smb[j]])
            return
        if not fox:
            P.op("act", lambda e: e.activation(out=E[j][:, 0:kend], in_=zps[:, 0:kend], func=AF.Exp, scale=scale),
                 reads=[zb], writes=[Eb[j]])
            P.op("act", lambda e: e.activation(out=L[j][:, 0:kend], in_=E[j][:, 0:kend], func=AF.Ln, bias=1.0, scale=1.0),
                 reads=[Eb[j]], writes=[Lb[j]])
            if part is None:
                stA2(t, "b")
        else:
            P.op("dve", lambda e: e.scalar_tensor_tensor(
                out=E[j][:, 0:kend], in0=zps[:, 0:kend], scalar=scale, in1=ncb[qi][:, 0:kend],
                op0=ALU.mult, op1=ALU.add), reads=[zb, ncbb[qi]], writes=[Eb[j]])
            P.op("dve", lambda e: e.tensor_reduce(
                out=sm[j][:, 0:1], in_=E[j][:, 0:kend], axis=AX.X, op=ALU.max, negate=True),
                reads=[Eb[j]], writes=[smb[j]])

    def stB(t):
        kh, g, h, qi, i, j = t
        kend = 128 * (i + 1)
        d0 = 128 * i
        if not fox:
            P.op("act", lambda e: e.activation(out=L[j][:, 0:kend], in_=C[j][:, 0:kend], func=AF.Exp,
                                               bias=sm[j][:, 0:1], scale=1.0),
                 reads=[Cb[j], smb[j]], writes=[Lb[j]])
            P.op(SB_MULT_ENG, lambda e: e.tensor_tensor(out=W[j][:, 0:kend], in0=E[j][:, 0:kend], in1=L[j][:, 0:kend],
                                                        op=ALU.mult), reads=[Eb[j], Lb[j]], writes=[Wb[j]])
        else:
            P.op("act", lambda e: e.activation(
                out=W[j][:, 0:kend], in_=E[j][:, 0:kend], func=AF.Exp, bias=sm[j][:, 0:1], scale=1.0,
                accum_out=sr[j][:, 1:2]), reads=[Eb[j], smb[j]], writes=[Wb[j], srb[j]])
            P.op("dve", lambda e: e.reciprocal(out=sr[j][:, 2:3], in_=sr[j][:, 1:2]),
                 reads=[srb[j]], writes=[srb[j]])

        def fnt(e):
            inst = None
            for a in range(i + 1):
                inst = e.transpose(out=tr[:, a * 128:(a + 1) * 128], in_=W[j][:, a * 128:(a + 1) * 128],
                                   identity=env.identb[:, :])
            return inst
        P.op("pe", fnt, reads=[Wb[j], env.identb_b], writes=[trb])

    def stC(t, part):
        kh, g, h, qi, i, j = t
        ki = kh % 2
        kend = 128 * (i + 1)
        d0 = 128 * i
        o_t, o_tb = oT[qi], oTb[qi]
        if part == 1:
            if not fox and SB_COPY_ENG == "dve":
                P.op("dve", lambda e: e.tensor_copy(out=WT[j][:, 0:kend], in_=tr[:, 0:kend]),
                     reads=[trb], writes=[WTb[j]])
            else:
                P.op("act", lambda e: e.activation(out=WT[j][:, 0:kend], in_=tr[:, 0:kend], func=AF.Copy),
                     reads=[trb], writes=[WTb[j]])
        if not fox:
            def fno(e):
                inst = None
                for a in range(i + 1):
                    inst = e.matmul(ops_[:, 0:128], vsb[ki][:, a, :], WT[j][:, a * 128:(a + 1) * 128],
                                    start=(a == 0), stop=(a == i))
                return inst
            if part == 1:
                P.op("pe", fno, reads=[vsbb[ki], WTb[j]], writes=[opb])
                return
            P.op("dve", lambda e: e.tensor_copy(out=o_t[:, d0:d0 + 128], in_=ops_[:, 0:128]),
                 reads=[opb], writes=[o_tb], partial=True)
        else:
            def fno(e):
                inst = None
                for a in range(i + 1):
                    inst = e.matmul(ops_[:, 0:128], WT[j][:, a * 128:(a + 1) * 128], vsb[ki][:, a, :],
                                    start=(a == 0), stop=(a == i))
                return inst
            if part == 1:
                P.op("pe", fno, reads=[vsbb[ki], WTb[j]], writes=[opb])
                return
            P.op("act", lambda e: e.activation(out=osb[j][:, :], in_=ops_[:, 0:128], func=AF.Identity,
                                               scale=sr[j][:, 2:3]),
                 reads=[opb, srb[j]], writes=[osbb[j]])
            P.op("pe", lambda e: e.transpose(out=tr2[:, 0:128], in_=osb[j][:, :], identity=env.identb[:, :]),
                 reads=[osbb[j], env.identb_b], writes=[tr2b])
            P.op("dve", lambda e: e.tensor_copy(out=o_t[:, d0:d0 + 128], in_=tr2[:, 0:128]),
                 reads=[tr2b], writes=[o_tb], partial=True)
        if i == NTB - 1:
            P.dma("sp", OT[h * 128:(h + 1) * 128, :], o_t[:, :], reads=[o_tb], writes=[OTb], owner=o_tb, partial=True)

    its = []
    hq = 0
    n = 0
    for kh in range(nkv):
        for g in range(group):
            h = kh * group + g
            qi = hq % 2
            hq += 1
            for i in range(NTB):
                its.append((kh, g, h, qi, i, n % NS))
                n += 1
    N = len(its)

    def at(k):
        return its[k] if 0 <= k < N else None
    if fox:
        for r in range(-2, N):
            if at(r + 2):
                stA1(at(r + 2))
                stA2(at(r + 2))
            if at(r):
                stC(at(r), 1)
            if at(r + 1):
                stB(at(r + 1))
            if at(r):
                stC(at(r), 2)
    else:
        for r in range(-3, N):
            if at(r + 3):
                stA1(at(r + 3))
            if at(r + 2):
                stA2(at(r + 2), "b")
            if at(r):
                stC(at(r), 1)
            if at(r + 1):
                stB(at(r + 1))
            if at(r):
                stC(at(r), 2)
            if at(r + 3):
                stA2(at(r + 3), "a")


def phase_in(P, env, cfg, x, H0, H0b):
    S, KC, NTB = cfg.S, cfg.KC, cfg.NTB
    xs = [P.sb("in_x%d" % i, [128, NTB, 128], F32) for i in range(2)]
    xsb = [P.buf("in_x%d" % i) for i in range(2)]
    pst = [P.ps("in_p%d" % i, [128, S], F32) for i in range(2)]
    pstb = [P.buf("in_p%d" % i) for i in range(2)]
    hst = [P.sb("in_h%d" % i, [128, S], F32) for i in range(2)]
    hstb = [P.buf("in_h%d" % i) for i in range(2)]
    sqt = P.sb("in_sq", [128, S], F32)
    sqtb = P.buf("in_sq")
    for kc in range(KC):
        i = kc % 2
        P.dma("sp", xs[i][:, :, :], x[:, kc * 128:(kc + 1) * 128].rearrange("(tb p) k -> p tb k", p=128),
              writes=[xsb[i]], owner=xsb[i])

        def fn(e, i=i):
            inst = None
            for tb in range(NTB):
                inst = e.transpose(out=pst[i][:, tb * 128:(tb + 1) * 128], in_=xs[i][:, tb, :],
                                   identity=env.identf[:, :])
            return inst
        P.op("pe", fn, reads=[xsb[i], env.identf_b], writes=[pstb[i]])
        if kc % 2:
            P.op("act", lambda e, i=i: e.activation(out=hst[i][:, :], in_=pst[i][:, :], func=AF.Copy),
                 reads=[pstb[i]], writes=[hstb[i]])
        else:
            P.op("dve", lambda e, i=i: e.tensor_copy(out=hst[i][:, :], in_=pst[i][:, :]),
                 reads=[pstb[i]], writes=[hstb[i]])
        acc_squares(P, env, hst[i][:, :], hstb[i], slice(0, S), kc == 0, sqt, sqtb)
        P.dma("sp", H0[kc * 128:(kc + 1) * 128, :], hst[i][:, :], reads=[hstb[i]], writes=[H0b], owner=hstb[i],
              partial=True)


def phase_out(P, env, cfg, Hap, Hb, g_ap, out):
    S, KC, NTB = cfg.S, cfg.KC, cfg.NTB
    pus = [P.ps("fo_p%d" % i, [128, S], F32) for i in range(2)]
    pubs = [P.buf("fo_p%d" % i) for i in range(2)]
    if S >= 1024:
        spus = [pus[0][:, 0:1024], pus[0][:, 1024:2048]] if S == 2048 else [pus[0][:, 0:1024]]
        spubs = [pubs[0]] * len(spus)
    else:
        spus, spubs = [pus[0]], [pubs[0]]
    norm_stats_acc(P, env, cfg, KC * 128, spus, spubs)
    gcol, gcolb = load_cols(P, env, g_ap, KC, "fo_g", pus[1], pubs[1])
    hs = [P.sb("fo_h%d" % i, [128, S], F32) for i in range(2)]
    hsb = [P.buf("fo_h%d" % i) for i in range(2)]
    ost = [P.sb("fo_o%d" % i, [128, NTB, 128], F32) for i in range(2)]
    ostb = [P.buf("fo_o%d" % i) for i in range(2)]
    outb = Buf("out")
    for kc in range(KC):
        i = kc % 2
        P.dma("sp", hs[i][:, :], Hap[kc * 128:(kc + 1) * 128, :], reads=[Hb], writes=[hsb[i]], owner=hsb[i])
        P.op("dve", lambda e, i=i, kc=kc: e.scalar_tensor_tensor(
            out=hs[i][:, :], in0=hs[i][:, :], scalar=gcol[:, kc:kc + 1], in1=env.rstd[:, :],
            op0=ALU.mult, op1=ALU.mult), reads=[hsb[i], gcolb, env.rstd_b], writes=[hsb[i]])

        def fn(e, i=i):
            inst = None
            for tb in range(NTB):
                inst = e.transpose(out=pus[i][:, tb * 128:(tb + 1) * 128], in_=hs[i][:, tb * 128:(tb + 1) * 128],
                                   identity=env.identf[:, :])
            return inst
        P.op("pe", fn, reads=[hsb[i], env.identf_b], writes=[pubs[i]])
        P.op("act", lambda e, i=i: e.activation(out=ost[i][:, :, :], in_=pus[i][:, :].rearrange("p (a d) -> p a d", d=128),
                                                func=AF.Copy), reads=[pubs[i]], writes=[ostb[i]])
        P.dma("sp", out[:, kc * 128:(kc + 1) * 128].rearrange("(tb p) k -> p tb k", p=128), ost[i][:, :, :],
              reads=[ostb[i]], writes=[outb], owner=ostb[i], partial=True)


PARAMS = [("a_attn_norm", "D"), ("a_w_qkv", "D,3D"), ("a_w_o", "D,D"), ("a_ffn_norm", "D"), ("a_w_up", "D,2F"),
          ("a_w_conv", "3,F"), ("a_b_conv", "F"), ("a_w_down", "F,D"), ("kv_norm", "D"), ("w_kv", "D,KV"),
          ("w_f", "D,H"), ("b_f", "H"), ("b_attn_norm", "D"), ("b_w_q", "D,D"), ("b_w_o", "D,D"),
          ("b_ffn_norm", "D"), ("b_w_up", "D,2F"), ("b_w_conv", "3,F"), ("b_b_conv", "F"), ("b_w_down", "F,D"),
          ("final_norm", "D")]


def param_shape(cfg, spec):
    m = {"D": cfg.D, "3D": 3 * cfg.D, "2F": 2 * cfg.F, "F": cfg.F, "KV": 2 * cfg.KVH * 128, "H": cfg.H, "3": 3}
    return [m[s] for s in spec.split(",")]


def build(cfg, dbg=(), stop_after=None, dbg_att=None):
    nc = bass.Bass("TRN2", target_bir_lowering=False)
    D, S, H, KVH, G, F, KC, FC = cfg.D, cfg.S, cfg.H, cfg.KVH, cfg.G, cfg.F, cfg.KC, cfg.FC
    x = nc.dram_tensor("x", [S, D], F32, kind="ExternalInput").ap()
    prm = {}
    for name, spec in PARAMS:
        prm[name] = nc.dram_tensor(name, param_shape(cfg, spec), F32, kind="ExternalInput").ap()
    out = nc.dram_tensor("out", [S, D], F32, kind="ExternalOutput").ap()
    sbufs = {}

    def scratch(name, shape, dt):
        kind = "ExternalOutput" if name in dbg else "Internal"
        sbufs[name] = Buf(name)
        return nc.dram_tensor(name, shape, dt, kind=kind).ap()
    Hs = [scratch("H%d" % i, [D, S], F32) for i in range(5)]
    Hb = [sbufs["H%d" % i] for i in range(5)]
    QKVT = scratch("QKVT", [3 * D, S], BF16)
    OTA = scratch("OTA", [D, S], BF16)
    ACTA = scratch("ACTA", [S // 512, 128, FC * 512], BF16)
    KVT = scratch("KVT", [2 * KVH * 128, S], BF16)
    NEGC = scratch("NEGC", [H, S], F32)
    QT = scratch("QT", [D, S], BF16)
    OTB = scratch("OTB", [D, S], BF16)
    ACTB = scratch("ACTB", [S // 512, 128, FC * 512], BF16)
    KHh = (FC + 1) // 2
    WCA = scratch("WCA", [2 * (D // 256), 128, KHh * 256], BF16)
    WCB = scratch("WCB", [2 * (D // 256), 128, KHh * 256], BF16)

    class Stop(Exception):
        pass

    def chk(tag):
        if stop_after == tag:
            P.dead = True

    with ExitStack() as top:
        P = Prog(nc, top)
        env = Env()
        env.dbg_att = dbg_att
        env.rstd = P.sb("rstd", [128, S], F32)
        env.rstd_b = P.buf("rstd")
        env.acc = P.sb("acc", [128, S], F32)
        env.acc_b = P.buf("acc")
        try:
            make_consts(P, env)
            with P.phase("init"):
                phase_in(P, env, cfg, x, Hs[0], Hb[0])
            chk("in")

            def units():
                pus = [P.ps("pu%d" % i, [128, 1024], F32) for i in range(4)]
                pubs = [P.buf("pu%d" % i) for i in range(4)]
                return pus, pubs

            def normed(Hi, gname, stats=True):
                with P.phase("norm"):
                    pus, pubs = units()
                    if stats:
                        norm_stats_acc(P, env, cfg, D, pus[0:2], pubs[0:2])
                    norm_apply(P, env, cfg, Hs[Hi], Hb[Hi], prm[gname], KC, xn, xnb, pus[2], pubs[2])

            def ffn(Hi, pre, ACT, ACTb):
                WC = WCA if pre == "a" else WCB
                WCb = sbufs["WCA" if pre == "a" else "WCB"]
                normed(Hi, pre + "_ffn_norm")
                with P.phase("up"):
                    pus, pubs = units()
                    up_proj(P, env, cfg, xn, xnb, prm[pre + "_w_up"], prm[pre + "_w_conv"], prm[pre + "_b_conv"],
                            ACT, ACTb, pus, pubs, conv_jobs=wd_convert_jobs(cfg, prm[pre + "_w_down"], WC), WCb=WCb)
                chk(pre + "_up")
                with P.phase("down"):
                    pus, pubs = units()
                    down_proj(P, env, cfg, xflat, xnb, ACT, ACTb, prm[pre + "_w_down"], Hs[Hi], Hb[Hi],
                              Hs[Hi + 1], Hb[Hi + 1], pus, pubs, WC=WC, WCb=WCb)
                chk(pre + "_down")

            def wo(OT, OTb, wname, Hi):
                with P.phase("wo"):
                    pus, pubs = units()
                    load_at(P, xn, xnb, OT, KC, [OTb])
                    pre_, epi = resid_epilogue(P, env, cfg, Hs[Hi], Hb[Hi], Hs[Hi + 1], Hb[Hi + 1], "wo")
                    gemm_plain(P, cfg, xn, xnb, KC, prm[wname], D, epi, pus, pubs, pre=pre_)

            with P.scope():
                xflat = P.sb("xn", [128, max(KC * S, FC * 512)], BF16)
                xn = xflat[:, 0:KC * S].rearrange("p (kc t) -> p kc t", t=S)
                xnb = P.buf("xn")
                normed(0, "a_attn_norm")
                with P.phase("qkv"):
                    pus, pubs = units()
                    epi = store_epilogue(P, cfg, QKVT, sbufs["QKVT"], "qkv")
                    gemm_plain(P, cfg, xn, xnb, KC, prm["a_w_qkv"], 3 * D, epi, pus, pubs)
            chk("qkv")
            with P.phase("attA"):
                attention(P, env, cfg, "sb", QKVT[0:D, :], QKVT[D:2 * D, :], QKVT[2 * D:3 * D, :], [sbufs["QKVT"]],
                          H, 1, OTA, sbufs["OTA"])
            chk("attA")
            with P.scope():
                xflat = P.sb("xn", [128, max(KC * S, FC * 512)], BF16)
                xn = xflat[:, 0:KC * S].rearrange("p (kc t) -> p kc t", t=S)
                xnb = P.buf("xn")
                wo(OTA, sbufs["OTA"], "a_w_o", 0)
                chk("woA")
                ffn(1, "a", ACTA, sbufs["ACTA"])
            with P.scope():
                xflat = P.sb("xn", [128, max(KC * S, FC * 512)], BF16)
                xn = xflat[:, 0:KC * S].rearrange("p (kc t) -> p kc t", t=S)
                xnb = P.buf("xn")
                normed(2, "kv_norm")
                with P.phase("kv"):
                    pus, pubs = units()
                    epi = store_epilogue(P, cfg, KVT, sbufs["KVT"], "kv")
                    gemm_plain(P, cfg, xn, xnb, KC, prm["w_kv"], 2 * KVH * 128, epi, pus, pubs)
                    forget_gates(P, env, cfg, xn, xnb, prm["w_f"], prm["b_f"], NEGC, sbufs["NEGC"], pus[0], pubs[0])
                chk("kv")
                normed(2, "b_attn_norm", stats=False)
                with P.phase("q"):
                    pus, pubs = units()
                    epi = store_epilogue(P, cfg, QT, sbufs["QT"], "q")
                    gemm_plain(P, cfg, xn, xnb, KC, prm["b_w_q"], D, epi, pus, pubs)
            chk("q")
            with P.phase("attB"):
                attention(P, env, cfg, "fox", QT, KVT[0:KVH * 128, :], KVT[KVH * 128:2 * KVH * 128, :],
                          [sbufs["QT"], sbufs["KVT"]], KVH, G, OTB, sbufs["OTB"], NEGC, sbufs["NEGC"])
            chk("attB")
            with P.scope():
                xflat = P.sb("xn", [128, max(KC * S, FC * 512)], BF16)
                xn = xflat[:, 0:KC * S].rearrange("p (kc t) -> p kc t", t=S)
                xnb = P.buf("xn")
                wo(OTB, sbufs["OTB"], "b_w_o", 2)
                chk("woB")
                ffn(3, "b", ACTB, sbufs["ACTB"])
            with P.phase("final"):
                phase_out(P, env, cfg, Hs[4], Hb[4], prm["final_norm"], out)
        except Stop:
            pass
        P.dead = False
        with P.phase("end"):
            P.finish()
    nc._prog_ninst = P.n_inst
    return nc


def load_at(P, at, atb, src_ap, KC, srcbufs):
    src = src_ap.rearrange("(kc p) t -> p kc t", p=128)
    for c in range(0, KC, 8):
        ce = min(KC, c + 8)
        P.dma("sp", at[:, c:ce, :], src[:, c:ce, :], reads=srcbufs, writes=[atb], owner=atb,
              partial=c > 0, group_cont=c > 0)


_CFG = Cfg()
_NC_CACHE = {}


def kernel(**inputs):
    cfg = _CFG
    B = inputs["x"].shape[0]
    if "nc" not in _NC_CACHE:
        _NC_CACHE["nc"] = build(cfg)
    nc = _NC_CACHE["nc"]
    shared = {}
    for name, spec in PARAMS:
        shared[name] = np.ascontiguousarray(np.asarray(inputs[name], dtype=np.float32).reshape(param_shape(cfg, spec)))
    x = np.asarray(inputs["x"], dtype=np.float32)
    in_maps = []
    for b in range(B):
        m = dict(shared)
        m["x"] = np.ascontiguousarray(x[b])
        in_maps.append(m)
    res = run_bass_kernel_spmd(nc, in_maps, core_ids=list(range(B)))
    return np.stack([np.asarray(r["out"]) for r in res.results], axis=0).astype(np.float32)
```

```python
from contextlib import ExitStack, contextmanager

import numpy as np
import concourse.bass as bass
import concourse.mybir as mybir
from concourse.bass_utils import run_bass_kernel_spmd

F32 = mybir.dt.float32
BF16 = mybir.dt.bfloat16
AF = mybir.ActivationFunctionType
ALU = mybir.AluOpType
AX = mybir.AxisListType

ENGS = ("pe", "act", "dve", "pool", "sp")
SB_MULT_ENG = "dve"
SB_COPY_ENG = "act"


class Cfg:
    def __init__(self, D=4096, S=2048, eps=1e-6):
        self.D, self.S, self.eps = D, S, eps
        self.H = D // 128
        self.KVH = max(1, self.H // 4)
        self.G = self.H // self.KVH
        self.F = ((8 * D // 3 + 255) // 256) * 256
        self.KC = D // 128
        self.FC = self.F // 128
        self.NTB = S // 128


class Sem:
    def __init__(self, h, name):
        self.h, self.name, self.count = h, name, 0


class Buf:
    def __init__(self, name):
        self.name = name
        self.writers = {}
        self.readers = {}
        self.dsem = None


def _merge(dst, src):
    for k, v in src.items():
        if dst.get(k, (None, 0))[1] < v[1]:
            dst[k] = v


class Prog:
    def __init__(self, nc, stack, n_dma_sems=48):
        self.nc = nc
        self.esem = {e: Sem(stack.enter_context(nc.semaphore("es_" + e)), e) for e in ENGS}
        self.dpool = [Sem(stack.enter_context(nc.semaphore("ds%d" % i)), "ds%d" % i)
                      for i in range(n_dma_sems)]
        self.waited = {e: {} for e in ENGS}
        self.lists = {e: [] for e in ENGS}
        self.stacks = [stack]
        self.bufstack = [[]]
        self.all_dsems = []
        self.n_inst = 0
        self.uid = 0
        self.dead = False

    def sb(self, name, shape, dtype):
        self.uid += 1
        return self.stacks[-1].enter_context(self.nc.sbuf_tensor("%s_%d" % (name, self.uid), list(shape), dtype))

    def ps(self, name, shape, dtype):
        self.uid += 1
        return self.stacks[-1].enter_context(self.nc.psum_tensor("%s_%d" % (name, self.uid), list(shape), dtype))

    def buf(self, name):
        b = Buf(name)
        self.bufstack[-1].append(b)
        return b

    def _deps(self, reads, writes):
        deps = {}
        for b in reads:
            _merge(deps, b.writers)
        for b in writes:
            _merge(deps, b.writers)
            _merge(deps, b.readers)
        return deps

    def _emit_waits(self, eng, deps):
        w = self.waited[eng]
        for k, (sem, v) in deps.items():
            if w.get(k, 0) < v:
                w[k] = v
                self.lists[eng].append(lambda e, h=sem.h, v=v: e.wait_ge(h, v))

    def _record(self, ev, reads, writes, partial):
        key = ev[0].name
        for b in writes:
            if partial:
                _merge(b.writers, {key: ev})
            else:
                b.writers = {key: ev}
                b.readers = {}
        for b in reads:
            if b not in writes:
                _merge(b.readers, {key: ev})

    def op(self, eng, fn, reads=(), writes=(), partial=False):
        if self.dead:
            return
        reads, writes = list(reads), list(writes)
        self._emit_waits(eng, self._deps(reads, writes))
        sem = self.esem[eng]
        sem.count += 1
        self.lists[eng].append(lambda e, fn=fn, h=sem.h: fn(e).then_inc(h, 1))
        self._record((sem, sem.count), reads, writes, partial)
        self.n_inst += 1

    def dma(self, q, out, in_, reads=(), writes=(), owner=None, partial=False, group_cont=False, **kw):
        if self.dead:
            return
        reads, writes = list(reads), list(writes)
        deps = self._deps(reads, writes)
        if owner.dsem is None:
            owner.dsem = self.dpool.pop()
            self.all_dsems.append(owner.dsem)
        sem = owner.dsem
        if not group_cont and sem.count > 0:
            _merge(deps, {sem.name: (sem, sem.count)})
        self._emit_waits(q, deps)
        sem.count += 16
        self.lists[q].append(
            lambda e, o=out, i=in_, h=sem.h, kw=kw: e.dma_start(out=o, in_=i, **kw).then_inc(h, 16))
        self._record((sem, sem.count), reads, writes, partial)
        self.n_inst += 1

    @contextmanager
    def scope(self, flush=False):
        st = ExitStack()
        bufs = []
        self.stacks.append(st)
        self.bufstack.append(bufs)
        try:
            with st:
                yield
                if flush:
                    for b in bufs:
                        sem = b.dsem
                        if sem is not None and sem.count > 0 and self.waited["sp"].get(sem.name, 0) < sem.count:
                            self.waited["sp"][sem.name] = sem.count
                            self.lists["sp"].append(lambda e, h=sem.h, v=sem.count: e.wait_ge(h, v))
                    self.flush()
        finally:
            self.stacks.pop()
            self.bufstack.pop()
            for b in bufs:
                if b.dsem is not None:
                    self.dpool.insert(0, b.dsem)
                    b.dsem = None

    def phase(self, name):
        return self.scope(flush=True)

    def flush(self):
        nc = self.nc
        lists = self.lists
        with nc.Block() as block:
            @block.tensor
            def _(e):
                for f in lists["pe"]:
                    f(e)

            @block.scalar
            def _(e):
                for f in lists["act"]:
                    f(e)

            @block.vector
            def _(e):
                for f in lists["dve"]:
                    f(e)

            @block.gpsimd
            def _(e):
                for f in lists["pool"]:
                    f(e)

            @block.sync
            def _(e):
                for f in lists["sp"]:
                    f(e)
        self.lists = {e: [] for e in ENGS}

    def dump(self, name, ap, shape, dtype, bufs):
        d = self.nc.dram_tensor(name, list(shape), dtype, kind="ExternalOutput").ap()
        b = self.buf("dump_" + name)
        self.dma("sp", d, ap, reads=bufs, owner=b)

    def finish(self):
        for sem in self.all_dsems:
            if sem.count > 0 and self.waited["sp"].get(sem.name, 0) < sem.count:
                self.waited["sp"][sem.name] = sem.count
                self.lists["sp"].append(lambda e, h=sem.h, v=sem.count: e.wait_ge(h, v))


class WStream:
    def __init__(self, P, name, KT, NC, nslots, extra=()):
        self.P, self.KT, self.NC = P, KT, NC
        self.tiles = [P.sb("%s_w%d" % (name, i), [128, KT, NC], BF16) for i in range(nslots)]
        self.bufs = [P.buf("%s_wb%d" % (name, i)) for i in range(nslots)]
        for i, t in enumerate(extra):
            self.tiles.append(t)
            self.bufs.append(P.buf("%s_wx%d" % (name, i)))
        self.n = 0

    def load(self, w2d, k0c, kt, col_ranges):
        P = self.P
        s = self.n % len(self.tiles)
        self.n += 1
        tile, buf = self.tiles[s], self.bufs[s]
        first = True
        off = 0
        for (c0, ncols) in col_ranges:
            src = w2d[k0c * 128:(k0c + kt) * 128, c0:c0 + ncols].rearrange("(kc p) n -> p kc n", p=128)
            step = max(1, min(16, 2048 // max(1, ncols // 64)))
            step = 16
            for a in range(0, kt, step):
                b = min(kt, a + step)
                P.dma("pool", tile[:, a:b, off:off + ncols], src[:, a:b, :], writes=[buf], owner=buf,
                      partial=not first, group_cont=not first)
                first = False
            off += ncols
        return tile, buf


def ws_load_bf16(ws, src3d, kt, srcbufs, q="act"):
    P = ws.P
    sl = ws.n % len(ws.tiles)
    ws.n += 1
    tile, buf = ws.tiles[sl], ws.bufs[sl]
    P.dma(q, tile[:, 0:kt, :], src3d, reads=srcbufs, writes=[buf], owner=buf)
    return tile, buf


def wd_convert_jobs(cfg, Wd, WC):
    FC, D = cfg.FC, cfg.D
    KH = (FC + 1) // 2
    jobs = []
    for cb in range(D // 256):
        for ki, (k0, kt) in enumerate(((0, KH), (KH, FC - KH))):
            tid = cb * 2 + ki
            src = Wd[k0 * 128:(k0 + kt) * 128, cb * 256:(cb + 1) * 256].rearrange("(kc p) n -> p kc n", p=128)
            dst = WC[tid].rearrange("p (k n) -> p k n", n=256)
            for a in range(0, kt, 16):
                b = min(kt, a + 16)
                jobs.append((dst[:, a:b, :], src[:, a:b, :]))
    return jobs


def mm_group(P, ps_ap, ps_bufs, wtile, wbuf, wc0, wn, kcs, at, atbufs, at_kc0, t0, tn, start, stop):
    def fn(e):
        inst = None
        n = len(kcs)
        for i, kc in enumerate(kcs):
            inst = e.matmul(ps_ap, wtile[:, kc, wc0:wc0 + wn], at[:, at_kc0 + kc, t0:t0 + tn],
                            start=(start and i == 0), stop=(stop and i == n - 1))
        return inst
    P.op("pe", fn, reads=[wbuf] + list(atbufs), writes=ps_bufs, partial=not start)


class Env:
    pass


def make_consts(P, env):
    env.identf = P.sb("identf", [128, 128], F32)
    env.identb = P.sb("identb", [128, 128], BF16)
    env.identf_b = P.buf("identf")
    env.identb_b = P.buf("identb")
    for t, b in ((env.identf, env.identf_b), (env.identb, env.identb_b)):
        P.op("pool", lambda e, t=t: e.memset(t[:, :], 1.0), writes=[b])
        P.op("pool", lambda e, t=t: e.affine_select(
            out=t[:, :], in_=t[:, :], pattern=[[-1, 128]], compare_op=ALU.is_equal, fill=0.0,
            base=0, channel_multiplier=1), reads=[b], writes=[b])


def load_cols(P, env, vec_ap, ncols, name, pu, pub):
    tmp = P.sb(name + "_r", [ncols, 128], F32)
    tmpb = P.buf(name + "_r")
    out = P.sb(name, [128, ncols], F32)
    outb = P.buf(name)
    P.dma("sp", tmp[:, :], vec_ap.rearrange("(c p) -> c p", p=128), writes=[tmpb], owner=tmpb)
    P.op("pe", lambda e: e.transpose(out=pu[:, 0:ncols], in_=tmp[:, :], identity=env.identf[0:ncols, 0:ncols]),
         reads=[tmpb, env.identf_b], writes=[pub])
    P.op("dve", lambda e: e.tensor_copy(out=out[:, :], in_=pu[:, 0:ncols]), reads=[pub], writes=[outb])
    return out, outb


def acc_squares(P, env, h_ap, hbuf, cols, first, tmp, tmpb):
    if first:
        P.op("pool", lambda e: e.tensor_tensor(out=env.acc[:, cols], in0=h_ap, in1=h_ap, op=ALU.mult),
             reads=[hbuf], writes=[env.acc_b], partial=True)
    else:
        n = cols.stop - cols.start
        P.op("pool", lambda e: e.tensor_tensor(out=tmp[:, 0:n], in0=h_ap, in1=h_ap, op=ALU.mult),
             reads=[hbuf], writes=[tmpb])
        P.op("pool", lambda e: e.tensor_tensor(out=env.acc[:, cols], in0=env.acc[:, cols], in1=tmp[:, 0:n], op=ALU.add),
             reads=[tmpb, env.acc_b], writes=[env.acc_b], partial=True)


def norm_stats_acc(P, env, cfg, D, pus, pubs):
    S = cfg.S
    onesf = P.sb("ns_ones", [128, 128], F32)
    onesb = P.buf("ns_ones")
    P.op("pool", lambda e: e.memset(onesf[:, :], 1.0), writes=[onesb])
    ntg = S // 512
    nu = (ntg + 1) // 2

    def fn(e):
        inst = None
        for tg in range(ntg):
            u = pus[tg // 2]
            inst = e.matmul(u[:, (tg % 2) * 512:(tg % 2) * 512 + 512], onesf[:, :],
                            env.acc[:, tg * 512:(tg + 1) * 512], start=True, stop=True)
        return inst
    P.op("pe", fn, reads=[env.acc_b, onesb], writes=pubs[:nu])
    for j in range(nu):
        w = min(1024, S - j * 1024)
        sl = slice(j * 1024, j * 1024 + w)
        P.op("dve", lambda e, j=j, w=w, sl=sl: e.tensor_scalar(
            out=env.rstd[:, sl], in0=pus[j][:, 0:w], scalar1=1.0 / D, scalar2=cfg.eps,
            op0=ALU.mult, op1=ALU.add), reads=[pubs[j]], writes=[env.rstd_b], partial=j > 0)
    P.op("act", lambda e: e.activation(out=env.rstd[:, :], in_=env.rstd[:, :], func=AF.Sqrt),
         reads=[env.rstd_b], writes=[env.rstd_b])
    P.op("dve", lambda e: e.reciprocal(out=env.rstd[:, :], in_=env.rstd[:, :]),
         reads=[env.rstd_b], writes=[env.rstd_b])


def norm_stats(P, env, cfg, Hap, Hb, KC, pus, pubs):
    S = cfg.S
    D = KC * 128
    hs = [P.sb("ns_h%d" % i, [128, S], F32) for i in range(2)]
    hsb = [P.buf("ns_h%d" % i) for i in range(2)]
    sq = [P.sb("ns_q%d" % i, [128, S], F32) for i in range(2)]
    sqb = [P.buf("ns_q%d" % i) for i in range(2)]
    onesf = P.sb("ns_ones", [128, 128], F32)
    onesb = P.buf("ns_ones")
    P.op("pool", lambda e: e.memset(onesf[:, :], 1.0), writes=[onesb])
    ntg = S // 512
    for kc in range(KC):
        i = kc % 2
        P.dma("sp", hs[i][:, :], Hap[kc * 128:(kc + 1) * 128, :], reads=[Hb], writes=[hsb[i]], owner=hsb[i])
        P.op("act", lambda e, i=i: e.activation(out=sq[i][:, :], in_=hs[i][:, :], func=AF.Square),
             reads=[hsb[i]], writes=[sqb[i]])

        def fn(e, i=i, kc=kc):
            inst = None
            for tg in range(ntg):
                u = pus[tg // 2]
                inst = e.matmul(u[:, (tg % 2) * 512:(tg % 2) * 512 + 512], onesf[:, :],
                                sq[i][:, tg * 512:(tg + 1) * 512], start=(kc == 0), stop=(kc == KC - 1))
            return inst
        nu = (ntg + 1) // 2
        P.op("pe", fn, reads=[sqb[i], onesb], writes=pubs[:nu], partial=kc > 0)
    for j in range((ntg + 1) // 2):
        w = min(1024, S - j * 1024)
        sl = slice(j * 1024, j * 1024 + w)
        P.op("dve", lambda e, j=j, w=w, sl=sl: e.tensor_scalar(
            out=env.rstd[:, sl], in0=pus[j][:, 0:w], scalar1=1.0 / D, scalar2=cfg.eps,
            op0=ALU.mult, op1=ALU.add), reads=[pubs[j]], writes=[env.rstd_b], partial=j > 0)
    P.op("act", lambda e: e.activation(out=env.rstd[:, :], in_=env.rstd[:, :], func=AF.Sqrt),
         reads=[env.rstd_b], writes=[env.rstd_b])
    P.op("dve", lambda e: e.reciprocal(out=env.rstd[:, :], in_=env.rstd[:, :]),
         reads=[env.rstd_b], writes=[env.rstd_b])


def norm_apply(P, env, cfg, Hap, Hb, g_ap, KC, xn, xnb, pu, pub):
    S = cfg.S
    gcol, gcolb = load_cols(P, env, g_ap, KC, "na_g", pu, pub)
    hs = [P.sb("na_h%d" % i, [128, S], F32) for i in range(3)]
    hsb = [P.buf("na_h%d" % i) for i in range(3)]
    for kc in range(KC):
        i = kc % 3
        P.dma("sp", hs[i][:, :], Hap[kc * 128:(kc + 1) * 128, :], reads=[Hb], writes=[hsb[i]], owner=hsb[i])
        P.op("dve", lambda e, i=i, kc=kc: e.scalar_tensor_tensor(
            out=xn[:, kc, :], in0=hs[i][:, :], scalar=gcol[:, kc:kc + 1], in1=env.rstd[:, :],
            op0=ALU.mult, op1=ALU.mult), reads=[hsb[i], gcolb, env.rstd_b], writes=[xnb], partial=kc > 0)


def gemm_plain(P, cfg, at, atb, KC, W2d, N, epilogue, pus, pubs, pre=None):
    S = cfg.S
    TU = min(1024, S)
    NCW = min(256, N)
    ws = WStream(P, "gp", KC, NCW, 2)
    ui = 0
    for c0 in range(0, N, NCW):
        wt, wb = ws.load(W2d, 0, KC, [(c0, NCW)])
        for sub in range(NCW // 128):
            n0 = c0 + sub * 128
            if pre is not None:
                pre(n0)
            for th in range(S // TU):
                u, ub = pus[ui % len(pus)], pubs[ui % len(pus)]
                ui += 1
                for tg in range(TU // 512):
                    mm_group(P, u[:, tg * 512:(tg + 1) * 512], [ub], wt, wb, sub * 128, 128,
                             list(range(KC)), at, [atb], 0, th * TU + tg * 512, 512, True, True)
                epilogue(n0, th, TU, u, ub, ui)


def store_epilogue(P, cfg, out_ap, outb, name):
    S = cfg.S
    stg = [P.sb("%s_st%d" % (name, i), [128, S], BF16) for i in range(2)]
    stgb = [P.buf("%s_st%d" % (name, i)) for i in range(2)]
    state = {"n": 0}

    def epi(n0, th, TU, u, ub, ui):
        i = state["n"] % 2
        st, sbf = stg[i], stgb[i]
        if ui % 2:
            P.op("act", lambda e: e.activation(out=st[:, th * TU:(th + 1) * TU], in_=u[:, 0:TU], func=AF.Copy),
                 reads=[ub], writes=[sbf], partial=True)
        else:
            P.op("dve", lambda e: e.tensor_copy(out=st[:, th * TU:(th + 1) * TU], in_=u[:, 0:TU]),
                 reads=[ub], writes=[sbf], partial=True)
        if (th + 1) * TU == S:
            P.dma("sp", out_ap[n0:n0 + 128, :], st[:, :], reads=[sbf], writes=[outb], owner=sbf, partial=True)
            state["n"] += 1
    return epi


def resid_epilogue(P, env, cfg, Hin, Hinb, Hout, Houtb, name):
    S = cfg.S
    hs = [P.sb("%s_h%d" % (name, i), [128, S], F32) for i in range(2)]
    hsb = [P.buf("%s_h%d" % (name, i)) for i in range(2)]
    sqt = P.sb("%s_sq" % name, [128, min(1024, S)], F32)
    sqtb = P.buf("%s_sq" % name)
    state = {"n": 0}

    def pre(n0):
        i = state["n"] % 2
        P.dma("sp", hs[i][:, :], Hin[n0:n0 + 128, :], reads=[Hinb], writes=[hsb[i]], owner=hsb[i])

    def epi(n0, th, TU, u, ub, ui):
        i = state["n"] % 2
        h, hb = hs[i], hsb[i]
        sl = slice(th * TU, (th + 1) * TU)
        P.op("dve", lambda e: e.tensor_tensor(out=h[:, sl], in0=u[:, 0:TU], in1=h[:, sl], op=ALU.add),
             reads=[ub, hb], writes=[hb], partial=True)
        if state["n"] == 0:
            P.op("act", lambda e: e.activation(out=env.acc[:, sl], in_=h[:, sl], func=AF.Square),
                 reads=[hb], writes=[env.acc_b], partial=True)
        else:
            P.op("act", lambda e: e.activation(out=sqt[:, 0:TU], in_=h[:, sl], func=AF.Square),
                 reads=[hb], writes=[sqtb])
            P.op("dve", lambda e: e.tensor_tensor(out=env.acc[:, sl], in0=env.acc[:, sl], in1=sqt[:, 0:TU], op=ALU.add),
                 reads=[sqtb, env.acc_b], writes=[env.acc_b], partial=True)
        if (th + 1) * TU == S:
            P.dma("sp", Hout[n0:n0 + 128, :], h[:, :], reads=[hb], writes=[Houtb], owner=hb, partial=True)
            state["n"] += 1
    return pre, epi


def load_at(P, at, atb, src_ap, KC, q="sp"):
    src = src_ap.rearrange("(kc p) t -> p kc t", p=128)
    for c in range(0, KC, 8):
        ce = min(KC, c + 8)
        P.dma(q, at[:, c:ce, :], src[:, c:ce, :], writes=[atb], owner=atb, partial=c > 0, group_cont=c > 0)


def up_proj(P, env, cfg, xn, xnb, Wup, wconv_ap, bconv_ap, ACTT, ACTTb, pus, pubs, conv_jobs=(), WCb=None):
    S, F, FC, KC = cfg.S, cfg.F, cfg.FC, cfg.KC
    TU = min(1024, S)
    wc = []
    for j in range(3):
        wc.append(load_cols(P, env, wconv_ap[j, :], FC, "wc%d" % j, pus[0], pubs[0]))
    bc, bcb = load_cols(P, env, bconv_ap, FC, "bc", pus[0], pubs[0])
    G = P.sb("up_G", [128, S + 2], F32)
    Gb = P.buf("up_G")
    P.op("pool", lambda e: e.memset(G[:, 0:2], 0.0), writes=[Gb])
    tmp = [P.sb("up_t%d" % i, [128, TU], F32) for i in range(2)]
    tmpb = [P.buf("up_t%d" % i) for i in range(2)]
    stg = [P.sb("up_s%d" % i, [128, S], BF16) for i in range(2)]
    stgb = [P.buf("up_s%d" % i) for i in range(2)]
    ws = WStream(P, "up", KC, 256, 2)
    ui = 0
    it = 0
    conv_jobs = list(conv_jobs)
    per = (len(conv_jobs) + FC - 1) // FC if conv_jobs else 0
    cvb = P.buf("up_cv")
    ncv = 0
    for f in range(FC):
        wt, wb = ws.load(Wup, 0, KC, [(f * 128, 128), (F + f * 128, 128)])
        for _ in range(per):
            if ncv < len(conv_jobs):
                dst, src = conv_jobs[ncv]
                P.dma("pool", dst, src, writes=[WCb], owner=cvb, partial=True, group_cont=ncv > 0)
                ncv += 1
        st, sbf = stg[f % 2], stgb[f % 2]
        for th in range(S // TU):
            ug, ugb = pus[ui % 4], pubs[ui % 4]
            uv, uvb = pus[(ui + 1) % 4], pubs[(ui + 1) % 4]
            ui += 2
            for tg in range(TU // 512):
                mm_group(P, ug[:, tg * 512:(tg + 1) * 512], [ugb], wt, wb, 0, 128, list(range(KC)),
                         xn, [xnb], 0, th * TU + tg * 512, 512, True, True)
            for tg in range(TU // 512):
                mm_group(P, uv[:, tg * 512:(tg + 1) * 512], [uvb], wt, wb, 128, 128, list(range(KC)),
                         xn, [xnb], 0, th * TU + tg * 512, 512, True, True)
            t0 = th * TU
            tm, tmb = tmp[it % 2], tmpb[it % 2]
            it += 1
            P.op("act", lambda e, ug=ug, t0=t0: e.activation(out=G[:, 2 + t0:2 + t0 + TU], in_=ug[:, 0:TU], func=AF.Copy),
                 reads=[ugb], writes=[Gb], partial=True)
            P.op("dve", lambda e, tm=tm, t0=t0, f=f: e.tensor_scalar(
                out=tm[:, :], in0=G[:, 2 + t0:2 + t0 + TU], scalar1=wc[2][0][:, f:f + 1], scalar2=bc[:, f:f + 1],
                op0=ALU.mult, op1=ALU.add), reads=[Gb, wc[2][1], bcb], writes=[tmb])
            P.op("dve", lambda e, tm=tm, t0=t0, f=f: e.scalar_tensor_tensor(
                out=tm[:, :], in0=G[:, 1 + t0:1 + t0 + TU], scalar=wc[1][0][:, f:f + 1], in1=tm[:, :],
                op0=ALU.mult, op1=ALU.add), reads=[Gb, wc[1][1], tmb], writes=[tmb])
            P.op("dve", lambda e, tm=tm, t0=t0, f=f: e.scalar_tensor_tensor(
                out=tm[:, :], in0=G[:, t0:t0 + TU], scalar=wc[0][0][:, f:f + 1], in1=tm[:, :],
                op0=ALU.mult, op1=ALU.add), reads=[Gb, wc[0][1], tmb], writes=[tmb])
            P.op("act", lambda e, tm=tm: e.activation(out=tm[:, :], in_=tm[:, :], func=AF.Silu),
                 reads=[tmb], writes=[tmb])
            P.op("dve", lambda e, tm=tm, st=st, uv=uv, t0=t0: e.tensor_tensor(
                out=st[:, t0:t0 + TU], in0=uv[:, 0:TU], in1=tm[:, :], op=ALU.mult),
                reads=[uvb, tmb], writes=[sbf], partial=True)
        P.dma("sp", ACTT[:, :, f * 512:(f + 1) * 512].rearrange("g p t -> p g t"),
              st[:, :].rearrange("p (g t) -> p g t", t=512), reads=[sbf], writes=[ACTTb], owner=sbf, partial=True)


def down_proj(P, env, cfg, atflat, atb, ACTT, ACTTb, Wd, Hin, Hinb, Hout, Houtb, pus, pubs, WC=None, WCb=None):
    S, D, FC = cfg.S, cfg.D, cfg.FC
    TG = 512
    KH = (FC + 1) // 2
    khs = [(0, KH), (KH, FC - KH)]
    at = atflat[:, 0:FC * TG].rearrange("p (kc t) -> p kc t", t=TG)
    athb = [P.buf("dn_at_lo"), P.buf("dn_at_hi")]
    extra = []
    used = FC * TG
    if atflat.shape[1] - used >= KH * 256:
        extra.append(atflat[:, used:used + KH * 256].rearrange("p (k n) -> p k n", n=256))
    ws = WStream(P, "dn", KH, 256, 2, extra=extra)
    hs = [P.sb("dn_h%d" % i, [128, 2, TG], F32) for i in range(3)]
    hsb = [P.buf("dn_h%d" % i) for i in range(3)]
    sqt = P.sb("dn_sq", [128, TG], F32)
    sqtb = P.buf("dn_sq")
    ui = 0
    hi = 0
    for g in range(S // TG):
        t0 = g * TG
        for (k0, kt), hb_ in zip(khs, athb):
            P.dma("sp", atflat[:, k0 * TG:(k0 + kt) * TG], ACTT[g][:, k0 * TG:(k0 + kt) * TG], reads=[ACTTb],
                  writes=[hb_], owner=hb_)
        for c0 in range(0, D, 256):
            h, hb = hs[hi % 3], hsb[hi % 3]
            hi += 1
            P.dma("sp", h[:, :, :], Hin[c0:c0 + 256, t0:t0 + TG].rearrange("(s p) t -> p s t", p=128),
                  reads=[Hinb], writes=[hb], owner=hb)
            u, ub = pus[ui % 4], pubs[ui % 4]
            ui += 1
            for ki, (k0, kt) in enumerate(khs):
                if WC is not None:
                    tid = (c0 // 256) * 2 + ki
                    wt, wb = ws_load_bf16(ws, WC[tid].rearrange("p (k n) -> p k n", n=256)[:, 0:kt, :], kt, [WCb])
                else:
                    wt, wb = ws.load(Wd, k0, kt, [(c0, 256)])
                for sub in range(2):
                    mm_group(P, u[:, sub * 512:sub * 512 + TG], [ub], wt, wb, sub * 128, 128, list(range(kt)),
                             at, [athb[ki]], k0, 0, TG, ki == 0, ki == 1)
            P.op("dve", lambda e, h=h, u=u: e.tensor_tensor(
                out=h[:, :, :], in0=u[:, 0:1024].rearrange("p (s t) -> p s t", s=2)[:, :, 0:TG], in1=h[:, :, :],
                op=ALU.add), reads=[ub, hb], writes=[hb])
            for sub in range(2):
                acc_squares(P, env, h[:, sub, :], hb, slice(t0, t0 + TG), c0 == 0 and sub == 0, sqt, sqtb)
            P.dma("sp", Hout[c0:c0 + 256, t0:t0 + TG].rearrange("(s p) t -> p s t", p=128), h[:, :, :],
                  reads=[hb], writes=[Houtb], owner=hb, partial=True)


def forget_gates(P, env, cfg, xn, xnb, wf_ap, bf_ap, NEGC, NEGCb, pu, pub):
    S, H, KC = cfg.S, cfg.H, cfg.KC
    wf = P.sb("fg_w", [128, KC, H], BF16)
    wfb = P.buf("fg_w")
    src = wf_ap.rearrange("(kc p) n -> p kc n", p=128)
    for a in range(0, KC, 16):
        b = min(KC, a + 16)
        P.dma("pool", wf[:, a:b, :], src[:, a:b, :], writes=[wfb], owner=wfb, partial=a > 0, group_cont=a > 0)
    bf = P.sb("fg_b", [H, 1], F32)
    bfb = P.buf("fg_b")
    P.dma("sp", bf[:, :], bf_ap.rearrange("(h o) -> h o", o=1), writes=[bfb], owner=bfb)
    P.op("dve", lambda e: e.tensor_scalar(out=bf[:, :], in0=bf[:, :], scalar1=-1.0, scalar2=None, op0=ALU.mult),
         reads=[bfb], writes=[bfb])
    l = P.sb("fg_l", [H, S], F32)
    lb = P.buf("fg_l")
    ng = P.sb("fg_n", [H, S], F32)
    ngb = P.buf("fg_n")
    TU = min(1024, S)
    for th in range(S // TU):
        for tg in range(TU // 512):
            def fn(e, th=th, tg=tg):
                inst = None
                for kc in range(KC):
                    inst = e.matmul(pu[0:H, tg * 512:(tg + 1) * 512], wf[:, kc, :],
                                    xn[:, kc, th * TU + tg * 512:th * TU + (tg + 1) * 512],
                                    start=(kc == 0), stop=(kc == KC - 1))
                return inst
            P.op("pe", fn, reads=[wfb, xnb], writes=[pub], partial=tg > 0)
        sl = slice(th * TU, (th + 1) * TU)
        P.op("act", lambda e, sl=sl: e.activation(out=l[:, sl], in_=pu[0:H, 0:TU], func=AF.Exp,
                                                   bias=bf[:, 0:1], scale=-1.0),
             reads=[pub, bfb], writes=[lb], partial=True)
    P.op("act", lambda e: e.activation(out=l[:, :], in_=l[:, :], func=AF.Ln, bias=1.0, scale=1.0),
         reads=[lb], writes=[lb])
    P.op("dve", lambda e: e.tensor_tensor_scan(out=ng[:, :], data0=l[:, :], data1=l[:, :], initial=0.0,
                                                op0=ALU.add, op1=ALU.max),
         reads=[lb], writes=[ngb])
    P.dma("sp", NEGC[:, :], ng[:, :], reads=[ngb], writes=[NEGCb], owner=ngb)


def attention(P, env, cfg, mode, q_ap, k_ap, v_ap, srcbufs, nkv, group, OT, OTb, NEGC=None, NEGCb=None):
    S, NTB = cfg.S, cfg.NTB
    scale = 128.0 ** -0.5
    fox = mode == "fox"
    zps = P.ps("at_z", [128, S], F32)
    zb = P.buf("at_z")
    tr = P.ps("at_tr", [128, S], BF16)
    trb = P.buf("at_tr")
    ops_ = P.ps("at_o", [128, 512], F32)
    opb = P.buf("at_o")
    tr2 = P.ps("at_tr2", [128, 1024], BF16)
    tr2b = P.buf("at_tr2")

    def mk(name, shape, dt, n=2):
        return ([P.sb("%s%d" % (name, i), shape, dt) for i in range(n)],
                [P.buf("%s%d" % (name, i)) for i in range(n)])
    kt, ktb = mk("at_k", [128, S], BF16)
    vt, vtb = mk("at_v", [128, S], BF16)
    qt, qtb = mk("at_q", [128, S], BF16)
    vsb, vsbb = mk("at_vs", [128, NTB, 128], BF16)
    NS = 2 if fox else 3
    E, Eb = mk("at_E", [128, S], F32, NS)
    W, Wb = mk("at_W", [128, S], BF16, NS)
    WT, WTb = mk("at_WT", [128, S], BF16, NS)
    oT, oTb = mk("at_oT", [128, S], BF16)
    sm, smb = mk("at_sm", [128, 4], F32, NS)
    sr, srb = mk("at_sr", [128, 4], F32, NS)
    maskadd = P.sb("at_mask", [128, 128], BF16)
    maskb = P.buf("at_mask")
    P.op("pool", lambda e: e.memset(maskadd[:, :], 0.0), writes=[maskb])
    P.op("pool", lambda e: e.affine_select(
        out=maskadd[:, :], in_=maskadd[:, :], pattern=[[-1, 128]],
        compare_op=(ALU.is_ge if fox else ALU.is_gt), fill=-1.0e30,
        base=0, channel_multiplier=1), reads=[maskb], writes=[maskb])
    if fox:
        ncb, ncbb = mk("at_nc", [128, S], F32)
        osb, osbb = mk("at_os", [128, 128], BF16)
    else:
        L, Lb = mk("at_L", [128, S], F32, NS)
        C, Cb = mk("at_C", [128, S + 1], F32, NS)
        ones = P.sb("at_ones", [128, S], F32)
        onesb = P.buf("at_ones")
        P.op("pool", lambda e: e.memset(ones[:, :], 1.0), writes=[onesb])
        for i in range(NS):
            P.op("pool", lambda e, i=i: e.memset(C[i][:, 0:1], 0.0), writes=[Cb[i]])

    def kv_setup(kh):
        ki = kh % 2
        P.dma("sp", kt[ki][:, :], k_ap[kh * 128:(kh + 1) * 128, :], reads=srcbufs, writes=[ktb[ki]], owner=ktb[ki])
        P.dma("sp", vt[ki][:, :], v_ap[kh * 128:(kh + 1) * 128, :], reads=srcbufs, writes=[vtb[ki]], owner=vtb[ki])
        for a0 in range(0, NTB, 8):
            a1 = min(NTB, a0 + 8)

            def fnv(e, a0=a0, a1=a1):
                inst = None
                for a in range(a0, a1):
                    inst = e.transpose(out=tr2[:, (a - a0) * 128:(a - a0 + 1) * 128],
                                       in_=vt[ki][:, a * 128:(a + 1) * 128], identity=env.identb[:, :])
                return inst
            P.op("pe", fnv, reads=[vtb[ki], env.identb_b], writes=[tr2b])
            P.op("act", lambda e, a0=a0, a1=a1: e.activation(
                out=vsb[ki][:, a0:a1, :], in_=tr2[:, 0:(a1 - a0) * 128].rearrange("p (a d) -> p a d", d=128),
                func=AF.Copy), reads=[tr2b], writes=[vsbb[ki]], partial=a0 > 0)

    def q_setup(h, qi):
        P.dma("sp", qt[qi][:, :], q_ap[h * 128:(h + 1) * 128, :], reads=srcbufs, writes=[qtb[qi]], owner=qtb[qi])
        if fox:
            P.dma("sp", ncb[qi][:, :], NEGC[h:h + 1, :].partition_broadcast(128), reads=[NEGCb],
                  writes=[ncbb[qi]], owner=ncbb[qi])

    def stA1(t):
        kh, g, h, qi, i, j = t
        if i == 0:
            if g == 0:
                kv_setup(kh)
            q_setup(h, qi)
        ki = kh % 2
        kend = 128 * (i + 1)

        def fnz(e):
            inst = None
            for c0 in range(0, kend, 512):
                cn = min(512, kend - c0)
                lastc = c0 + cn == kend
                inst = e.matmul(zps[:, c0:c0 + cn], qt[qi][:, i * 128:(i + 1) * 128], kt[ki][:, c0:c0 + cn],
                                start=True, stop=not lastc)
            inst = e.matmul(zps[:, kend - 128:kend], env.identb[:, :], maskadd[:, :], start=False, stop=True)
            return inst
        P.op("pe", fnz, reads=[qtb[qi], ktb[ki], env.identb_b, maskb], writes=[zb])

    def stA2(t, part=None):
        kh, g, h, qi, i, j = t
        kend = 128 * (i + 1)
        d0 = 128 * i
        if not fox and part == "b":
            P.op("dve", lambda e: e.tensor_tensor_scan(
                out=C[j][:, 1:kend + 1], data0=ones[:, 0:kend], data1=L[j][:, 0:kend], initial=0.0,
                op0=ALU.mult, op1=ALU.add), reads=[Lb[j], onesb], writes=[Cb[j]])
            P.op("dve", lambda e: e.tensor_scalar(
                out=sm[j][:, 0:1], in0=C[j][:, kend:kend + 1], scalar1=-1.0, scalar2=None, op0=ALU.mult),
                reads=[Cb[j]], writes=[smb[j]])
            return
        if not fox:
            P.op("act", lambda e: e.activation(out=E[j][:, 0:kend], in_=zps[:, 0:kend], func=AF.Exp, scale=scale),
                 reads=[zb], writes=[Eb[j]])
            P.op("act", lambda e: e.activation(out=L[j][:, 0:kend], in_=E[j][:, 0:kend], func=AF.Ln, bias=1.0, scale=1.0),
                 reads=[Eb[j]], writes=[Lb[j]])
            if part is None:
                stA2(t, "b")
        else:
            P.op("dve", lambda e: e.scalar_tensor_tensor(
                out=E[j][:, 0:kend], in0=zps[:, 0:kend], scalar=scale, in1=ncb[qi][:, 0:kend],
                op0=ALU.mult, op1=ALU.add), reads=[zb, ncbb[qi]], writes=[Eb[j]])
            P.op("dve", lambda e: e.tensor_reduce(
                out=sm[j][:, 0:1], in_=E[j][:, 0:kend], axis=AX.X, op=ALU.max, negate=True),
                reads=[Eb[j]], writes=[smb[j]])

    def stB(t):
        kh, g, h, qi, i, j = t
        kend = 128 * (i + 1)
        d0 = 128 * i
        if not fox:
            P.op("act", lambda e: e.activation(out=L[j][:, 0:kend], in_=C[j][:, 0:kend], func=AF.Exp,
                                               bias=sm[j][:, 0:1], scale=1.0),
                 reads=[Cb[j], smb[j]], writes=[Lb[j]])
            P.op(SB_MULT_ENG, lambda e: e.tensor_tensor(out=W[j][:, 0:kend], in0=E[j][:, 0:kend], in1=L[j][:, 0:kend],
                                                        op=ALU.mult), reads=[Eb[j], Lb[j]], writes=[Wb[j]])
        else:
            P.op("act", lambda e: e.activation(
                out=W[j][:, 0:kend], in_=E[j][:, 0:kend], func=AF.Exp, bias=sm[j][:, 0:1], scale=1.0,
                accum_out=sr[j][:, 1:2]), reads=[Eb[j], smb[j]], writes=[Wb[j], srb[j]])
            P.op("dve", lambda e: e.reciprocal(out=sr[j][:, 2:3], in_=sr[j][:, 1:2]),
                 reads=[srb[j]], writes=[srb[j]])

        def fnt(e):
            inst = None
            for a in range(i + 1):
                inst = e.transpose(out=tr[:, a * 128:(a + 1) * 128], in_=W[j][:, a * 128:(a + 1) * 128],
                                   identity=env.identb[:, :])
            return inst
        P.op("pe", fnt, reads=[Wb[j], env.identb_b], writes=[trb])

    def stC(t, part):
        kh, g, h, qi, i, j = t
        ki = kh % 2
        kend = 128 * (i + 1)
        d0 = 128 * i
        o_t, o_tb = oT[qi], oTb[qi]
        if part == 1:
            if not fox and SB_COPY_ENG == "dve":
                P.op("dve", lambda e: e.tensor_copy(out=WT[j][:, 0:kend], in_=tr[:, 0:kend]),
                     reads=[trb], writes=[WTb[j]])
            else:
                P.op("act", lambda e: e.activation(out=WT[j][:, 0:kend], in_=tr[:, 0:kend], func=AF.Copy),
                     reads=[trb], writes=[WTb[j]])
        if not fox:
            def fno(e):
                inst = None
                for a in range(i + 1):
                    inst = e.matmul(ops_[:, 0:128], vsb[ki][:, a, :], WT[j][:, a * 128:(a + 1) * 128],
                                    start=(a == 0), stop=(a == i))
                return inst
            if part == 1:
                P.op("pe", fno, reads=[vsbb[ki], WTb[j]], writes=[opb])
                return
            P.op("dve", lambda e: e.tensor_copy(out=o_t[:, d0:d0 + 128], in_=ops_[:, 0:128]),
                 reads=[opb], writes=[o_tb], partial=True)
        else:
            def fno(e):
                inst = None
                for a in range(i + 1):
                    inst = e.matmul(ops_[:, 0:128], WT[j][:, a * 128:(a + 1) * 128], vsb[ki][:, a, :],
                                    start=(a == 0), stop=(a == i))
                return inst
            if part == 1:
                P.op("pe", fno, reads=[vsbb[ki], WTb[j]], writes=[opb])
                return
            P.op("act", lambda e: e.activation(out=osb[j][:, :], in_=ops_[:, 0:128], func=AF.Identity,
                                               scale=sr[j][:, 2:3]),
                 reads=[opb, srb[j]], writes=[osbb[j]])
            P.op("pe", lambda e: e.transpose(out=tr2[:, 0:128], in_=osb[j][:, :], identity=env.identb[:, :]),
                 reads=[osbb[j], env.identb_b], writes=[tr2b])
            P.op("dve", lambda e: e.tensor_copy(out=o_t[:, d0:d0 + 128], in_=tr2[:, 0:128]),
                 reads=[tr2b], writes=[o_tb], partial=True)
        if i == NTB - 1:
            P.dma("sp", OT[h * 128:(h + 1) * 128, :], o_t[:, :], reads=[o_tb], writes=[OTb], owner=o_tb, partial=True)

    its = []
    hq = 0
    n = 0
    for kh in range(nkv):
        for g in range(group):
            h = kh * group + g
            qi = hq % 2
            hq += 1
            for i in range(NTB):
                its.append((kh, g, h, qi, i, n % NS))
                n += 1
    N = len(its)

    def at(k):
        return its[k] if 0 <= k < N else None
    if fox:
        for r in range(-2, N):
            if at(r + 2):
                stA1(at(r + 2))
                stA2(at(r + 2))
            if at(r):
                stC(at(r), 1)
            if at(r + 1):
                stB(at(r + 1))
            if at(r):
                stC(at(r), 2)
    else:
        for r in range(-3, N):
            if at(r + 3):
                stA1(at(r + 3))
            if at(r + 2):
                stA2(at(r + 2), "b")
            if at(r):
                stC(at(r), 1)
            if at(r + 1):
                stB(at(r + 1))
            if at(r):
                stC(at(r), 2)
            if at(r + 3):
                stA2(at(r + 3), "a")


def phase_in(P, env, cfg, x, H0, H0b):
    S, D, KC, NTB = cfg.S, cfg.D, cfg.KC, cfg.NTB
    GB = 4
    HC = min(16, KC)
    xs = [P.sb("in_x%d" % i, [128, D], F32) for i in range(2)]
    xsb = [P.buf("in_x%d" % i) for i in range(2)]
    pst = [P.ps("in_p%d" % i, [128, HC * 128], F32) for i in range(2)]
    pstb = [P.buf("in_p%d" % i) for i in range(2)]
    stg = [P.sb("in_s%d" % i, [128, KC, 512], F32) for i in range(2)]
    stgb = [P.buf("in_s%d" % i) for i in range(2)]
    n = 0
    for g in range(NTB // GB):
        st, stb = stg[g % 2], stgb[g % 2]
        for tbi in range(GB):
            tb = g * GB + tbi
            xi = tb % 2
            P.dma("sp", xs[xi][:, :], x[tb * 128:(tb + 1) * 128, :], writes=[xsb[xi]], owner=xsb[xi])
            for hc in range(0, KC, HC):
                pi = n % 2
                n += 1

                def fn(e, xi=xi, pi=pi, hc=hc):
                    inst = None
                    for kc in range(hc, hc + HC):
                        inst = e.transpose(out=pst[pi][:, (kc - hc) * 128:(kc - hc + 1) * 128],
                                           in_=xs[xi][:, kc * 128:(kc + 1) * 128], identity=env.identf[:, :])
                    return inst
                P.op("pe", fn, reads=[xsb[xi], env.identf_b], writes=[pstb[pi]])
                src = pst[pi][:, :].rearrange("p (k t) -> p k t", t=128)
                dst = st[:, hc:hc + HC, tbi * 128:(tbi + 1) * 128]
                if n % 2:
                    P.op("act", lambda e, src=src, dst=dst: e.activation(out=dst, in_=src, func=AF.Copy),
                         reads=[pstb[pi]], writes=[stb], partial=True)
                else:
                    P.op("dve", lambda e, src=src, dst=dst: e.tensor_copy(out=dst, in_=src),
                         reads=[pstb[pi]], writes=[stb], partial=True)
        cols = slice(g * 512, (g + 1) * 512)
        P.dma("sp", H0[:, cols].rearrange("(kc p) t -> p kc t", p=128), st[:, :, :], reads=[stb], writes=[H0b],
              owner=stb, partial=True)
        P.op("act", lambda e, st=st: e.activation(out=st[:, :, :], in_=st[:, :, :], func=AF.Square),
             reads=[stb], writes=[stb])
        P.op("dve", lambda e, st=st, cols=cols: e.tensor_reduce(
            out=env.acc[:, cols], in_=st[:, :, :].rearrange("p k t -> p t k"), axis=AX.X, op=ALU.add),
            reads=[stb], writes=[env.acc_b], partial=True)


def phase_out(P, env, cfg, Hap, Hb, g_ap, out):
    S, D, KC, NTB = cfg.S, cfg.D, cfg.KC, cfg.NTB
    GB = 4
    HC = min(16, KC)
    pst = [P.ps("fo_p%d" % i, [128, HC * 128], F32) for i in range(2)]
    pstb = [P.buf("fo_p%d" % i) for i in range(2)]
    if HC * 128 >= S:
        sp_t, sp_b = pst[0], pstb[0]
    else:
        sp_t, sp_b = P.ps("fo_s", [128, S], F32), P.buf("fo_s")
    nun = (S + 1023) // 1024
    spus = [sp_t[:, j * 1024:min(S, (j + 1) * 1024)] for j in range(nun)]
    spubs = [sp_b] * nun
    norm_stats_acc(P, env, cfg, D, spus, spubs)
    gcol, gcolb = load_cols(P, env, g_ap, KC, "fo_g", pst[1], pstb[1])
    hs = [P.sb("fo_h%d" % i, [128, KC, 512], F32) for i in range(2)]
    hsb = [P.buf("fo_h%d" % i) for i in range(2)]
    ost = [P.sb("fo_o%d" % i, [128, D], F32) for i in range(2)]
    ostb = [P.buf("fo_o%d" % i) for i in range(2)]
    outb = Buf("out")
    n = 0
    for g in range(NTB // GB):
        h, hb = hs[g % 2], hsb[g % 2]
        cols = slice(g * 512, (g + 1) * 512)
        P.dma("sp", h[:, :, :], Hap[:, cols].rearrange("(kc p) t -> p kc t", p=128), reads=[Hb], writes=[hb], owner=hb)
        for kc in range(KC):
            P.op("dve", lambda e, h=h, kc=kc, cols=cols: e.scalar_tensor_tensor(
                out=h[:, kc, :], in0=h[:, kc, :], scalar=gcol[:, kc:kc + 1], in1=env.rstd[:, cols],
                op0=ALU.mult, op1=ALU.mult), reads=[hb, gcolb, env.rstd_b], writes=[hb])
        for tbi in range(GB):
            tb = g * GB + tbi
            o, ob = ost[tb % 2], ostb[tb % 2]
            for hc in range(0, KC, HC):
                pi = n % 2
                n += 1

                def fn(e, h=h, pi=pi, hc=hc, tbi=tbi):
                    inst = None
                    for kc in range(hc, hc + HC):
                        inst = e.transpose(out=pst[pi][:, (kc - hc) * 128:(kc - hc + 1) * 128],
                                           in_=h[:, kc, tbi * 128:(tbi + 1) * 128], identity=env.identf[:, :])
                    return inst
                P.op("pe", fn, reads=[hb, env.identf_b], writes=[pstb[pi]])
                P.op("act", lambda e, o=o, pi=pi, hc=hc: e.activation(
                    out=o[:, hc * 128:(hc + HC) * 128], in_=pst[pi][:, :], func=AF.Copy),
                    reads=[pstb[pi]], writes=[ob], partial=True)
            P.dma("sp", out[tb * 128:(tb + 1) * 128, :], o[:, :], reads=[ob], writes=[outb], owner=ob, partial=True)


PARAMS = [("a_attn_norm", "D"), ("a_w_qkv", "D,3D"), ("a_w_o", "D,D"), ("a_ffn_norm", "D"), ("a_w_up", "D,2F"),
          ("a_w_conv", "3,F"), ("a_b_conv", "F"), ("a_w_down", "F,D"), ("kv_norm", "D"), ("w_kv", "D,KV"),
          ("w_f", "D,H"), ("b_f", "H"), ("b_attn_norm", "D"), ("b_w_q", "D,D"), ("b_w_o", "D,D"),
          ("b_ffn_norm", "D"), ("b_w_up", "D,2F"), ("b_w_conv", "3,F"), ("b_b_conv", "F"), ("b_w_down", "F,D"),
          ("final_norm", "D")]


def param_shape(cfg, spec):
    m = {"D": cfg.D, "3D": 3 * cfg.D, "2F": 2 * cfg.F, "F": cfg.F, "KV": 2 * cfg.KVH * 128, "H": cfg.H, "3": 3}
    return [m[s] for s in spec.split(",")]


def build(cfg, dbg=(), stop_after=None, dbg_att=None):
    nc = bass.Bass("TRN2", target_bir_lowering=False)
    D, S, H, KVH, G, F, KC, FC = cfg.D, cfg.S, cfg.H, cfg.KVH, cfg.G, cfg.F, cfg.KC, cfg.FC
    x = nc.dram_tensor("x", [S, D], F32, kind="ExternalInput").ap()
    prm = {}
    for name, spec in PARAMS:
        prm[name] = nc.dram_tensor(name, param_shape(cfg, spec), F32, kind="ExternalInput").ap()
    out = nc.dram_tensor("out", [S, D], F32, kind="ExternalOutput").ap()
    sbufs = {}

    def scratch(name, shape, dt):
        kind = "ExternalOutput" if name in dbg else "Internal"
        sbufs[name] = Buf(name)
        return nc.dram_tensor(name, shape, dt, kind=kind).ap()
    Hs = [scratch("H%d" % i, [D, S], F32) for i in range(5)]
    Hb = [sbufs["H%d" % i] for i in range(5)]
    QKVT = scratch("QKVT", [3 * D, S], BF16)
    OTA = scratch("OTA", [D, S], BF16)
    ACTA = scratch("ACTA", [S // 512, 128, FC * 512], BF16)
    KVT = scratch("KVT", [2 * KVH * 128, S], BF16)
    NEGC = scratch("NEGC", [H, S], F32)
    QT = scratch("QT", [D, S], BF16)
    OTB = scratch("OTB", [D, S], BF16)
    ACTB = scratch("ACTB", [S // 512, 128, FC * 512], BF16)
    KHh = (FC + 1) // 2
    WCA = scratch("WCA", [2 * (D // 256), 128, KHh * 256], BF16)
    WCB = scratch("WCB", [2 * (D // 256), 128, KHh * 256], BF16)

    class Stop(Exception):
        pass

    def chk(tag):
        if stop_after == tag:
            P.dead = True

    with ExitStack() as top:
        P = Prog(nc, top)
        env = Env()
        env.dbg_att = dbg_att
        env.rstd = P.sb("rstd", [128, S], F32)
        env.rstd_b = P.buf("rstd")
        env.acc = P.sb("acc", [128, S], F32)
        env.acc_b = P.buf("acc")
        try:
            make_consts(P, env)
            with P.phase("init"):
                phase_in(P, env, cfg, x, Hs[0], Hb[0])
            chk("in")

            def units():
                pus = [P.ps("pu%d" % i, [128, 1024], F32) for i in range(4)]
                pubs = [P.buf("pu%d" % i) for i in range(4)]
                return pus, pubs

            def normed(Hi, gname, stats=True):
                with P.phase("norm"):
                    pus, pubs = units()
                    if stats:
                        norm_stats_acc(P, env, cfg, D, pus[0:2], pubs[0:2])
                    norm_apply(P, env, cfg, Hs[Hi], Hb[Hi], prm[gname], KC, xn, xnb, pus[2], pubs[2])

            def ffn(Hi, pre, ACT, ACTb):
                WC = WCA if pre == "a" else WCB
                WCb = sbufs["WCA" if pre == "a" else "WCB"]
                normed(Hi, pre + "_ffn_norm")
                with P.phase("up"):
                    pus, pubs = units()
                    up_proj(P, env, cfg, xn, xnb, prm[pre + "_w_up"], prm[pre + "_w_conv"], prm[pre + "_b_conv"],
                            ACT, ACTb, pus, pubs, conv_jobs=wd_convert_jobs(cfg, prm[pre + "_w_down"], WC), WCb=WCb)
                chk(pre + "_up")
                with P.phase("down"):
                    pus, pubs = units()
                    down_proj(P, env, cfg, xflat, xnb, ACT, ACTb, prm[pre + "_w_down"], Hs[Hi], Hb[Hi],
                              Hs[Hi + 1], Hb[Hi + 1], pus, pubs, WC=WC, WCb=WCb)
                chk(pre + "_down")

            def wo(OT, OTb, wname, Hi):
                with P.phase("wo"):
                    pus, pubs = units()
                    load_at(P, xn, xnb, OT, KC, [OTb])
                    pre_, epi = resid_epilogue(P, env, cfg, Hs[Hi], Hb[Hi], Hs[Hi + 1], Hb[Hi + 1], "wo")
                    gemm_plain(P, cfg, xn, xnb, KC, prm[wname], D, epi, pus, pubs, pre=pre_)

            with P.scope():
                xflat = P.sb("xn", [128, max(KC * S, FC * 512)], BF16)
                xn = xflat[:, 0:KC * S].rearrange("p (kc t) -> p kc t", t=S)
                xnb = P.buf("xn")
                normed(0, "a_attn_norm")
                with P.phase("qkv"):
                    pus, pubs = units()
                    epi = store_epilogue(P, cfg, QKVT, sbufs["QKVT"], "qkv")
                    gemm_plain(P, cfg, xn, xnb, KC, prm["a_w_qkv"], 3 * D, epi, pus, pubs)
            chk("qkv")
            with P.phase("attA"):
                attention(P, env, cfg, "sb", QKVT[0:D, :], QKVT[D:2 * D, :], QKVT[2 * D:3 * D, :], [sbufs["QKVT"]],
                          H, 1, OTA, sbufs["OTA"])
            chk("attA")
            with P.scope():
                xflat = P.sb("xn", [128, max(KC * S, FC * 512)], BF16)
                xn = xflat[:, 0:KC * S].rearrange("p (kc t) -> p kc t", t=S)
                xnb = P.buf("xn")
                wo(OTA, sbufs["OTA"], "a_w_o", 0)
                chk("woA")
                ffn(1, "a", ACTA, sbufs["ACTA"])
            with P.scope():
                xflat = P.sb("xn", [128, max(KC * S, FC * 512)], BF16)
                xn = xflat[:, 0:KC * S].rearrange("p (kc t) -> p kc t", t=S)
                xnb = P.buf("xn")
                normed(2, "kv_norm")
                with P.phase("kv"):
                    pus, pubs = units()
                    epi = store_epilogue(P, cfg, KVT, sbufs["KVT"], "kv")
                    gemm_plain(P, cfg, xn, xnb, KC, prm["w_kv"], 2 * KVH * 128, epi, pus, pubs)
                    forget_gates(P, env, cfg, xn, xnb, prm["w_f"], prm["b_f"], NEGC, sbufs["NEGC"], pus[0], pubs[0])
                chk("kv")
                normed(2, "b_attn_norm", stats=False)
                with P.phase("q"):
                    pus, pubs = units()
                    epi = store_epilogue(P, cfg, QT, sbufs["QT"], "q")
                    gemm_plain(P, cfg, xn, xnb, KC, prm["b_w_q"], D, epi, pus, pubs)
            chk("q")
            with P.phase("attB"):
                attention(P, env, cfg, "fox", QT, KVT[0:KVH * 128, :], KVT[KVH * 128:2 * KVH * 128, :],
                          [sbufs["QT"], sbufs["KVT"]], KVH, G, OTB, sbufs["OTB"], NEGC, sbufs["NEGC"])
            chk("attB")
            with P.scope():
                xflat = P.sb("xn", [128, max(KC * S, FC * 512)], BF16)
                xn = xflat[:, 0:KC * S].rearrange("p (kc t) -> p kc t", t=S)
                xnb = P.buf("xn")
                wo(OTB, sbufs["OTB"], "b_w_o", 2)
                chk("woB")
                ffn(3, "b", ACTB, sbufs["ACTB"])
            with P.phase("final"):
                phase_out(P, env, cfg, Hs[4], Hb[4], prm["final_norm"], out)
        except Stop:
            pass
        P.dead = False
        with P.phase("end"):
            P.finish()
    nc._prog_ninst = P.n_inst
    return nc


def load_at(P, at, atb, src_ap, KC, srcbufs):
    src = src_ap.rearrange("(kc p) t -> p kc t", p=128)
    for c in range(0, KC, 8):
        ce = min(KC, c + 8)
        P.dma("sp", at[:, c:ce, :], src[:, c:ce, :], reads=srcbufs, writes=[atb], owner=atb,
              partial=c > 0, group_cont=c > 0)


_CFG = Cfg()
_NC_CACHE = {}


def kernel(**inputs):
    cfg = _CFG
    B = inputs["x"].shape[0]
    if "nc" not in _NC_CACHE:
        _NC_CACHE["nc"] = build(cfg)
    nc = _NC_CACHE["nc"]
    shared = {}
    for name, spec in PARAMS:
        shared[name] = np.ascontiguousarray(np.asarray(inputs[name], dtype=np.float32).reshape(param_shape(cfg, spec)))
    x = np.asarray(inputs["x"], dtype=np.float32)
    in_maps = []
    for b in range(B):
        m = dict(shared)
        m["x"] = np.ascontiguousarray(x[b])
        in_maps.append(m)
    res = run_bass_kernel_spmd(nc, in_maps, core_ids=list(range(B)))
    return np.stack([np.asarray(r["out"]) for r in res.results], axis=0).astype(np.float32)
```

```python
from contextlib import ExitStack, contextmanager

import numpy as np
import concourse.bass as bass
import concourse.mybir as mybir
from concourse.bass_utils import run_bass_kernel_spmd

F32 = mybir.dt.float32
BF16 = mybir.dt.bfloat16
AF = mybir.ActivationFunctionType
ALU = mybir.AluOpType
AX = mybir.AxisListType

ENGS = ("pe", "act", "dve", "pool", "sp")
SB_MULT_ENG = "dve"
SB_COPY_ENG = "act"


class Cfg:
    def __init__(self, D=4096, S=2048, eps=1e-6):
        self.D, self.S, self.eps = D, S, eps
        self.H = D // 128
        self.KVH = max(1, self.H // 4)
        self.G = self.H // self.KVH
        self.F = ((8 * D // 3 + 255) // 256) * 256
        self.KC = D // 128
        self.FC = self.F // 128
        self.NTB = S // 128


class Sem:
    def __init__(self, h, name):
        self.h, self.name, self.count = h, name, 0


class Buf:
    def __init__(self, name):
        self.name = name
        self.writers = {}
        self.readers = {}
        self.dsem = None


def _merge(dst, src):
    for k, v in src.items():
        if dst.get(k, (None, 0))[1] < v[1]:
            dst[k] = v


class Prog:
    def __init__(self, nc, stack, n_dma_sems=48):
        self.nc = nc
        self.esem = {e: Sem(stack.enter_context(nc.semaphore("es_" + e)), e) for e in ENGS}
        self.dpool = [Sem(stack.enter_context(nc.semaphore("ds%d" % i)), "ds%d" % i)
                      for i in range(n_dma_sems)]
        self.waited = {e: {} for e in ENGS}
        self.lists = {e: [] for e in ENGS}
        self.stacks = [stack]
        self.bufstack = [[]]
        self.all_dsems = []
        self.n_inst = 0
        self.uid = 0
        self.dead = False

    def sb(self, name, shape, dtype):
        self.uid += 1
        return self.stacks[-1].enter_context(self.nc.sbuf_tensor("%s_%d" % (name, self.uid), list(shape), dtype))

    def ps(self, name, shape, dtype):
        self.uid += 1
        return self.stacks[-1].enter_context(self.nc.psum_tensor("%s_%d" % (name, self.uid), list(shape), dtype))

    def buf(self, name):
        b = Buf(name)
        self.bufstack[-1].append(b)
        return b

    def _deps(self, reads, writes):
        deps = {}
        for b in reads:
            _merge(deps, b.writers)
        for b in writes:
            _merge(deps, b.writers)
            _merge(deps, b.readers)
        return deps

    def _emit_waits(self, eng, deps):
        w = self.waited[eng]
        for k, (sem, v) in deps.items():
            if w.get(k, 0) < v:
                w[k] = v
                self.lists[eng].append(lambda e, h=sem.h, v=v: e.wait_ge(h, v))

    def _record(self, ev, reads, writes, partial):
        key = ev[0].name
        for b in writes:
            if partial:
                _merge(b.writers, {key: ev})
            else:
                b.writers = {key: ev}
                b.readers = {}
        for b in reads:
            if b not in writes:
                _merge(b.readers, {key: ev})

    def op(self, eng, fn, reads=(), writes=(), partial=False):
        if self.dead:
            return
        reads, writes = list(reads), list(writes)
        self._emit_waits(eng, self._deps(reads, writes))
        sem = self.esem[eng]
        sem.count += 1
        self.lists[eng].append(lambda e, fn=fn, h=sem.h: fn(e).then_inc(h, 1))
        self._record((sem, sem.count), reads, writes, partial)
        self.n_inst += 1

    def dma(self, q, out, in_, reads=(), writes=(), owner=None, partial=False, group_cont=False, **kw):
        if self.dead:
            return
        reads, writes = list(reads), list(writes)
        deps = self._deps(reads, writes)
        if owner.dsem is None:
            owner.dsem = self.dpool.pop()
            self.all_dsems.append(owner.dsem)
        sem = owner.dsem
        if not group_cont and sem.count > 0:
            _merge(deps, {sem.name: (sem, sem.count)})
        self._emit_waits(q, deps)
        sem.count += 16
        self.lists[q].append(
            lambda e, o=out, i=in_, h=sem.h, kw=kw: e.dma_start(out=o, in_=i, **kw).then_inc(h, 16))
        self._record((sem, sem.count), reads, writes, partial)
        self.n_inst += 1

    @contextmanager
    def scope(self, flush=False):
        st = ExitStack()
        bufs = []
        self.stacks.append(st)
        self.bufstack.append(bufs)
        try:
            with st:
                yield
                if flush:
                    for b in bufs:
                        sem = b.dsem
                        if sem is not None and sem.count > 0 and self.waited["sp"].get(sem.name, 0) < sem.count:
                            self.waited["sp"][sem.name] = sem.count
                            self.lists["sp"].append(lambda e, h=sem.h, v=sem.count: e.wait_ge(h, v))
                    self.flush()
        finally:
            self.stacks.pop()
            self.bufstack.pop()
            for b in bufs:
                if b.dsem is not None:
                    self.dpool.insert(0, b.dsem)
                    b.dsem = None

    def phase(self, name):
        return self.scope(flush=True)

    def flush(self):
        nc = self.nc
        lists = self.lists
        with nc.Block() as block:
            @block.tensor
            def _(e):
                for f in lists["pe"]:
                    f(e)

            @block.scalar
            def _(e):
                for f in lists["act"]:
                    f(e)

            @block.vector
            def _(e):
                for f in lists["dve"]:
                    f(e)

            @block.gpsimd
            def _(e):
                for f in lists["pool"]:
                    f(e)

            @block.sync
            def _(e):
                for f in lists["sp"]:
                    f(e)
        self.lists = {e: [] for e in ENGS}

    def dump(self, name, ap, shape, dtype, bufs):
        d = self.nc.dram_tensor(name, list(shape), dtype, kind="ExternalOutput").ap()
        b = self.buf("dump_" + name)
        self.dma("sp", d, ap, reads=bufs, owner=b)

    def finish(self):
        for sem in self.all_dsems:
            if sem.count > 0 and self.waited["sp"].get(sem.name, 0) < sem.count:
                self.waited["sp"][sem.name] = sem.count
                self.lists["sp"].append(lambda e, h=sem.h, v=sem.count: e.wait_ge(h, v))


class WStream:
    def __init__(self, P, name, KT, NC, nslots, extra=()):
        self.P, self.KT, self.NC = P, KT, NC
        self.tiles = [P.sb("%s_w%d" % (name, i), [128, KT, NC], BF16) for i in range(nslots)]
        self.bufs = [P.buf("%s_wb%d" % (name, i)) for i in range(nslots)]
        for i, t in enumerate(extra):
            self.tiles.append(t)
            self.bufs.append(P.buf("%s_wx%d" % (name, i)))
        self.n = 0

    def load(self, w2d, k0c, kt, col_ranges):
        P = self.P
        s = self.n % len(self.tiles)
        self.n += 1
        tile, buf = self.tiles[s], self.bufs[s]
        first = True
        off = 0
        for (c0, ncols) in col_ranges:
            src = w2d[k0c * 128:(k0c + kt) * 128, c0:c0 + ncols].rearrange("(kc p) n -> p kc n", p=128)
            step = max(1, min(16, 2048 // max(1, ncols // 64)))
            step = 16
            for a in range(0, kt, step):
                b = min(kt, a + step)
                P.dma("pool", tile[:, a:b, off:off + ncols], src[:, a:b, :], writes=[buf], owner=buf,
                      partial=not first, group_cont=not first)
                first = False
            off += ncols
        return tile, buf


def ws_load_bf16(ws, src3d, kt, srcbufs, q="act"):
    P = ws.P
    sl = ws.n % len(ws.tiles)
    ws.n += 1
    tile, buf = ws.tiles[sl], ws.bufs[sl]
    P.dma(q, tile[:, 0:kt, :], src3d, reads=srcbufs, writes=[buf], owner=buf)
    return tile, buf


def wd_convert_jobs(cfg, Wd, WC):
    FC, D = cfg.FC, cfg.D
    KH = (FC + 1) // 2
    jobs = []
    for cb in range(D // 256):
        for ki, (k0, kt) in enumerate(((0, KH), (KH, FC - KH))):
            tid = cb * 2 + ki
            src = Wd[k0 * 128:(k0 + kt) * 128, cb * 256:(cb + 1) * 256].rearrange("(kc p) n -> p kc n", p=128)
            dst = WC[tid].rearrange("p (k n) -> p k n", n=256)
            for a in range(0, kt, 16):
                b = min(kt, a + 16)
                jobs.append((dst[:, a:b, :], src[:, a:b, :]))
    return jobs


def mm_group(P, ps_ap, ps_bufs, wtile, wbuf, wc0, wn, kcs, at, atbufs, at_kc0, t0, tn, start, stop):
    def fn(e):
        inst = None
        n = len(kcs)
        for i, kc in enumerate(kcs):
            inst = e.matmul(ps_ap, wtile[:, kc, wc0:wc0 + wn], at[:, at_kc0 + kc, t0:t0 + tn],
                            start=(start and i == 0), stop=(stop and i == n - 1))
        return inst
    P.op("pe", fn, reads=[wbuf] + list(atbufs), writes=ps_bufs, partial=not start)


class Env:
    pass


def make_consts(P, env):
    env.identf = P.sb("identf", [128, 128], F32)
    env.identb = P.sb("identb", [128, 128], BF16)
    env.identf_b = P.buf("identf")
    env.identb_b = P.buf("identb")
    for t, b in ((env.identf, env.identf_b), (env.identb, env.identb_b)):
        P.op("pool", lambda e, t=t: e.memset(t[:, :], 1.0), writes=[b])
        P.op("pool", lambda e, t=t: e.affine_select(
            out=t[:, :], in_=t[:, :], pattern=[[-1, 128]], compare_op=ALU.is_equal, fill=0.0,
            base=0, channel_multiplier=1), reads=[b], writes=[b])


def load_cols(P, env, vec_ap, ncols, name, pu, pub):
    tmp = P.sb(name + "_r", [ncols, 128], F32)
    tmpb = P.buf(name + "_r")
    out = P.sb(name, [128, ncols], F32)
    outb = P.buf(name)
    P.dma("sp", tmp[:, :], vec_ap.rearrange("(c p) -> c p", p=128), writes=[tmpb], owner=tmpb)
    P.op("pe", lambda e: e.transpose(out=pu[:, 0:ncols], in_=tmp[:, :], identity=env.identf[0:ncols, 0:ncols]),
         reads=[tmpb, env.identf_b], writes=[pub])
    P.op("dve", lambda e: e.tensor_copy(out=out[:, :], in_=pu[:, 0:ncols]), reads=[pub], writes=[outb])
    return out, outb


def acc_squares(P, env, h_ap, hbuf, cols, first, tmp, tmpb):
    if first:
        P.op("pool", lambda e: e.tensor_tensor(out=env.acc[:, cols], in0=h_ap, in1=h_ap, op=ALU.mult),
             reads=[hbuf], writes=[env.acc_b], partial=True)
    else:
        n = cols.stop - cols.start
        P.op("pool", lambda e: e.tensor_tensor(out=tmp[:, 0:n], in0=h_ap, in1=h_ap, op=ALU.mult),
             reads=[hbuf], writes=[tmpb])
        P.op("pool", lambda e: e.tensor_tensor(out=env.acc[:, cols], in0=env.acc[:, cols], in1=tmp[:, 0:n], op=ALU.add),
             reads=[tmpb, env.acc_b], writes=[env.acc_b], partial=True)


def norm_stats_acc(P, env, cfg, D, pus, pubs):
    S = cfg.S
    onesf = P.sb("ns_ones", [128, 128], F32)
    onesb = P.buf("ns_ones")
    P.op("pool", lambda e: e.memset(onesf[:, :], 1.0), writes=[onesb])
    ntg = S // 512
    nu = (ntg + 1) // 2

    def fn(e):
        inst = None
        for tg in range(ntg):
            u = pus[tg // 2]
            inst = e.matmul(u[:, (tg % 2) * 512:(tg % 2) * 512 + 512], onesf[:, :],
                            env.acc[:, tg * 512:(tg + 1) * 512], start=True, stop=True)
        return inst
    P.op("pe", fn, reads=[env.acc_b, onesb], writes=pubs[:nu])
    for j in range(nu):
        w = min(1024, S - j * 1024)
        sl = slice(j * 1024, j * 1024 + w)
        P.op("dve", lambda e, j=j, w=w, sl=sl: e.tensor_scalar(
            out=env.rstd[:, sl], in0=pus[j][:, 0:w], scalar1=1.0 / D, scalar2=cfg.eps,
            op0=ALU.mult, op1=ALU.add), reads=[pubs[j]], writes=[env.rstd_b], partial=j > 0)
    P.op("act", lambda e: e.activation(out=env.rstd[:, :], in_=env.rstd[:, :], func=AF.Sqrt),
         reads=[env.rstd_b], writes=[env.rstd_b])
    P.op("dve", lambda e: e.reciprocal(out=env.rstd[:, :], in_=env.rstd[:, :]),
         reads=[env.rstd_b], writes=[env.rstd_b])


def norm_stats(P, env, cfg, Hap, Hb, KC, pus, pubs):
    S = cfg.S
    D = KC * 128
    hs = [P.sb("ns_h%d" % i, [128, S], F32) for i in range(2)]
    hsb = [P.buf("ns_h%d" % i) for i in range(2)]
    sq = [P.sb("ns_q%d" % i, [128, S], F32) for i in range(2)]
    sqb = [P.buf("ns_q%d" % i) for i in range(2)]
    onesf = P.sb("ns_ones", [128, 128], F32)
    onesb = P.buf("ns_ones")
    P.op("pool", lambda e: e.memset(onesf[:, :], 1.0), writes=[onesb])
    ntg = S // 512
    for kc in range(KC):
        i = kc % 2
        P.dma("sp", hs[i][:, :], Hap[kc * 128:(kc + 1) * 128, :], reads=[Hb], writes=[hsb[i]], owner=hsb[i])
        P.op("act", lambda e, i=i: e.activation(out=sq[i][:, :], in_=hs[i][:, :], func=AF.Square),
             reads=[hsb[i]], writes=[sqb[i]])

        def fn(e, i=i, kc=kc):
            inst = None
            for tg in range(ntg):
                u = pus[tg // 2]
                inst = e.matmul(u[:, (tg % 2) * 512:(tg % 2) * 512 + 512], onesf[:, :],
                                sq[i][:, tg * 512:(tg + 1) * 512], start=(kc == 0), stop=(kc == KC - 1))
            return inst
        nu = (ntg + 1) // 2
        P.op("pe", fn, reads=[sqb[i], onesb], writes=pubs[:nu], partial=kc > 0)
    for j in range((ntg + 1) // 2):
        w = min(1024, S - j * 1024)
        sl = slice(j * 1024, j * 1024 + w)
        P.op("dve", lambda e, j=j, w=w, sl=sl: e.tensor_scalar(
            out=env.rstd[:, sl], in0=pus[j][:, 0:w], scalar1=1.0 / D, scalar2=cfg.eps,
            op0=ALU.mult, op1=ALU.add), reads=[pubs[j]], writes=[env.rstd_b], partial=j > 0)
    P.op("act", lambda e: e.activation(out=env.rstd[:, :], in_=env.rstd[:, :], func=AF.Sqrt),
         reads=[env.rstd_b], writes=[env.rstd_b])
    P.op("dve", lambda e: e.reciprocal(out=env.rstd[:, :], in_=env.rstd[:, :]),
         reads=[env.rstd_b], writes=[env.rstd_b])


def norm_apply(P, env, cfg, Hap, Hb, g_ap, KC, xn, xnb, pu, pub):
    S = cfg.S
    gcol, gcolb = load_cols(P, env, g_ap, KC, "na_g", pu, pub)
    hs = [P.sb("na_h%d" % i, [128, S], F32) for i in range(3)]
    hsb = [P.buf("na_h%d" % i) for i in range(3)]
    for kc in range(KC):
        i = kc % 3
        P.dma("sp", hs[i][:, :], Hap[kc * 128:(kc + 1) * 128, :], reads=[Hb], writes=[hsb[i]], owner=hsb[i])
        P.op("dve", lambda e, i=i, kc=kc: e.scalar_tensor_tensor(
            out=xn[:, kc, :], in0=hs[i][:, :], scalar=gcol[:, kc:kc + 1], in1=env.rstd[:, :],
            op0=ALU.mult, op1=ALU.mult), reads=[hsb[i], gcolb, env.rstd_b], writes=[xnb], partial=kc > 0)


def gemm_plain(P, cfg, at, atb, KC, W2d, N, epilogue, pus, pubs, pre=None):
    S = cfg.S
    TU = min(1024, S)
    NCW = min(256, N)
    ws = WStream(P, "gp", KC, NCW, 2)
    ui = 0
    for c0 in range(0, N, NCW):
        wt, wb = ws.load(W2d, 0, KC, [(c0, NCW)])
        for sub in range(NCW // 128):
            n0 = c0 + sub * 128
            if pre is not None:
                pre(n0)
            for th in range(S // TU):
                u, ub = pus[ui % len(pus)], pubs[ui % len(pus)]
                ui += 1
                for tg in range(TU // 512):
                    mm_group(P, u[:, tg * 512:(tg + 1) * 512], [ub], wt, wb, sub * 128, 128,
                             list(range(KC)), at, [atb], 0, th * TU + tg * 512, 512, True, True)
                epilogue(n0, th, TU, u, ub, ui)


def store_epilogue(P, cfg, out_ap, outb, name):
    S = cfg.S
    stg = [P.sb("%s_st%d" % (name, i), [128, S], BF16) for i in range(2)]
    stgb = [P.buf("%s_st%d" % (name, i)) for i in range(2)]
    state = {"n": 0}

    def epi(n0, th, TU, u, ub, ui):
        i = state["n"] % 2
        st, sbf = stg[i], stgb[i]
        if ui % 2:
            P.op("act", lambda e: e.activation(out=st[:, th * TU:(th + 1) * TU], in_=u[:, 0:TU], func=AF.Copy),
                 reads=[ub], writes=[sbf], partial=True)
        else:
            P.op("dve", lambda e: e.tensor_copy(out=st[:, th * TU:(th + 1) * TU], in_=u[:, 0:TU]),
                 reads=[ub], writes=[sbf], partial=True)
        if (th + 1) * TU == S:
            P.dma("sp", out_ap[n0:n0 + 128, :], st[:, :], reads=[sbf], writes=[outb], owner=sbf, partial=True)
            state["n"] += 1
    return epi


def resid_epilogue(P, env, cfg, Hin, Hinb, Hout, Houtb, name):
    S = cfg.S
    hs = [P.sb("%s_h%d" % (name, i), [128, S], F32) for i in range(2)]
    hsb = [P.buf("%s_h%d" % (name, i)) for i in range(2)]
    sqt = P.sb("%s_sq" % name, [128, min(1024, S)], F32)
    sqtb = P.buf("%s_sq" % name)
    state = {"n": 0}

    def pre(n0):
        i = state["n"] % 2
        P.dma("sp", hs[i][:, :], Hin[n0:n0 + 128, :], reads=[Hinb], writes=[hsb[i]], owner=hsb[i])

    def epi(n0, th, TU, u, ub, ui):
        i = state["n"] % 2
        h, hb = hs[i], hsb[i]
        sl = slice(th * TU, (th + 1) * TU)
        P.op("dve", lambda e: e.tensor_tensor(out=h[:, sl], in0=u[:, 0:TU], in1=h[:, sl], op=ALU.add),
             reads=[ub, hb], writes=[hb], partial=True)
        if state["n"] == 0:
            P.op("act", lambda e: e.activation(out=env.acc[:, sl], in_=h[:, sl], func=AF.Square),
                 reads=[hb], writes=[env.acc_b], partial=True)
        else:
            P.op("act", lambda e: e.activation(out=sqt[:, 0:TU], in_=h[:, sl], func=AF.Square),
                 reads=[hb], writes=[sqtb])
            P.op("dve", lambda e: e.tensor_tensor(out=env.acc[:, sl], in0=env.acc[:, sl], in1=sqt[:, 0:TU], op=ALU.add),
                 reads=[sqtb, env.acc_b], writes=[env.acc_b], partial=True)
        if (th + 1) * TU == S:
            P.dma("sp", Hout[n0:n0 + 128, :], h[:, :], reads=[hb], writes=[Houtb], owner=hb, partial=True)
            state["n"] += 1
    return pre, epi


def load_at(P, at, atb, src_ap, KC, q="sp"):
    src = src_ap.rearrange("(kc p) t -> p kc t", p=128)
    for c in range(0, KC, 8):
        ce = min(KC, c + 8)
        P.dma(q, at[:, c:ce, :], src[:, c:ce, :], writes=[atb], owner=atb, partial=c > 0, group_cont=c > 0)


def up_proj(P, env, cfg, xn, xnb, Wup, wconv_ap, bconv_ap, ACTT, ACTTb, pus, pubs, conv_jobs=(), WCb=None):
    S, F, FC, KC = cfg.S, cfg.F, cfg.FC, cfg.KC
    TU = min(1024, S)
    wc = []
    for j in range(3):
        wc.append(load_cols(P, env, wconv_ap[j, :], FC, "wc%d" % j, pus[0], pubs[0]))
    bc, bcb = load_cols(P, env, bconv_ap, FC, "bc", pus[0], pubs[0])
    G = P.sb("up_G", [128, S + 2], F32)
    Gb = P.buf("up_G")
    P.op("pool", lambda e: e.memset(G[:, 0:2], 0.0), writes=[Gb])
    tmp = [P.sb("up_t%d" % i, [128, TU], F32) for i in range(2)]
    tmpb = [P.buf("up_t%d" % i) for i in range(2)]
    stg = [P.sb("up_s%d" % i, [128, S], BF16) for i in range(2)]
    stgb = [P.buf("up_s%d" % i) for i in range(2)]
    ws = WStream(P, "up", KC, 256, 2)
    ui = 0
    it = 0
    conv_jobs = list(conv_jobs)
    per = (len(conv_jobs) + FC - 1) // FC if conv_jobs else 0
    cvb = P.buf("up_cv")
    ncv = 0
    for f in range(FC):
        wt, wb = ws.load(Wup, 0, KC, [(f * 128, 128), (F + f * 128, 128)])
        for _ in range(per):
            if ncv < len(conv_jobs):
                dst, src = conv_jobs[ncv]
                P.dma("pool", dst, src, writes=[WCb], owner=cvb, partial=True, group_cont=ncv > 0)
                ncv += 1
        st, sbf = stg[f % 2], stgb[f % 2]
        for th in range(S // TU):
            ug, ugb = pus[ui % 4], pubs[ui % 4]
            uv, uvb = pus[(ui + 1) % 4], pubs[(ui + 1) % 4]
            ui += 2
            for tg in range(TU // 512):
                mm_group(P, ug[:, tg * 512:(tg + 1) * 512], [ugb], wt, wb, 0, 128, list(range(KC)),
                         xn, [xnb], 0, th * TU + tg * 512, 512, True, True)
            for tg in range(TU // 512):
                mm_group(P, uv[:, tg * 512:(tg + 1) * 512], [uvb], wt, wb, 128, 128, list(range(KC)),
                         xn, [xnb], 0, th * TU + tg * 512, 512, True, True)
            t0 = th * TU
            tm, tmb = tmp[it % 2], tmpb[it % 2]
            it += 1
            P.op("act", lambda e, ug=ug, t0=t0: e.activation(out=G[:, 2 + t0:2 + t0 + TU], in_=ug[:, 0:TU], func=AF.Copy),
                 reads=[ugb], writes=[Gb], partial=True)
            P.op("dve", lambda e, tm=tm, t0=t0, f=f: e.tensor_scalar(
                out=tm[:, :], in0=G[:, 2 + t0:2 + t0 + TU], scalar1=wc[2][0][:, f:f + 1], scalar2=bc[:, f:f + 1],
                op0=ALU.mult, op1=ALU.add), reads=[Gb, wc[2][1], bcb], writes=[tmb])
            P.op("dve", lambda e, tm=tm, t0=t0, f=f: e.scalar_tensor_tensor(
                out=tm[:, :], in0=G[:, 1 + t0:1 + t0 + TU], scalar=wc[1][0][:, f:f + 1], in1=tm[:, :],
                op0=ALU.mult, op1=ALU.add), reads=[Gb, wc[1][1], tmb], writes=[tmb])
            P.op("dve", lambda e, tm=tm, t0=t0, f=f: e.scalar_tensor_tensor(
                out=tm[:, :], in0=G[:, t0:t0 + TU], scalar=wc[0][0][:, f:f + 1], in1=tm[:, :],
                op0=ALU.mult, op1=ALU.add), reads=[Gb, wc[0][1], tmb], writes=[tmb])
            P.op("act", lambda e, tm=tm: e.activation(out=tm[:, :], in_=tm[:, :], func=AF.Silu),
                 reads=[tmb], writes=[tmb])
            P.op("dve", lambda e, tm=tm, st=st, uv=uv, t0=t0: e.tensor_tensor(
                out=st[:, t0:t0 + TU], in0=uv[:, 0:TU], in1=tm[:, :], op=ALU.mult),
                reads=[uvb, tmb], writes=[sbf], partial=True)
        P.dma("sp", ACTT[:, :, f * 512:(f + 1) * 512].rearrange("g p t -> p g t"),
              st[:, :].rearrange("p (g t) -> p g t", t=512), reads=[sbf], writes=[ACTTb], owner=sbf, partial=True)


def down_proj(P, env, cfg, atflat, atb, ACTT, ACTTb, Wd, Hin, Hinb, Hout, Houtb, pus, pubs, WC=None, WCb=None):
    S, D, FC = cfg.S, cfg.D, cfg.FC
    TG = 512
    KH = (FC + 1) // 2
    khs = [(0, KH), (KH, FC - KH)]
    at = atflat[:, 0:FC * TG].rearrange("p (kc t) -> p kc t", t=TG)
    athb = [P.buf("dn_at_lo"), P.buf("dn_at_hi")]
    extra = []
    used = FC * TG
    if atflat.shape[1] - used >= KH * 256:
        extra.append(atflat[:, used:used + KH * 256].rearrange("p (k n) -> p k n", n=256))
    ws = WStream(P, "dn", KH, 256, 2, extra=extra)
    hs = [P.sb("dn_h%d" % i, [128, 2, TG], F32) for i in range(3)]
    hsb = [P.buf("dn_h%d" % i) for i in range(3)]
    sqt = P.sb("dn_sq", [128, TG], F32)
    sqtb = P.buf("dn_sq")
    ui = 0
    hi = 0
    for g in range(S // TG):
        t0 = g * TG
        for (k0, kt), hb_ in zip(khs, athb):
            P.dma("sp", atflat[:, k0 * TG:(k0 + kt) * TG], ACTT[g][:, k0 * TG:(k0 + kt) * TG], reads=[ACTTb],
                  writes=[hb_], owner=hb_)
        for c0 in range(0, D, 256):
            h, hb = hs[hi % 3], hsb[hi % 3]
            hi += 1
            P.dma("sp", h[:, :, :], Hin[c0:c0 + 256, t0:t0 + TG].rearrange("(s p) t -> p s t", p=128),
                  reads=[Hinb], writes=[hb], owner=hb)
            u, ub = pus[ui % 4], pubs[ui % 4]
            ui += 1
            for ki, (k0, kt) in enumerate(khs):
                if WC is not None:
                    tid = (c0 // 256) * 2 + ki
                    wt, wb = ws_load_bf16(ws, WC[tid].rearrange("p (k n) -> p k n", n=256)[:, 0:kt, :], kt, [WCb])
                else:
                    wt, wb = ws.load(Wd, k0, kt, [(c0, 256)])
                for sub in range(2):
                    mm_group(P, u[:, sub * 512:sub * 512 + TG], [ub], wt, wb, sub * 128, 128, list(range(kt)),
                             at, [athb[ki]], k0, 0, TG, ki == 0, ki == 1)
            P.op("dve", lambda e, h=h, u=u: e.tensor_tensor(
                out=h[:, :, :], in0=u[:, 0:1024].rearrange("p (s t) -> p s t", s=2)[:, :, 0:TG], in1=h[:, :, :],
                op=ALU.add), reads=[ub, hb], writes=[hb])
            for sub in range(2):
                acc_squares(P, env, h[:, sub, :], hb, slice(t0, t0 + TG), c0 == 0 and sub == 0, sqt, sqtb)
            P.dma("sp", Hout[c0:c0 + 256, t0:t0 + TG].rearrange("(s p) t -> p s t", p=128), h[:, :, :],
                  reads=[hb], writes=[Houtb], owner=hb, partial=True)


def forget_gates(P, env, cfg, xn, xnb, wf_ap, bf_ap, NEGC, NEGCb, pu, pub):
    S, H, KC = cfg.S, cfg.H, cfg.KC
    wf = P.sb("fg_w", [128, KC, H], BF16)
    wfb = P.buf("fg_w")
    src = wf_ap.rearrange("(kc p) n -> p kc n", p=128)
    for a in range(0, KC, 16):
        b = min(KC, a + 16)
        P.dma("pool", wf[:, a:b, :], src[:, a:b, :], writes=[wfb], owner=wfb, partial=a > 0, group_cont=a > 0)
    bf = P.sb("fg_b", [H, 1], F32)
    bfb = P.buf("fg_b")
    P.dma("sp", bf[:, :], bf_ap.rearrange("(h o) -> h o", o=1), writes=[bfb], owner=bfb)
    P.op("dve", lambda e: e.tensor_scalar(out=bf[:, :], in0=bf[:, :], scalar1=-1.0, scalar2=None, op0=ALU.mult),
         reads=[bfb], writes=[bfb])
    l = P.sb("fg_l", [H, S], F32)
    lb = P.buf("fg_l")
    ng = P.sb("fg_n", [H, S], F32)
    ngb = P.buf("fg_n")
    TU = min(1024, S)
    for th in range(S // TU):
        for tg in range(TU // 512):
            def fn(e, th=th, tg=tg):
                inst = None
                for kc in range(KC):
                    inst = e.matmul(pu[0:H, tg * 512:(tg + 1) * 512], wf[:, kc, :],
                                    xn[:, kc, th * TU + tg * 512:th * TU + (tg + 1) * 512],
                                    start=(kc == 0), stop=(kc == KC - 1))
                return inst
            P.op("pe", fn, reads=[wfb, xnb], writes=[pub], partial=tg > 0)
        sl = slice(th * TU, (th + 1) * TU)
        P.op("act", lambda e, sl=sl: e.activation(out=l[:, sl], in_=pu[0:H, 0:TU], func=AF.Exp,
                                                   bias=bf[:, 0:1], scale=-1.0),
             reads=[pub, bfb], writes=[lb], partial=True)
    P.op("act", lambda e: e.activation(out=l[:, :], in_=l[:, :], func=AF.Ln, bias=1.0, scale=1.0),
         reads=[lb], writes=[lb])
    P.op("dve", lambda e: e.tensor_tensor_scan(out=ng[:, :], data0=l[:, :], data1=l[:, :], initial=0.0,
                                                op0=ALU.add, op1=ALU.max),
         reads=[lb], writes=[ngb])
    P.dma("sp", NEGC[:, :], ng[:, :], reads=[ngb], writes=[NEGCb], owner=ngb)


def attention(P, env, cfg, mode, q_ap, k_ap, v_ap, srcbufs, nkv, group, OT, OTb, NEGC=None, NEGCb=None):
    S, NTB = cfg.S, cfg.NTB
    scale = 128.0 ** -0.5
    fox = mode == "fox"
    zps = P.ps("at_z", [128, S], F32)
    zb = P.buf("at_z")
    tr = P.ps("at_tr", [128, S], BF16)
    trb = P.buf("at_tr")
    ops_ = P.ps("at_o", [128, 512], F32)
    opb = P.buf("at_o")
    tr2 = P.ps("at_tr2", [128, 1024], BF16)
    tr2b = P.buf("at_tr2")

    def mk(name, shape, dt, n=2):
        return ([P.sb("%s%d" % (name, i), shape, dt) for i in range(n)],
                [P.buf("%s%d" % (name, i)) for i in range(n)])
    kt, ktb = mk("at_k", [128, S], BF16)
    vt, vtb = mk("at_v", [128, S], BF16)
    qt, qtb = mk("at_q", [128, S], BF16)
    vsb, vsbb = mk("at_vs", [128, NTB, 128], BF16)
    NS = 2 if fox else 3
    E, Eb = mk("at_E", [128, S], F32, NS)
    W, Wb = mk("at_W", [128, S], BF16, NS)
    WT, WTb = mk("at_WT", [128, S], BF16, NS)
    oT, oTb = mk("at_oT", [128, S], BF16)
    sm, smb = mk("at_sm", [128, 4], F32, NS)
    sr, srb = mk("at_sr", [128, 4], F32, NS)
    maskadd = P.sb("at_mask", [128, 128], BF16)
    maskb = P.buf("at_mask")
    P.op("pool", lambda e: e.memset(maskadd[:, :], 0.0), writes=[maskb])
    P.op("pool", lambda e: e.affine_select(
        out=maskadd[:, :], in_=maskadd[:, :], pattern=[[-1, 128]],
        compare_op=(ALU.is_ge if fox else ALU.is_gt), fill=-1.0e30,
        base=0, channel_multiplier=1), reads=[maskb], writes=[maskb])
    if fox:
        ncb, ncbb = mk("at_nc", [128, S], F32)
        osb, osbb = mk("at_os", [128, 128], BF16)
    else:
        L, Lb = mk("at_L", [128, S], F32, NS)
        C, Cb = mk("at_C", [128, S + 1], F32, NS)
        ones = P.sb("at_ones", [128, S], F32)
        onesb = P.buf("at_ones")
        P.op("pool", lambda e: e.memset(ones[:, :], 1.0), writes=[onesb])
        for i in range(NS):
            P.op("pool", lambda e, i=i: e.memset(C[i][:, 0:1], 0.0), writes=[Cb[i]])

    def kv_setup(kh):
        ki = kh % 2
        P.dma("sp", kt[ki][:, :], k_ap[kh * 128:(kh + 1) * 128, :], reads=srcbufs, writes=[ktb[ki]], owner=ktb[ki])
        P.dma("sp", vt[ki][:, :], v_ap[kh * 128:(kh + 1) * 128, :], reads=srcbufs, writes=[vtb[ki]], owner=vtb[ki])
        for a0 in range(0, NTB, 8):
            a1 = min(NTB, a0 + 8)

            def fnv(e, a0=a0, a1=a1):
                inst = None
                for a in range(a0, a1):
                    inst = e.transpose(out=tr2[:, (a - a0) * 128:(a - a0 + 1) * 128],
                                       in_=vt[ki][:, a * 128:(a + 1) * 128], identity=env.identb[:, :])
                return inst
            P.op("pe", fnv, reads=[vtb[ki], env.identb_b], writes=[tr2b])
            P.op("act", lambda e, a0=a0, a1=a1: e.activation(
                out=vsb[ki][:, a0:a1, :], in_=tr2[:, 0:(a1 - a0) * 128].rearrange("p (a d) -> p a d", d=128),
                func=AF.Copy), reads=[tr2b], writes=[vsbb[ki]], partial=a0 > 0)

    def q_setup(h, qi):
        P.dma("sp", qt[qi][:, :], q_ap[h * 128:(h + 1) * 128, :], reads=srcbufs, writes=[qtb[qi]], owner=qtb[qi])
        if fox:
            P.dma("sp", ncb[qi][:, :], NEGC[h:h + 1, :].partition_broadcast(128), reads=[NEGCb],
                  writes=[ncbb[qi]], owner=ncbb[qi])

    nheads = nkv * group

    def stA1(t):
        kh, g, h, qi, i, j = t
        if i == 0:
            if h == 0:
                kv_setup(0)
                q_setup(0, 0)
            if h + 1 < nheads:
                q_setup(h + 1, (qi + 1) % 2)
        if i == min(4, NTB - 1) and g == 0 and kh + 1 < nkv:
            kv_setup(kh + 1)
        ki = kh % 2
        kend = 128 * (i + 1)

        def fnz(e):
            inst = None
            for c0 in range(0, kend, 512):
                cn = min(512, kend - c0)
                lastc = c0 + cn == kend
                inst = e.matmul(zps[:, c0:c0 + cn], qt[qi][:, i * 128:(i + 1) * 128], kt[ki][:, c0:c0 + cn],
                                start=True, stop=not lastc)
            inst = e.matmul(zps[:, kend - 128:kend], env.identb[:, :], maskadd[:, :], start=False, stop=True)
            return inst
        P.op("pe", fnz, reads=[qtb[qi], ktb[ki], env.identb_b, maskb], writes=[zb])

    def stA2(t, part=None):
        kh, g, h, qi, i, j = t
        kend = 128 * (i + 1)
        d0 = 128 * i
        if not fox and part == "b":
            P.op("dve", lambda e: e.tensor_tensor_scan(
                out=C[j][:, 1:kend + 1], data0=ones[:, 0:kend], data1=L[j][:, 0:kend], initial=0.0,
                op0=ALU.mult, op1=ALU.add), reads=[Lb[j], onesb], writes=[Cb[j]])
            P.op("dve", lambda e: e.tensor_scalar(
                out=sm[j][:, 0:1], in0=C[j][:, kend:kend + 1], scalar1=-1.0, scalar2=None, op0=ALU.mult),
                reads=[Cb[j]], writes=[smb[j]])
            return
        if not fox:
            P.op("act", lambda e: e.activation(out=E[j][:, 0:kend], in_=zps[:, 0:kend], func=AF.Exp, scale=scale),
                 reads=[zb], writes=[Eb[j]])
            P.op("act", lambda e: e.activation(out=L[j][:, 0:kend], in_=E[j][:, 0:kend], func=AF.Ln, bias=1.0, scale=1.0),
                 reads=[Eb[j]], writes=[Lb[j]])
            if part is None:
                stA2(t, "b")
        else:
            P.op("dve", lambda e: e.scalar_tensor_tensor(
                out=E[j][:, 0:kend], in0=zps[:, 0:kend], scalar=scale, in1=ncb[qi][:, 0:kend],
                op0=ALU.mult, op1=ALU.add), reads=[zb, ncbb[qi]], writes=[Eb[j]])
            P.op("dve", lambda e: e.tensor_reduce(
                out=sm[j][:, 0:1], in_=E[j][:, 0:kend], axis=AX.X, op=ALU.max, negate=True),
                reads=[Eb[j]], writes=[smb[j]])

    def stB(t):
        kh, g, h, qi, i, j = t
        kend = 128 * (i + 1)
        d0 = 128 * i
        if not fox:
            P.op("act", lambda e: e.activation(out=L[j][:, 0:kend], in_=C[j][:, 0:kend], func=AF.Exp,
                                               bias=sm[j][:, 0:1], scale=1.0),
                 reads=[Cb[j], smb[j]], writes=[Lb[j]])
            P.op(SB_MULT_ENG, lambda e: e.tensor_tensor(out=W[j][:, 0:kend], in0=E[j][:, 0:kend], in1=L[j][:, 0:kend],
                                                        op=ALU.mult), reads=[Eb[j], Lb[j]], writes=[Wb[j]])
        else:
            P.op("act", lambda e: e.activation(
                out=W[j][:, 0:kend], in_=E[j][:, 0:kend], func=AF.Exp, bias=sm[j][:, 0:1], scale=1.0,
                accum_out=sr[j][:, 1:2]), reads=[Eb[j], smb[j]], writes=[Wb[j], srb[j]])
            P.op("dve", lambda e: e.reciprocal(out=sr[j][:, 2:3], in_=sr[j][:, 1:2]),
                 reads=[srb[j]], writes=[srb[j]])

        def fnt(e):
            inst = None
            for a in range(i + 1):
                inst = e.transpose(out=tr[:, a * 128:(a + 1) * 128], in_=W[j][:, a * 128:(a + 1) * 128],
                                   identity=env.identb[:, :])
            return inst
        P.op("pe", fnt, reads=[Wb[j], env.identb_b], writes=[trb])

    def stC(t, part):
        kh, g, h, qi, i, j = t
        ki = kh % 2
        kend = 128 * (i + 1)
        d0 = 128 * i
        o_t, o_tb = oT[qi], oTb[qi]
        if part == 1:
            if not fox and SB_COPY_ENG == "dve":
                P.op("dve", lambda e: e.tensor_copy(out=WT[j][:, 0:kend], in_=tr[:, 0:kend]),
                     reads=[trb], writes=[WTb[j]])
            else:
                P.op("act", lambda e: e.activation(out=WT[j][:, 0:kend], in_=tr[:, 0:kend], func=AF.Copy),
                     reads=[trb], writes=[WTb[j]])
        if not fox:
            def fno(e):
                inst = None
                for a in range(i + 1):
                    inst = e.matmul(ops_[:, 0:128], vsb[ki][:, a, :], WT[j][:, a * 128:(a + 1) * 128],
                                    start=(a == 0), stop=(a == i))
                return inst
            if part == 1:
                P.op("pe", fno, reads=[vsbb[ki], WTb[j]], writes=[opb])
                return
            P.op("dve", lambda e: e.tensor_copy(out=o_t[:, d0:d0 + 128], in_=ops_[:, 0:128]),
                 reads=[opb], writes=[o_tb], partial=True)
        else:
            def fno(e):
                inst = None
                for a in range(i + 1):
                    inst = e.matmul(ops_[:, 0:128], WT[j][:, a * 128:(a + 1) * 128], vsb[ki][:, a, :],
                                    start=(a == 0), stop=(a == i))
                return inst
            if part == 1:
                P.op("pe", fno, reads=[vsbb[ki], WTb[j]], writes=[opb])
                return
            P.op("act", lambda e: e.activation(out=osb[j][:, :], in_=ops_[:, 0:128], func=AF.Identity,
                                               scale=sr[j][:, 2:3]),
                 reads=[opb, srb[j]], writes=[osbb[j]])
            P.op("pe", lambda e: e.transpose(out=tr2[:, 0:128], in_=osb[j][:, :], identity=env.identb[:, :]),
                 reads=[osbb[j], env.identb_b], writes=[tr2b])
            P.op("dve", lambda e: e.tensor_copy(out=o_t[:, d0:d0 + 128], in_=tr2[:, 0:128]),
                 reads=[tr2b], writes=[o_tb], partial=True)
        if i == NTB - 1:
            P.dma("sp", OT[h * 128:(h + 1) * 128, :], o_t[:, :], reads=[o_tb], writes=[OTb], owner=o_tb, partial=True)

    its = []
    hq = 0
    n = 0
    for kh in range(nkv):
        for g in range(group):
            h = kh * group + g
            qi = hq % 2
            hq += 1
            for i in range(NTB):
                its.append((kh, g, h, qi, i, n % NS))
                n += 1
    N = len(its)

    def at(k):
        return its[k] if 0 <= k < N else None
    if fox:
        for r in range(-2, N):
            if at(r + 2):
                stA1(at(r + 2))
                stA2(at(r + 2))
            if at(r):
                stC(at(r), 1)
            if at(r + 1):
                stB(at(r + 1))
            if at(r):
                stC(at(r), 2)
    else:
        for r in range(-3, N):
            if at(r + 3):
                stA1(at(r + 3))
            if at(r + 2):
                stA2(at(r + 2), "b")
            if at(r):
                stC(at(r), 1)
            if at(r + 1):
                stB(at(r + 1))
            if at(r):
                stC(at(r), 2)
            if at(r + 3):
                stA2(at(r + 3), "a")


def phase_in(P, env, cfg, x, H0, H0b):
    S, D, KC, NTB = cfg.S, cfg.D, cfg.KC, cfg.NTB
    GB = 4
    HC = min(16, KC)
    xs = [P.sb("in_x%d" % i, [128, D], F32) for i in range(2)]
    xsb = [P.buf("in_x%d" % i) for i in range(2)]
    pst = [P.ps("in_p%d" % i, [128, HC * 128], F32) for i in range(2)]
    pstb = [P.buf("in_p%d" % i) for i in range(2)]
    stg = [P.sb("in_s%d" % i, [128, KC, 512], F32) for i in range(2)]
    stgb = [P.buf("in_s%d" % i) for i in range(2)]
    n = 0
    for g in range(NTB // GB):
        st, stb = stg[g % 2], stgb[g % 2]
        for tbi in range(GB):
            tb = g * GB + tbi
            xi = tb % 2
            P.dma("sp", xs[xi][:, :], x[tb * 128:(tb + 1) * 128, :], writes=[xsb[xi]], owner=xsb[xi])
            for hc in range(0, KC, HC):
                pi = n % 2
                n += 1

                def fn(e, xi=xi, pi=pi, hc=hc):
                    inst = None
                    for kc in range(hc, hc + HC):
                        inst = e.transpose(out=pst[pi][:, (kc - hc) * 128:(kc - hc + 1) * 128],
                                           in_=xs[xi][:, kc * 128:(kc + 1) * 128], identity=env.identf[:, :])
                    return inst
                P.op("pe", fn, reads=[xsb[xi], env.identf_b], writes=[pstb[pi]])
                src = pst[pi][:, :].rearrange("p (k t) -> p k t", t=128)
                dst = st[:, hc:hc + HC, tbi * 128:(tbi + 1) * 128]
                if n % 2:
                    P.op("act", lambda e, src=src, dst=dst: e.activation(out=dst, in_=src, func=AF.Copy),
                         reads=[pstb[pi]], writes=[stb], partial=True)
                else:
                    P.op("dve", lambda e, src=src, dst=dst: e.tensor_copy(out=dst, in_=src),
                         reads=[pstb[pi]], writes=[stb], partial=True)
        cols = slice(g * 512, (g + 1) * 512)
        P.dma("sp", H0[:, cols].rearrange("(kc p) t -> p kc t", p=128), st[:, :, :], reads=[stb], writes=[H0b],
              owner=stb, partial=True)
        P.op("act", lambda e, st=st: e.activation(out=st[:, :, :], in_=st[:, :, :], func=AF.Square),
             reads=[stb], writes=[stb])
        P.op("dve", lambda e, st=st, cols=cols: e.tensor_reduce(
            out=env.acc[:, cols], in_=st[:, :, :].rearrange("p k t -> p t k"), axis=AX.X, op=ALU.add),
            reads=[stb], writes=[env.acc_b], partial=True)


def phase_out(P, env, cfg, Hap, Hb, g_ap, out):
    S, D, KC, NTB = cfg.S, cfg.D, cfg.KC, cfg.NTB
    GB = 4
    HC = min(16, KC)
    pst = [P.ps("fo_p%d" % i, [128, HC * 128], F32) for i in range(2)]
    pstb = [P.buf("fo_p%d" % i) for i in range(2)]
    if HC * 128 >= S:
        sp_t, sp_b = pst[0], pstb[0]
    else:
        sp_t, sp_b = P.ps("fo_s", [128, S], F32), P.buf("fo_s")
    nun = (S + 1023) // 1024
    spus = [sp_t[:, j * 1024:min(S, (j + 1) * 1024)] for j in range(nun)]
    spubs = [sp_b] * nun
    norm_stats_acc(P, env, cfg, D, spus, spubs)
    gcol, gcolb = load_cols(P, env, g_ap, KC, "fo_g", pst[1], pstb[1])
    hs = [P.sb("fo_h%d" % i, [128, KC, 512], F32) for i in range(2)]
    hsb = [P.buf("fo_h%d" % i) for i in range(2)]
    ost = [P.sb("fo_o%d" % i, [128, D], F32) for i in range(2)]
    ostb = [P.buf("fo_o%d" % i) for i in range(2)]
    outb = Buf("out")
    n = 0
    for g in range(NTB // GB):
        h, hb = hs[g % 2], hsb[g % 2]
        cols = slice(g * 512, (g + 1) * 512)
        P.dma("sp", h[:, :, :], Hap[:, cols].rearrange("(kc p) t -> p kc t", p=128), reads=[Hb], writes=[hb], owner=hb)
        for kc in range(KC):
            P.op("dve", lambda e, h=h, kc=kc, cols=cols: e.scalar_tensor_tensor(
                out=h[:, kc, :], in0=h[:, kc, :], scalar=gcol[:, kc:kc + 1], in1=env.rstd[:, cols],
                op0=ALU.mult, op1=ALU.mult), reads=[hb, gcolb, env.rstd_b], writes=[hb])
        for tbi in range(GB):
            tb = g * GB + tbi
            o, ob = ost[tb % 2], ostb[tb % 2]
            for hc in range(0, KC, HC):
                pi = n % 2
                n += 1

                def fn(e, h=h, pi=pi, hc=hc, tbi=tbi):
                    inst = None
                    for kc in range(hc, hc + HC):
                        inst = e.transpose(out=pst[pi][:, (kc - hc) * 128:(kc - hc + 1) * 128],
                                           in_=h[:, kc, tbi * 128:(tbi + 1) * 128], identity=env.identf[:, :])
                    return inst
                P.op("pe", fn, reads=[hb, env.identf_b], writes=[pstb[pi]])
                P.op("act", lambda e, o=o, pi=pi, hc=hc: e.activation(
                    out=o[:, hc * 128:(hc + HC) * 128], in_=pst[pi][:, :], func=AF.Copy),
                    reads=[pstb[pi]], writes=[ob], partial=True)
            P.dma("sp", out[tb * 128:(tb + 1) * 128, :], o[:, :], reads=[ob], writes=[outb], owner=ob, partial=True)


PARAMS = [("a_attn_norm", "D"), ("a_w_qkv", "D,3D"), ("a_w_o", "D,D"), ("a_ffn_norm", "D"), ("a_w_up", "D,2F"),
          ("a_w_conv", "3,F"), ("a_b_conv", "F"), ("a_w_down", "F,D"), ("kv_norm", "D"), ("w_kv", "D,KV"),
          ("w_f", "D,H"), ("b_f", "H"), ("b_attn_norm", "D"), ("b_w_q", "D,D"), ("b_w_o", "D,D"),
          ("b_ffn_norm", "D"), ("b_w_up", "D,2F"), ("b_w_conv", "3,F"), ("b_b_conv", "F"), ("b_w_down", "F,D"),
          ("final_norm", "D")]


def param_shape(cfg, spec):
    m = {"D": cfg.D, "3D": 3 * cfg.D, "2F": 2 * cfg.F, "F": cfg.F, "KV": 2 * cfg.KVH * 128, "H": cfg.H, "3": 3}
    return [m[s] for s in spec.split(",")]


def build(cfg, dbg=(), stop_after=None, dbg_att=None):
    nc = bass.Bass("TRN2", target_bir_lowering=False)
    D, S, H, KVH, G, F, KC, FC = cfg.D, cfg.S, cfg.H, cfg.KVH, cfg.G, cfg.F, cfg.KC, cfg.FC
    x = nc.dram_tensor("x", [S, D], F32, kind="ExternalInput").ap()
    prm = {}
    for name, spec in PARAMS:
        prm[name] = nc.dram_tensor(name, param_shape(cfg, spec), F32, kind="ExternalInput").ap()
    out = nc.dram_tensor("out", [S, D], F32, kind="ExternalOutput").ap()
    sbufs = {}

    def scratch(name, shape, dt):
        kind = "ExternalOutput" if name in dbg else "Internal"
        sbufs[name] = Buf(name)
        return nc.dram_tensor(name, shape, dt, kind=kind).ap()
    Hs = [scratch("H%d" % i, [D, S], F32) for i in range(5)]
    Hb = [sbufs["H%d" % i] for i in range(5)]
    QKVT = scratch("QKVT", [3 * D, S], BF16)
    OTA = scratch("OTA", [D, S], BF16)
    ACTA = scratch("ACTA", [S // 512, 128, FC * 512], BF16)
    KVT = scratch("KVT", [2 * KVH * 128, S], BF16)
    NEGC = scratch("NEGC", [H, S], F32)
    QT = scratch("QT", [D, S], BF16)
    OTB = scratch("OTB", [D, S], BF16)
    ACTB = scratch("ACTB", [S // 512, 128, FC * 512], BF16)
    KHh = (FC + 1) // 2
    WCA = scratch("WCA", [2 * (D // 256), 128, KHh * 256], BF16)
    WCB = scratch("WCB", [2 * (D // 256), 128, KHh * 256], BF16)

    class Stop(Exception):
        pass

    def chk(tag):
        if stop_after == tag:
            P.dead = True

    with ExitStack() as top:
        P = Prog(nc, top)
        env = Env()
        env.dbg_att = dbg_att
        env.rstd = P.sb("rstd", [128, S], F32)
        env.rstd_b = P.buf("rstd")
        env.acc = P.sb("acc", [128, S], F32)
        env.acc_b = P.buf("acc")
        try:
            make_consts(P, env)
            with P.phase("init"):
                phase_in(P, env, cfg, x, Hs[0], Hb[0])
            chk("in")

            def units():
                pus = [P.ps("pu%d" % i, [128, 1024], F32) for i in range(4)]
                pubs = [P.buf("pu%d" % i) for i in range(4)]
                return pus, pubs

            def normed(Hi, gname, stats=True):
                with P.phase("norm"):
                    pus, pubs = units()
                    if stats:
                        norm_stats_acc(P, env, cfg, D, pus[0:2], pubs[0:2])
                    norm_apply(P, env, cfg, Hs[Hi], Hb[Hi], prm[gname], KC, xn, xnb, pus[2], pubs[2])

            def ffn(Hi, pre, ACT, ACTb):
                WC = WCA if pre == "a" else WCB
                WCb = sbufs["WCA" if pre == "a" else "WCB"]
                normed(Hi, pre + "_ffn_norm")
                with P.phase("up"):
                    pus, pubs = units()
                    up_proj(P, env, cfg, xn, xnb, prm[pre + "_w_up"], prm[pre + "_w_conv"], prm[pre + "_b_conv"],
                            ACT, ACTb, pus, pubs, conv_jobs=wd_convert_jobs(cfg, prm[pre + "_w_down"], WC), WCb=WCb)
                chk(pre + "_up")
                with P.phase("down"):
                    pus, pubs = units()
                    down_proj(P, env, cfg, xflat, xnb, ACT, ACTb, prm[pre + "_w_down"], Hs[Hi], Hb[Hi],
                              Hs[Hi + 1], Hb[Hi + 1], pus, pubs, WC=WC, WCb=WCb)
                chk(pre + "_down")

            def wo(OT, OTb, wname, Hi):
                with P.phase("wo"):
                    pus, pubs = units()
                    load_at(P, xn, xnb, OT, KC, [OTb])
                    pre_, epi = resid_epilogue(P, env, cfg, Hs[Hi], Hb[Hi], Hs[Hi + 1], Hb[Hi + 1], "wo")
                    gemm_plain(P, cfg, xn, xnb, KC, prm[wname], D, epi, pus, pubs, pre=pre_)

            with P.scope():
                xflat = P.sb("xn", [128, max(KC * S, FC * 512)], BF16)
                xn = xflat[:, 0:KC * S].rearrange("p (kc t) -> p kc t", t=S)
                xnb = P.buf("xn")
                normed(0, "a_attn_norm")
                with P.phase("qkv"):
                    pus, pubs = units()
                    epi = store_epilogue(P, cfg, QKVT, sbufs["QKVT"], "qkv")
                    gemm_plain(P, cfg, xn, xnb, KC, prm["a_w_qkv"], 3 * D, epi, pus, pubs)
            chk("qkv")
            with P.phase("attA"):
                attention(P, env, cfg, "sb", QKVT[0:D, :], QKVT[D:2 * D, :], QKVT[2 * D:3 * D, :], [sbufs["QKVT"]],
                          H, 1, OTA, sbufs["OTA"])
            chk("attA")
            with P.scope():
                xflat = P.sb("xn", [128, max(KC * S, FC * 512)], BF16)
                xn = xflat[:, 0:KC * S].rearrange("p (kc t) -> p kc t", t=S)
                xnb = P.buf("xn")
                wo(OTA, sbufs["OTA"], "a_w_o", 0)
                chk("woA")
                ffn(1, "a", ACTA, sbufs["ACTA"])
            with P.scope():
                xflat = P.sb("xn", [128, max(KC * S, FC * 512)], BF16)
                xn = xflat[:, 0:KC * S].rearrange("p (kc t) -> p kc t", t=S)
                xnb = P.buf("xn")
                normed(2, "kv_norm")
                with P.phase("kv"):
                    pus, pubs = units()
                    epi = store_epilogue(P, cfg, KVT, sbufs["KVT"], "kv")
                    gemm_plain(P, cfg, xn, xnb, KC, prm["w_kv"], 2 * KVH * 128, epi, pus, pubs)
                    forget_gates(P, env, cfg, xn, xnb, prm["w_f"], prm["b_f"], NEGC, sbufs["NEGC"], pus[0], pubs[0])
                chk("kv")
                normed(2, "b_attn_norm", stats=False)
                with P.phase("q"):
                    pus, pubs = units()
                    epi = store_epilogue(P, cfg, QT, sbufs["QT"], "q")
                    gemm_plain(P, cfg, xn, xnb, KC, prm["b_w_q"], D, epi, pus, pubs)
            chk("q")
            with P.phase("attB"):
                attention(P, env, cfg, "fox", QT, KVT[0:KVH * 128, :], KVT[KVH * 128:2 * KVH * 128, :],
                          [sbufs["QT"], sbufs["KVT"]], KVH, G, OTB, sbufs["OTB"], NEGC, sbufs["NEGC"])
            chk("attB")
            with P.scope():
                xflat = P.sb("xn", [128, max(KC * S, FC * 512)], BF16)
                xn = xflat[:, 0:KC * S].rearrange("p (kc t) -> p kc t", t=S)
                xnb = P.buf("xn")
                wo(OTB, sbufs["OTB"], "b_w_o", 2)
                chk("woB")
                ffn(3, "b", ACTB, sbufs["ACTB"])
            with P.phase("final"):
                phase_out(P, env, cfg, Hs[4], Hb[4], prm["final_norm"], out)
        except Stop:
            pass
        P.dead = False
        with P.phase("end"):
            P.finish()
    nc._prog_ninst = P.n_inst
    return nc


def load_at(P, at, atb, src_ap, KC, srcbufs):
    src = src_ap.rearrange("(kc p) t -> p kc t", p=128)
    for c in range(0, KC, 8):
        ce = min(KC, c + 8)
        P.dma("sp", at[:, c:ce, :], src[:, c:ce, :], reads=srcbufs, writes=[atb], owner=atb,
              partial=c > 0, group_cont=c > 0)


_CFG = Cfg()
_NC_CACHE = {}


def kernel(**inputs):
    cfg = _CFG
    B = inputs["x"].shape[0]
    if "nc" not in _NC_CACHE:
        _NC_CACHE["nc"] = build(cfg)
    nc = _NC_CACHE["nc"]
    shared = {}
    for name, spec in PARAMS:
        shared[name] = np.ascontiguousarray(np.asarray(inputs[name], dtype=np.float32).reshape(param_shape(cfg, spec)))
    x = np.asarray(inputs["x"], dtype=np.float32)
    in_maps = []
    for b in range(B):
        m = dict(shared)
        m["x"] = np.ascontiguousarray(x[b])
        in_maps.append(m)
    res = run_bass_kernel_spmd(nc, in_maps, core_ids=list(range(B)))
    return np.stack([np.asarray(r["out"]) for r in res.results], axis=0).astype(np.float32)
```

```python
from contextlib import ExitStack, contextmanager

import numpy as np
import concourse.bass as bass
import concourse.mybir as mybir
from concourse.bass_utils import run_bass_kernel_spmd

F32 = mybir.dt.float32
BF16 = mybir.dt.bfloat16
AF = mybir.ActivationFunctionType
ALU = mybir.AluOpType
AX = mybir.AxisListType

ENGS = ("pe", "act", "dve", "pool", "sp")
SB_MULT_ENG = "dve"
SB_COPY_ENG = "act"


class Cfg:
    def __init__(self, D=4096, S=2048, eps=1e-6):
        self.D, self.S, self.eps = D, S, eps
        self.H = D // 128
        self.KVH = max(1, self.H // 4)
        self.G = self.H // self.KVH
        self.F = ((8 * D // 3 + 255) // 256) * 256
        self.KC = D // 128
        self.FC = self.F // 128
        self.NTB = S // 128


class Sem:
    def __init__(self, h, name):
        self.h, self.name, self.count = h, name, 0


class Buf:
    def __init__(self, name):
        self.name = name
        self.writers = {}
        self.readers = {}
        self.dsem = None


def _merge(dst, src):
    for k, v in src.items():
        if dst.get(k, (None, 0))[1] < v[1]:
            dst[k] = v


class Prog:
    def __init__(self, nc, stack, n_dma_sems=48):
        self.nc = nc
        self.esem = {e: Sem(stack.enter_context(nc.semaphore("es_" + e)), e) for e in ENGS}
        self.dpool = [Sem(stack.enter_context(nc.semaphore("ds%d" % i)), "ds%d" % i)
                      for i in range(n_dma_sems)]
        self.waited = {e: {} for e in ENGS}
        self.lists = {e: [] for e in ENGS}
        self.stacks = [stack]
        self.bufstack = [[]]
        self.all_dsems = []
        self.n_inst = 0
        self.uid = 0
        self.dead = False

    def sb(self, name, shape, dtype):
        self.uid += 1
        return self.stacks[-1].enter_context(self.nc.sbuf_tensor("%s_%d" % (name, self.uid), list(shape), dtype))

    def ps(self, name, shape, dtype):
        self.uid += 1
        return self.stacks[-1].enter_context(self.nc.psum_tensor("%s_%d" % (name, self.uid), list(shape), dtype))

    def buf(self, name):
        b = Buf(name)
        self.bufstack[-1].append(b)
        return b

    def _deps(self, reads, writes):
        deps = {}
        for b in reads:
            _merge(deps, b.writers)
        for b in writes:
            _merge(deps, b.writers)
            _merge(deps, b.readers)
        return deps

    def _emit_waits(self, eng, deps):
        w = self.waited[eng]
        for k, (sem, v) in deps.items():
            if w.get(k, 0) < v:
                w[k] = v
                self.lists[eng].append(lambda e, h=sem.h, v=v: e.wait_ge(h, v))

    def _record(self, ev, reads, writes, partial):
        key = ev[0].name
        for b in writes:
            if partial:
                _merge(b.writers, {key: ev})
            else:
                b.writers = {key: ev}
                b.readers = {}
        for b in reads:
            if b not in writes:
                _merge(b.readers, {key: ev})

    def op(self, eng, fn, reads=(), writes=(), partial=False):
        if self.dead:
            return
        reads, writes = list(reads), list(writes)
        self._emit_waits(eng, self._deps(reads, writes))
        sem = self.esem[eng]
        sem.count += 1
        self.lists[eng].append(lambda e, fn=fn, h=sem.h: fn(e).then_inc(h, 1))
        self._record((sem, sem.count), reads, writes, partial)
        self.n_inst += 1

    def dma(self, q, out, in_, reads=(), writes=(), owner=None, partial=False, group_cont=False, **kw):
        if self.dead:
            return
        reads, writes = list(reads), list(writes)
        deps = self._deps(reads, writes)
        if owner.dsem is None:
            owner.dsem = self.dpool.pop()
            self.all_dsems.append(owner.dsem)
        sem = owner.dsem
        if not group_cont and sem.count > 0:
            _merge(deps, {sem.name: (sem, sem.count)})
        self._emit_waits(q, deps)
        sem.count += 16
        self.lists[q].append(
            lambda e, o=out, i=in_, h=sem.h, kw=kw: e.dma_start(out=o, in_=i, **kw).then_inc(h, 16))
        self._record((sem, sem.count), reads, writes, partial)
        self.n_inst += 1

    @contextmanager
    def scope(self, flush=False):
        st = ExitStack()
        bufs = []
        self.stacks.append(st)
        self.bufstack.append(bufs)
        try:
            with st:
                yield
                if flush:
                    for b in bufs:
                        sem = b.dsem
                        if sem is not None and sem.count > 0 and self.waited["sp"].get(sem.name, 0) < sem.count:
                            self.waited["sp"][sem.name] = sem.count
                            self.lists["sp"].append(lambda e, h=sem.h, v=sem.count: e.wait_ge(h, v))
                    self.flush()
        finally:
            self.stacks.pop()
            self.bufstack.pop()
            for b in bufs:
                if b.dsem is not None:
                    self.dpool.insert(0, b.dsem)
                    b.dsem = None

    def phase(self, name):
        return self.scope(flush=True)

    def flush(self):
        nc = self.nc
        lists = self.lists
        with nc.Block() as block:
            @block.tensor
            def _(e):
                for f in lists["pe"]:
                    f(e)

            @block.scalar
            def _(e):
                for f in lists["act"]:
                    f(e)

            @block.vector
            def _(e):
                for f in lists["dve"]:
                    f(e)

            @block.gpsimd
            def _(e):
                for f in lists["pool"]:
                    f(e)

            @block.sync
            def _(e):
                for f in lists["sp"]:
                    f(e)
        self.lists = {e: [] for e in ENGS}

    def dump(self, name, ap, shape, dtype, bufs):
        d = self.nc.dram_tensor(name, list(shape), dtype, kind="ExternalOutput").ap()
        b = self.buf("dump_" + name)
        self.dma("sp", d, ap, reads=bufs, owner=b)

    def finish(self):
        for sem in self.all_dsems:
            if sem.count > 0 and self.waited["sp"].get(sem.name, 0) < sem.count:
                self.waited["sp"][sem.name] = sem.count
                self.lists["sp"].append(lambda e, h=sem.h, v=sem.count: e.wait_ge(h, v))


class WStream:
    def __init__(self, P, name, KT, NC, nslots, extra=()):
        self.P, self.KT, self.NC = P, KT, NC
        self.tiles = [P.sb("%s_w%d" % (name, i), [128, KT, NC], BF16) for i in range(nslots)]
        self.bufs = [P.buf("%s_wb%d" % (name, i)) for i in range(nslots)]
        for i, t in enumerate(extra):
            self.tiles.append(t)
            self.bufs.append(P.buf("%s_wx%d" % (name, i)))
        self.n = 0

    def load(self, w2d, k0c, kt, col_ranges):
        P = self.P
        s = self.n % len(self.tiles)
        self.n += 1
        tile, buf = self.tiles[s], self.bufs[s]
        first = True
        off = 0
        for (c0, ncols) in col_ranges:
            src = w2d[k0c * 128:(k0c + kt) * 128, c0:c0 + ncols].rearrange("(kc p) n -> p kc n", p=128)
            step = max(1, min(16, 2048 // max(1, ncols // 64)))
            step = 16
            for a in range(0, kt, step):
                b = min(kt, a + step)
                P.dma("pool", tile[:, a:b, off:off + ncols], src[:, a:b, :], writes=[buf], owner=buf,
                      partial=not first, group_cont=not first)
                first = False
            off += ncols
        return tile, buf


def ws_load_bf16(ws, src3d, kt, srcbufs, q="act"):
    P = ws.P
    sl = ws.n % len(ws.tiles)
    ws.n += 1
    tile, buf = ws.tiles[sl], ws.bufs[sl]
    P.dma(q, tile[:, 0:kt, :], src3d, reads=srcbufs, writes=[buf], owner=buf)
    return tile, buf


def wd_convert_jobs(cfg, Wd, WC):
    FC, D = cfg.FC, cfg.D
    KH = (FC + 1) // 2
    jobs = []
    for cb in range(D // 256):
        for ki, (k0, kt) in enumerate(((0, KH), (KH, FC - KH))):
            tid = cb * 2 + ki
            src = Wd[k0 * 128:(k0 + kt) * 128, cb * 256:(cb + 1) * 256].rearrange("(kc p) n -> p kc n", p=128)
            dst = WC[tid].rearrange("p (k n) -> p k n", n=256)
            for a in range(0, kt, 16):
                b = min(kt, a + 16)
                jobs.append((dst[:, a:b, :], src[:, a:b, :]))
    return jobs


def mm_group(P, ps_ap, ps_bufs, wtile, wbuf, wc0, wn, kcs, at, atbufs, at_kc0, t0, tn, start, stop):
    def fn(e):
        inst = None
        n = len(kcs)
        for i, kc in enumerate(kcs):
            inst = e.matmul(ps_ap, wtile[:, kc, wc0:wc0 + wn], at[:, at_kc0 + kc, t0:t0 + tn],
                            start=(start and i == 0), stop=(stop and i == n - 1))
        return inst
    P.op("pe", fn, reads=[wbuf] + list(atbufs), writes=ps_bufs, partial=not start)


class Env:
    pass


def make_consts(P, env):
    env.identf = P.sb("identf", [128, 128], F32)
    env.identb = P.sb("identb", [128, 128], BF16)
    env.identf_b = P.buf("identf")
    env.identb_b = P.buf("identb")
    for t, b in ((env.identf, env.identf_b), (env.identb, env.identb_b)):
        P.op("pool", lambda e, t=t: e.memset(t[:, :], 1.0), writes=[b])
        P.op("pool", lambda e, t=t: e.affine_select(
            out=t[:, :], in_=t[:, :], pattern=[[-1, 128]], compare_op=ALU.is_equal, fill=0.0,
            base=0, channel_multiplier=1), reads=[b], writes=[b])


def load_cols(P, env, vec_ap, ncols, name, pu, pub):
    tmp = P.sb(name + "_r", [ncols, 128], F32)
    tmpb = P.buf(name + "_r")
    out = P.sb(name, [128, ncols], F32)
    outb = P.buf(name)
    P.dma("sp", tmp[:, :], vec_ap.rearrange("(c p) -> c p", p=128), writes=[tmpb], owner=tmpb)
    P.op("pe", lambda e: e.transpose(out=pu[:, 0:ncols], in_=tmp[:, :], identity=env.identf[0:ncols, 0:ncols]),
         reads=[tmpb, env.identf_b], writes=[pub])
    P.op("dve", lambda e: e.tensor_copy(out=out[:, :], in_=pu[:, 0:ncols]), reads=[pub], writes=[outb])
    return out, outb


def acc_squares(P, env, h_ap, hbuf, cols, first, tmp, tmpb):
    if first:
        P.op("pool", lambda e: e.tensor_tensor(out=env.acc[:, cols], in0=h_ap, in1=h_ap, op=ALU.mult),
             reads=[hbuf], writes=[env.acc_b], partial=True)
    else:
        n = cols.stop - cols.start
        P.op("pool", lambda e: e.tensor_tensor(out=tmp[:, 0:n], in0=h_ap, in1=h_ap, op=ALU.mult),
             reads=[hbuf], writes=[tmpb])
        P.op("pool", lambda e: e.tensor_tensor(out=env.acc[:, cols], in0=env.acc[:, cols], in1=tmp[:, 0:n], op=ALU.add),
             reads=[tmpb, env.acc_b], writes=[env.acc_b], partial=True)


def norm_stats_acc(P, env, cfg, D, pus, pubs):
    S = cfg.S
    onesf = P.sb("ns_ones", [128, 128], F32)
    onesb = P.buf("ns_ones")
    P.op("pool", lambda e: e.memset(onesf[:, :], 1.0), writes=[onesb])
    ntg = S // 512
    nu = (ntg + 1) // 2

    def fn(e):
        inst = None
        for tg in range(ntg):
            u = pus[tg // 2]
            inst = e.matmul(u[:, (tg % 2) * 512:(tg % 2) * 512 + 512], onesf[:, :],
                            env.acc[:, tg * 512:(tg + 1) * 512], start=True, stop=True)
        return inst
    P.op("pe", fn, reads=[env.acc_b, onesb], writes=pubs[:nu])
    for j in range(nu):
        w = min(1024, S - j * 1024)
        sl = slice(j * 1024, j * 1024 + w)
        P.op("dve", lambda e, j=j, w=w, sl=sl: e.tensor_scalar(
            out=env.rstd[:, sl], in0=pus[j][:, 0:w], scalar1=1.0 / D, scalar2=cfg.eps,
            op0=ALU.mult, op1=ALU.add), reads=[pubs[j]], writes=[env.rstd_b], partial=j > 0)
    P.op("act", lambda e: e.activation(out=env.rstd[:, :], in_=env.rstd[:, :], func=AF.Sqrt),
         reads=[env.rstd_b], writes=[env.rstd_b])
    P.op("dve", lambda e: e.reciprocal(out=env.rstd[:, :], in_=env.rstd[:, :]),
         reads=[env.rstd_b], writes=[env.rstd_b])


def norm_stats(P, env, cfg, Hap, Hb, KC, pus, pubs):
    S = cfg.S
    D = KC * 128
    hs = [P.sb("ns_h%d" % i, [128, S], F32) for i in range(2)]
    hsb = [P.buf("ns_h%d" % i) for i in range(2)]
    sq = [P.sb("ns_q%d" % i, [128, S], F32) for i in range(2)]
    sqb = [P.buf("ns_q%d" % i) for i in range(2)]
    onesf = P.sb("ns_ones", [128, 128], F32)
    onesb = P.buf("ns_ones")
    P.op("pool", lambda e: e.memset(onesf[:, :], 1.0), writes=[onesb])
    ntg = S // 512
    for kc in range(KC):
        i = kc % 2
        P.dma("sp", hs[i][:, :], Hap[kc * 128:(kc + 1) * 128, :], reads=[Hb], writes=[hsb[i]], owner=hsb[i])
        P.op("act", lambda e, i=i: e.activation(out=sq[i][:, :], in_=hs[i][:, :], func=AF.Square),
             reads=[hsb[i]], writes=[sqb[i]])

        def fn(e, i=i, kc=kc):
            inst = None
            for tg in range(ntg):
                u = pus[tg // 2]
                inst = e.matmul(u[:, (tg % 2) * 512:(tg % 2) * 512 + 512], onesf[:, :],
                                sq[i][:, tg * 512:(tg + 1) * 512], start=(kc == 0), stop=(kc == KC - 1))
            return inst
        nu = (ntg + 1) // 2
        P.op("pe", fn, reads=[sqb[i], onesb], writes=pubs[:nu], partial=kc > 0)
    for j in range((ntg + 1) // 2):
        w = min(1024, S - j * 1024)
        sl = slice(j * 1024, j * 1024 + w)
        P.op("dve", lambda e, j=j, w=w, sl=sl: e.tensor_scalar(
            out=env.rstd[:, sl], in0=pus[j][:, 0:w], scalar1=1.0 / D, scalar2=cfg.eps,
            op0=ALU.mult, op1=ALU.add), reads=[pubs[j]], writes=[env.rstd_b], partial=j > 0)
    P.op("act", lambda e: e.activation(out=env.rstd[:, :], in_=env.rstd[:, :], func=AF.Sqrt),
         reads=[env.rstd_b], writes=[env.rstd_b])
    P.op("dve", lambda e: e.reciprocal(out=env.rstd[:, :], in_=env.rstd[:, :]),
         reads=[env.rstd_b], writes=[env.rstd_b])


def norm_apply(P, env, cfg, Hap, Hb, g_ap, KC, xn, xnb, pu, pub):
    S = cfg.S
    gcol, gcolb = load_cols(P, env, g_ap, KC, "na_g", pu, pub)
    hs = [P.sb("na_h%d" % i, [128, S], F32) for i in range(3)]
    hsb = [P.buf("na_h%d" % i) for i in range(3)]
    for kc in range(KC):
        i = kc % 3
        P.dma("sp", hs[i][:, :], Hap[kc * 128:(kc + 1) * 128, :], reads=[Hb], writes=[hsb[i]], owner=hsb[i])
        P.op("dve", lambda e, i=i, kc=kc: e.scalar_tensor_tensor(
            out=xn[:, kc, :], in0=hs[i][:, :], scalar=gcol[:, kc:kc + 1], in1=env.rstd[:, :],
            op0=ALU.mult, op1=ALU.mult), reads=[hsb[i], gcolb, env.rstd_b], writes=[xnb], partial=kc > 0)


def gemm_plain(P, cfg, at, atb, KC, W2d, N, epilogue, pus, pubs, pre=None):
    S = cfg.S
    TU = min(1024, S)
    NCW = min(256, N)
    ws = WStream(P, "gp", KC, NCW, 2)
    ui = 0
    for c0 in range(0, N, NCW):
        wt, wb = ws.load(W2d, 0, KC, [(c0, NCW)])
        for sub in range(NCW // 128):
            n0 = c0 + sub * 128
            if pre is not None:
                pre(n0)
            for th in range(S // TU):
                u, ub = pus[ui % len(pus)], pubs[ui % len(pus)]
                ui += 1
                for tg in range(TU // 512):
                    mm_group(P, u[:, tg * 512:(tg + 1) * 512], [ub], wt, wb, sub * 128, 128,
                             list(range(KC)), at, [atb], 0, th * TU + tg * 512, 512, True, True)
                epilogue(n0, th, TU, u, ub, ui)


def store_epilogue(P, cfg, out_ap, outb, name):
    S = cfg.S
    stg = [P.sb("%s_st%d" % (name, i), [128, S], BF16) for i in range(2)]
    stgb = [P.buf("%s_st%d" % (name, i)) for i in range(2)]
    state = {"n": 0}

    def epi(n0, th, TU, u, ub, ui):
        i = state["n"] % 2
        st, sbf = stg[i], stgb[i]
        if ui % 2:
            P.op("act", lambda e: e.activation(out=st[:, th * TU:(th + 1) * TU], in_=u[:, 0:TU], func=AF.Copy),
                 reads=[ub], writes=[sbf], partial=True)
        else:
            P.op("dve", lambda e: e.tensor_copy(out=st[:, th * TU:(th + 1) * TU], in_=u[:, 0:TU]),
                 reads=[ub], writes=[sbf], partial=True)
        if (th + 1) * TU == S:
            P.dma("sp", out_ap[n0:n0 + 128, :], st[:, :], reads=[sbf], writes=[outb], owner=sbf, partial=True)
            state["n"] += 1
    return epi


def resid_epilogue(P, env, cfg, Hin, Hinb, Hout, Houtb, name):
    S = cfg.S
    hs = [P.sb("%s_h%d" % (name, i), [128, S], F32) for i in range(2)]
    hsb = [P.buf("%s_h%d" % (name, i)) for i in range(2)]
    sqt = P.sb("%s_sq" % name, [128, min(1024, S)], F32)
    sqtb = P.buf("%s_sq" % name)
    state = {"n": 0}

    def pre(n0):
        i = state["n"] % 2
        P.dma("sp", hs[i][:, :], Hin[n0:n0 + 128, :], reads=[Hinb], writes=[hsb[i]], owner=hsb[i])

    def epi(n0, th, TU, u, ub, ui):
        i = state["n"] % 2
        h, hb = hs[i], hsb[i]
        sl = slice(th * TU, (th + 1) * TU)
        P.op("dve", lambda e: e.tensor_tensor(out=h[:, sl], in0=u[:, 0:TU], in1=h[:, sl], op=ALU.add),
             reads=[ub, hb], writes=[hb], partial=True)
        if state["n"] == 0:
            P.op("act", lambda e: e.activation(out=env.acc[:, sl], in_=h[:, sl], func=AF.Square),
                 reads=[hb], writes=[env.acc_b], partial=True)
        else:
            P.op("act", lambda e: e.activation(out=sqt[:, 0:TU], in_=h[:, sl], func=AF.Square),
                 reads=[hb], writes=[sqtb])
            P.op("dve", lambda e: e.tensor_tensor(out=env.acc[:, sl], in0=env.acc[:, sl], in1=sqt[:, 0:TU], op=ALU.add),
                 reads=[sqtb, env.acc_b], writes=[env.acc_b], partial=True)
        if (th + 1) * TU == S:
            P.dma("sp", Hout[n0:n0 + 128, :], h[:, :], reads=[hb], writes=[Houtb], owner=hb, partial=True)
            state["n"] += 1
    return pre, epi


def load_at(P, at, atb, src_ap, KC, q="sp"):
    src = src_ap.rearrange("(kc p) t -> p kc t", p=128)
    for c in range(0, KC, 8):
        ce = min(KC, c + 8)
        P.dma(q, at[:, c:ce, :], src[:, c:ce, :], writes=[atb], owner=atb, partial=c > 0, group_cont=c > 0)


def up_proj(P, env, cfg, xn, xnb, Wup, wconv_ap, bconv_ap, ACTT, ACTTb, pus, pubs, conv_jobs=(), WCb=None):
    S, F, FC, KC = cfg.S, cfg.F, cfg.FC, cfg.KC
    TU = min(1024, S)
    wc = []
    for j in range(3):
        wc.append(load_cols(P, env, wconv_ap[j, :], FC, "wc%d" % j, pus[0], pubs[0]))
    bc, bcb = load_cols(P, env, bconv_ap, FC, "bc", pus[0], pubs[0])
    G = P.sb("up_G", [128, S + 2], F32)
    Gb = P.buf("up_G")
    P.op("pool", lambda e: e.memset(G[:, 0:2], 0.0), writes=[Gb])
    tmp = [P.sb("up_t%d" % i, [128, TU], F32) for i in range(2)]
    tmpb = [P.buf("up_t%d" % i) for i in range(2)]
    stg = [P.sb("up_s%d" % i, [128, S], BF16) for i in range(2)]
    stgb = [P.buf("up_s%d" % i) for i in range(2)]
    ws = WStream(P, "up", KC, 256, 2)
    ui = 0
    it = 0
    conv_jobs = list(conv_jobs)
    per = (len(conv_jobs) + FC - 1) // FC if conv_jobs else 0
    cvb = P.buf("up_cv")
    ncv = 0
    for f in range(FC):
        wt, wb = ws.load(Wup, 0, KC, [(f * 128, 128), (F + f * 128, 128)])
        for _ in range(per):
            if ncv < len(conv_jobs):
                dst, src = conv_jobs[ncv]
                P.dma("pool", dst, src, writes=[WCb], owner=cvb, partial=True, group_cont=ncv > 0)
                ncv += 1
        st, sbf = stg[f % 2], stgb[f % 2]
        for th in range(S // TU):
            ug, ugb = pus[ui % 4], pubs[ui % 4]
            uv, uvb = pus[(ui + 1) % 4], pubs[(ui + 1) % 4]
            ui += 2
            for tg in range(TU // 512):
                mm_group(P, ug[:, tg * 512:(tg + 1) * 512], [ugb], wt, wb, 0, 128, list(range(KC)),
                         xn, [xnb], 0, th * TU + tg * 512, 512, True, True)
            for tg in range(TU // 512):
                mm_group(P, uv[:, tg * 512:(tg + 1) * 512], [uvb], wt, wb, 128, 128, list(range(KC)),
                         xn, [xnb], 0, th * TU + tg * 512, 512, True, True)
            t0 = th * TU
            tm, tmb = tmp[it % 2], tmpb[it % 2]
            it += 1
            P.op("act", lambda e, ug=ug, t0=t0: e.activation(out=G[:, 2 + t0:2 + t0 + TU], in_=ug[:, 0:TU], func=AF.Copy),
                 reads=[ugb], writes=[Gb], partial=True)
            P.op("dve", lambda e, tm=tm, t0=t0, f=f: e.tensor_scalar(
                out=tm[:, :], in0=G[:, 2 + t0:2 + t0 + TU], scalar1=wc[2][0][:, f:f + 1], scalar2=bc[:, f:f + 1],
                op0=ALU.mult, op1=ALU.add), reads=[Gb, wc[2][1], bcb], writes=[tmb])
            P.op("dve", lambda e, tm=tm, t0=t0, f=f: e.scalar_tensor_tensor(
                out=tm[:, :], in0=G[:, 1 + t0:1 + t0 + TU], scalar=wc[1][0][:, f:f + 1], in1=tm[:, :],
                op0=ALU.mult, op1=ALU.add), reads=[Gb, wc[1][1], tmb], writes=[tmb])
            P.op("dve", lambda e, tm=tm, t0=t0, f=f: e.scalar_tensor_tensor(
                out=tm[:, :], in0=G[:, t0:t0 + TU], scalar=wc[0][0][:, f:f + 1], in1=tm[:, :],
                op0=ALU.mult, op1=ALU.add), reads=[Gb, wc[0][1], tmb], writes=[tmb])
            P.op("act", lambda e, tm=tm: e.activation(out=tm[:, :], in_=tm[:, :], func=AF.Silu),
                 reads=[tmb], writes=[tmb])
            P.op("dve", lambda e, tm=tm, st=st, uv=uv, t0=t0: e.tensor_tensor(
                out=st[:, t0:t0 + TU], in0=uv[:, 0:TU], in1=tm[:, :], op=ALU.mult),
                reads=[uvb, tmb], writes=[sbf], partial=True)
        P.dma("sp", ACTT[:, :, f * 512:(f + 1) * 512].rearrange("g p t -> p g t"),
              st[:, :].rearrange("p (g t) -> p g t", t=512), reads=[sbf], writes=[ACTTb], owner=sbf, partial=True)


def down_proj(P, env, cfg, atflat, atb, ACTT, ACTTb, Wd, Hin, Hinb, Hout, Houtb, pus, pubs, WC=None, WCb=None):
    S, D, FC = cfg.S, cfg.D, cfg.FC
    TG = 512
    KH = (FC + 1) // 2
    khs = [(0, KH), (KH, FC - KH)]
    at = atflat[:, 0:FC * TG].rearrange("p (kc t) -> p kc t", t=TG)
    athb = [P.buf("dn_at_lo"), P.buf("dn_at_hi")]
    extra = []
    used = FC * TG
    if atflat.shape[1] - used >= KH * 256:
        extra.append(atflat[:, used:used + KH * 256].rearrange("p (k n) -> p k n", n=256))
    ws = WStream(P, "dn", KH, 256, 2, extra=extra)
    hs = [P.sb("dn_h%d" % i, [128, 2, TG], F32) for i in range(3)]
    hsb = [P.buf("dn_h%d" % i) for i in range(3)]
    sqt = P.sb("dn_sq", [128, TG], F32)
    sqtb = P.buf("dn_sq")
    ui = 0
    hi = 0
    for g in range(S // TG):
        t0 = g * TG
        for (k0, kt), hb_ in zip(khs, athb):
            P.dma("sp", atflat[:, k0 * TG:(k0 + kt) * TG], ACTT[g][:, k0 * TG:(k0 + kt) * TG], reads=[ACTTb],
                  writes=[hb_], owner=hb_)
        for c0 in range(0, D, 256):
            h, hb = hs[hi % 3], hsb[hi % 3]
            hi += 1
            P.dma("sp", h[:, :, :], Hin[c0:c0 + 256, t0:t0 + TG].rearrange("(s p) t -> p s t", p=128),
                  reads=[Hinb], writes=[hb], owner=hb)
            u, ub = pus[ui % 4], pubs[ui % 4]
            ui += 1
            for ki, (k0, kt) in enumerate(khs):
                if WC is not None:
                    tid = (c0 // 256) * 2 + ki
                    wt, wb = ws_load_bf16(ws, WC[tid].rearrange("p (k n) -> p k n", n=256)[:, 0:kt, :], kt, [WCb])
                else:
                    wt, wb = ws.load(Wd, k0, kt, [(c0, 256)])
                for sub in range(2):
                    mm_group(P, u[:, sub * 512:sub * 512 + TG], [ub], wt, wb, sub * 128, 128, list(range(kt)),
                             at, [athb[ki]], k0, 0, TG, ki == 0, ki == 1)
            P.op("dve", lambda e, h=h, u=u: e.tensor_tensor(
                out=h[:, :, :], in0=u[:, 0:1024].rearrange("p (s t) -> p s t", s=2)[:, :, 0:TG], in1=h[:, :, :],
                op=ALU.add), reads=[ub, hb], writes=[hb])
            for sub in range(2):
                acc_squares(P, env, h[:, sub, :], hb, slice(t0, t0 + TG), c0 == 0 and sub == 0, sqt, sqtb)
            P.dma("sp", Hout[c0:c0 + 256, t0:t0 + TG].rearrange("(s p) t -> p s t", p=128), h[:, :, :],
                  reads=[hb], writes=[Houtb], owner=hb, partial=True)


def forget_gates(P, env, cfg, xn, xnb, wf_ap, bf_ap, NEGC, NEGCb, pu, pub):
    S, H, KC = cfg.S, cfg.H, cfg.KC
    wf = P.sb("fg_w", [128, KC, H], BF16)
    wfb = P.buf("fg_w")
    src = wf_ap.rearrange("(kc p) n -> p kc n", p=128)
    for a in range(0, KC, 16):
        b = min(KC, a + 16)
        P.dma("pool", wf[:, a:b, :], src[:, a:b, :], writes=[wfb], owner=wfb, partial=a > 0, group_cont=a > 0)
    bf = P.sb("fg_b", [H, 1], F32)
    bfb = P.buf("fg_b")
    P.dma("sp", bf[:, :], bf_ap.rearrange("(h o) -> h o", o=1), writes=[bfb], owner=bfb)
    P.op("dve", lambda e: e.tensor_scalar(out=bf[:, :], in0=bf[:, :], scalar1=-1.0, scalar2=None, op0=ALU.mult),
         reads=[bfb], writes=[bfb])
    l = P.sb("fg_l", [H, S], F32)
    lb = P.buf("fg_l")
    ng = P.sb("fg_n", [H, S], F32)
    ngb = P.buf("fg_n")
    TU = min(1024, S)
    for th in range(S // TU):
        for tg in range(TU // 512):
            def fn(e, th=th, tg=tg):
                inst = None
                for kc in range(KC):
                    inst = e.matmul(pu[0:H, tg * 512:(tg + 1) * 512], wf[:, kc, :],
                                    xn[:, kc, th * TU + tg * 512:th * TU + (tg + 1) * 512],
                                    start=(kc == 0), stop=(kc == KC - 1))
                return inst
            P.op("pe", fn, reads=[wfb, xnb], writes=[pub], partial=tg > 0)
        sl = slice(th * TU, (th + 1) * TU)
        P.op("act", lambda e, sl=sl: e.activation(out=l[:, sl], in_=pu[0:H, 0:TU], func=AF.Exp,
                                                   bias=bf[:, 0:1], scale=-1.0),
             reads=[pub, bfb], writes=[lb], partial=True)
    P.op("act", lambda e: e.activation(out=l[:, :], in_=l[:, :], func=AF.Ln, bias=1.0, scale=1.0),
         reads=[lb], writes=[lb])
    P.op("dve", lambda e: e.tensor_tensor_scan(out=ng[:, :], data0=l[:, :], data1=l[:, :], initial=0.0,
                                                op0=ALU.add, op1=ALU.max),
         reads=[lb], writes=[ngb])
    P.dma("sp", NEGC[:, :], ng[:, :], reads=[ngb], writes=[NEGCb], owner=ngb)


def attention(P, env, cfg, mode, q_ap, k_ap, v_ap, srcbufs, nkv, group, OT, OTb, NEGC=None, NEGCb=None):
    S, NTB = cfg.S, cfg.NTB
    scale = 128.0 ** -0.5
    fox = mode == "fox"
    zps = P.ps("at_z", [128, S], F32)
    zb = P.buf("at_z")
    tr = P.ps("at_tr", [128, S], BF16)
    trb = P.buf("at_tr")
    ops_ = P.ps("at_o", [128, 512], F32)
    opb = P.buf("at_o")
    tr2 = P.ps("at_tr2", [128, 1024], BF16)
    tr2b = P.buf("at_tr2")

    def mk(name, shape, dt, n=2):
        return ([P.sb("%s%d" % (name, i), shape, dt) for i in range(n)],
                [P.buf("%s%d" % (name, i)) for i in range(n)])
    kt, ktb = mk("at_k", [128, S], BF16)
    vt, vtb = mk("at_v", [128, S], BF16)
    qt, qtb = mk("at_q", [128, S], BF16)
    vsb, vsbb = mk("at_vs", [128, NTB, 128], BF16)
    NS = 2 if fox else 3
    E, Eb = mk("at_E", [128, S], F32, NS)
    W, Wb = mk("at_W", [128, S], BF16, NS)
    WT, WTb = mk("at_WT", [128, S], BF16, NS)
    oT, oTb = mk("at_oT", [128, S], BF16)
    sm, smb = mk("at_sm", [128, 4], F32, NS)
    sr, srb = mk("at_sr", [128, 4], F32, NS)
    maskadd = P.sb("at_mask", [128, 128], BF16)
    maskb = P.buf("at_mask")
    P.op("pool", lambda e: e.memset(maskadd[:, :], 0.0), writes=[maskb])
    P.op("pool", lambda e: e.affine_select(
        out=maskadd[:, :], in_=maskadd[:, :], pattern=[[-1, 128]],
        compare_op=(ALU.is_ge if fox else ALU.is_gt), fill=-1.0e30,
        base=0, channel_multiplier=1), reads=[maskb], writes=[maskb])
    if fox:
        ncb, ncbb = mk("at_nc", [128, S], F32)
        osb, osbb = mk("at_os", [128, 128], BF16)
    else:
        L, Lb = mk("at_L", [128, S], F32, NS)
        C, Cb = mk("at_C", [128, S + 1], F32, NS)
        ones = P.sb("at_ones", [128, S], F32)
        onesb = P.buf("at_ones")
        P.op("pool", lambda e: e.memset(ones[:, :], 1.0), writes=[onesb])
        for i in range(NS):
            P.op("pool", lambda e, i=i: e.memset(C[i][:, 0:1], 0.0), writes=[Cb[i]])

    def kv_setup(kh):
        ki = kh % 2
        P.dma("sp", kt[ki][:, :], k_ap[kh * 128:(kh + 1) * 128, :], reads=srcbufs, writes=[ktb[ki]], owner=ktb[ki])
        P.dma("sp", vt[ki][:, :], v_ap[kh * 128:(kh + 1) * 128, :], reads=srcbufs, writes=[vtb[ki]], owner=vtb[ki])
        for a0 in range(0, NTB, 8):
            a1 = min(NTB, a0 + 8)

            def fnv(e, a0=a0, a1=a1):
                inst = None
                for a in range(a0, a1):
                    inst = e.transpose(out=tr2[:, (a - a0) * 128:(a - a0 + 1) * 128],
                                       in_=vt[ki][:, a * 128:(a + 1) * 128], identity=env.identb[:, :])
                return inst
            P.op("pe", fnv, reads=[vtb[ki], env.identb_b], writes=[tr2b])
            P.op("act", lambda e, a0=a0, a1=a1: e.activation(
                out=vsb[ki][:, a0:a1, :], in_=tr2[:, 0:(a1 - a0) * 128].rearrange("p (a d) -> p a d", d=128),
                func=AF.Copy), reads=[tr2b], writes=[vsbb[ki]], partial=a0 > 0)

    def q_setup(h, qi):
        P.dma("sp", qt[qi][:, :], q_ap[h * 128:(h + 1) * 128, :], reads=srcbufs, writes=[qtb[qi]], owner=qtb[qi])
        if fox:
            P.dma("sp", ncb[qi][:, :], NEGC[h:h + 1, :].partition_broadcast(128), reads=[NEGCb],
                  writes=[ncbb[qi]], owner=ncbb[qi])

    nheads = nkv * group

    def stA1(t):
        kh, g, h, qi, i, j = t
        if i == 0:
            if h == 0:
                q_setup(0, 0)
                if fox:
                    kv_setup(0)
            if not fox:
                kv_setup(kh)
            if h + 1 < nheads:
                q_setup(h + 1, (qi + 1) % 2)
        if fox and i == min(4, NTB - 1) and g == 0 and kh + 1 < nkv:
            kv_setup(kh + 1)
        ki = kh % 2
        kend = 128 * (i + 1)

        def fnz(e):
            inst = None
            for c0 in range(0, kend, 512):
                cn = min(512, kend - c0)
                lastc = c0 + cn == kend
                inst = e.matmul(zps[:, c0:c0 + cn], qt[qi][:, i * 128:(i + 1) * 128], kt[ki][:, c0:c0 + cn],
                                start=True, stop=not lastc)
            inst = e.matmul(zps[:, kend - 128:kend], env.identb[:, :], maskadd[:, :], start=False, stop=True)
            return inst
        P.op("pe", fnz, reads=[qtb[qi], ktb[ki], env.identb_b, maskb], writes=[zb])

    def stA2(t, part=None):
        kh, g, h, qi, i, j = t
        kend = 128 * (i + 1)
        d0 = 128 * i
        if not fox and part == "b":
            P.op("dve", lambda e: e.tensor_tensor_scan(
                out=C[j][:, 1:kend + 1], data0=ones[:, 0:kend], data1=L[j][:, 0:kend], initial=0.0,
                op0=ALU.mult, op1=ALU.add), reads=[Lb[j], onesb], writes=[Cb[j]])
            P.op("dve", lambda e: e.tensor_scalar(
                out=sm[j][:, 0:1], in0=C[j][:, kend:kend + 1], scalar1=-1.0, scalar2=None, op0=ALU.mult),
                reads=[Cb[j]], writes=[smb[j]])
            return
        if not fox:
            P.op("act", lambda e: e.activation(out=E[j][:, 0:kend], in_=zps[:, 0:kend], func=AF.Exp, scale=scale),
                 reads=[zb], writes=[Eb[j]])
            P.op("act", lambda e: e.activation(out=L[j][:, 0:kend], in_=E[j][:, 0:kend], func=AF.Ln, bias=1.0, scale=1.0),
                 reads=[Eb[j]], writes=[Lb[j]])
            if part is None:
                stA2(t, "b")
        else:
            P.op("dve", lambda e: e.scalar_tensor_tensor(
                out=E[j][:, 0:kend], in0=zps[:, 0:kend], scalar=scale, in1=ncb[qi][:, 0:kend],
                op0=ALU.mult, op1=ALU.add), reads=[zb, ncbb[qi]], writes=[Eb[j]])
            P.op("dve", lambda e: e.tensor_reduce(
                out=sm[j][:, 0:1], in_=E[j][:, 0:kend], axis=AX.X, op=ALU.max, negate=True),
                reads=[Eb[j]], writes=[smb[j]])

    def stB(t):
        kh, g, h, qi, i, j = t
        kend = 128 * (i + 1)
        d0 = 128 * i
        if not fox:
            P.op("act", lambda e: e.activation(out=L[j][:, 0:kend], in_=C[j][:, 0:kend], func=AF.Exp,
                                               bias=sm[j][:, 0:1], scale=1.0),
                 reads=[Cb[j], smb[j]], writes=[Lb[j]])
            P.op(SB_MULT_ENG, lambda e: e.tensor_tensor(out=W[j][:, 0:kend], in0=E[j][:, 0:kend], in1=L[j][:, 0:kend],
                                                        op=ALU.mult), reads=[Eb[j], Lb[j]], writes=[Wb[j]])
        else:
            P.op("act", lambda e: e.activation(
                out=W[j][:, 0:kend], in_=E[j][:, 0:kend], func=AF.Exp, bias=sm[j][:, 0:1], scale=1.0,
                accum_out=sr[j][:, 1:2]), reads=[Eb[j], smb[j]], writes=[Wb[j], srb[j]])
            P.op("dve", lambda e: e.reciprocal(out=sr[j][:, 2:3], in_=sr[j][:, 1:2]),
                 reads=[srb[j]], writes=[srb[j]])

        def fnt(e):
            inst = None
            for a in range(i + 1):
                inst = e.transpose(out=tr[:, a * 128:(a + 1) * 128], in_=W[j][:, a * 128:(a + 1) * 128],
                                   identity=env.identb[:, :])
            return inst
        P.op("pe", fnt, reads=[Wb[j], env.identb_b], writes=[trb])

    def stC(t, part):
        kh, g, h, qi, i, j = t
        ki = kh % 2
        kend = 128 * (i + 1)
        d0 = 128 * i
        o_t, o_tb = oT[qi], oTb[qi]
        if part == 1:
            if not fox and SB_COPY_ENG == "dve":
                P.op("dve", lambda e: e.tensor_copy(out=WT[j][:, 0:kend], in_=tr[:, 0:kend]),
                     reads=[trb], writes=[WTb[j]])
            else:
                P.op("act", lambda e: e.activation(out=WT[j][:, 0:kend], in_=tr[:, 0:kend], func=AF.Copy),
                     reads=[trb], writes=[WTb[j]])
        if not fox:
            def fno(e):
                inst = None
                for a in range(i + 1):
                    inst = e.matmul(ops_[:, 0:128], vsb[ki][:, a, :], WT[j][:, a * 128:(a + 1) * 128],
                                    start=(a == 0), stop=(a == i))
                return inst
            if part == 1:
                P.op("pe", fno, reads=[vsbb[ki], WTb[j]], writes=[opb])
                return
            P.op("dve", lambda e: e.tensor_copy(out=o_t[:, d0:d0 + 128], in_=ops_[:, 0:128]),
                 reads=[opb], writes=[o_tb], partial=True)
        else:
            def fno(e):
                inst = None
                for a in range(i + 1):
                    inst = e.matmul(ops_[:, 0:128], WT[j][:, a * 128:(a + 1) * 128], vsb[ki][:, a, :],
                                    start=(a == 0), stop=(a == i))
                return inst
            if part == 1:
                P.op("pe", fno, reads=[vsbb[ki], WTb[j]], writes=[opb])
                return
            P.op("act", lambda e: e.activation(out=osb[j][:, :], in_=ops_[:, 0:128], func=AF.Identity,
                                               scale=sr[j][:, 2:3]),
                 reads=[opb, srb[j]], writes=[osbb[j]])
            P.op("pe", lambda e: e.transpose(out=tr2[:, 0:128], in_=osb[j][:, :], identity=env.identb[:, :]),
                 reads=[osbb[j], env.identb_b], writes=[tr2b])
            P.op("dve", lambda e: e.tensor_copy(out=o_t[:, d0:d0 + 128], in_=tr2[:, 0:128]),
                 reads=[tr2b], writes=[o_tb], partial=True)
        if i == NTB - 1:
            P.dma("sp", OT[h * 128:(h + 1) * 128, :], o_t[:, :], reads=[o_tb], writes=[OTb], owner=o_tb, partial=True)

    its = []
    hq = 0
    n = 0
    for kh in range(nkv):
        for g in range(group):
            h = kh * group + g
            qi = hq % 2
            hq += 1
            for i in range(NTB):
                its.append((kh, g, h, qi, i, n % NS))
                n += 1
    N = len(its)

    def at(k):
        return its[k] if 0 <= k < N else None
    if fox:
        for r in range(-2, N):
            if at(r + 2):
                stA1(at(r + 2))
                stA2(at(r + 2))
            if at(r):
                stC(at(r), 1)
            if at(r + 1):
                stB(at(r + 1))
            if at(r):
                stC(at(r), 2)
    else:
        for r in range(-3, N):
            if at(r + 3):
                stA1(at(r + 3))
            if at(r + 2):
                stA2(at(r + 2), "b")
            if at(r):
                stC(at(r), 1)
            if at(r + 1):
                stB(at(r + 1))
            if at(r):
                stC(at(r), 2)
            if at(r + 3):
                stA2(at(r + 3), "a")


def phase_in(P, env, cfg, x, H0, H0b):
    S, D, KC, NTB = cfg.S, cfg.D, cfg.KC, cfg.NTB
    GB = 4
    HC = min(16, KC)
    xs = [P.sb("in_x%d" % i, [128, D], F32) for i in range(2)]
    xsb = [P.buf("in_x%d" % i) for i in range(2)]
    pst = [P.ps("in_p%d" % i, [128, HC * 128], F32) for i in range(2)]
    pstb = [P.buf("in_p%d" % i) for i in range(2)]
    stg = [P.sb("in_s%d" % i, [128, KC, 512], F32) for i in range(2)]
    stgb = [P.buf("in_s%d" % i) for i in range(2)]
    n = 0
    for g in range(NTB // GB):
        st, stb = stg[g % 2], stgb[g % 2]
        for tbi in range(GB):
            tb = g * GB + tbi
            xi = tb % 2
            P.dma("sp", xs[xi][:, :], x[tb * 128:(tb + 1) * 128, :], writes=[xsb[xi]], owner=xsb[xi])
            for hc in range(0, KC, HC):
                pi = n % 2
                n += 1

                def fn(e, xi=xi, pi=pi, hc=hc):
                    inst = None
                    for kc in range(hc, hc + HC):
                        inst = e.transpose(out=pst[pi][:, (kc - hc) * 128:(kc - hc + 1) * 128],
                                           in_=xs[xi][:, kc * 128:(kc + 1) * 128], identity=env.identf[:, :])
                    return inst
                P.op("pe", fn, reads=[xsb[xi], env.identf_b], writes=[pstb[pi]])
                src = pst[pi][:, :].rearrange("p (k t) -> p k t", t=128)
                dst = st[:, hc:hc + HC, tbi * 128:(tbi + 1) * 128]
                if n % 2:
                    P.op("act", lambda e, src=src, dst=dst: e.activation(out=dst, in_=src, func=AF.Copy),
                         reads=[pstb[pi]], writes=[stb], partial=True)
                else:
                    P.op("dve", lambda e, src=src, dst=dst: e.tensor_copy(out=dst, in_=src),
                         reads=[pstb[pi]], writes=[stb], partial=True)
        cols = slice(g * 512, (g + 1) * 512)
        P.dma("sp", H0[:, cols].rearrange("(kc p) t -> p kc t", p=128), st[:, :, :], reads=[stb], writes=[H0b],
              owner=stb, partial=True)
        P.op("act", lambda e, st=st: e.activation(out=st[:, :, :], in_=st[:, :, :], func=AF.Square),
             reads=[stb], writes=[stb])
        P.op("dve", lambda e, st=st, cols=cols: e.tensor_reduce(
            out=env.acc[:, cols], in_=st[:, :, :].rearrange("p k t -> p t k"), axis=AX.X, op=ALU.add),
            reads=[stb], writes=[env.acc_b], partial=True)


def phase_out(P, env, cfg, Hap, Hb, g_ap, out):
    S, D, KC, NTB = cfg.S, cfg.D, cfg.KC, cfg.NTB
    GB = 4
    HC = min(16, KC)
    pst = [P.ps("fo_p%d" % i, [128, HC * 128], F32) for i in range(2)]
    pstb = [P.buf("fo_p%d" % i) for i in range(2)]
    if HC * 128 >= S:
        sp_t, sp_b = pst[0], pstb[0]
    else:
        sp_t, sp_b = P.ps("fo_s", [128, S], F32), P.buf("fo_s")
    nun = (S + 1023) // 1024
    spus = [sp_t[:, j * 1024:min(S, (j + 1) * 1024)] for j in range(nun)]
    spubs = [sp_b] * nun
    norm_stats_acc(P, env, cfg, D, spus, spubs)
    gcol, gcolb = load_cols(P, env, g_ap, KC, "fo_g", pst[1], pstb[1])
    hs = [P.sb("fo_h%d" % i, [128, KC, 512], F32) for i in range(2)]
    hsb = [P.buf("fo_h%d" % i) for i in range(2)]
    ost = [P.sb("fo_o%d" % i, [128, D], F32) for i in range(2)]
    ostb = [P.buf("fo_o%d" % i) for i in range(2)]
    outb = Buf("out")
    n = 0
    for g in range(NTB // GB):
        h, hb = hs[g % 2], hsb[g % 2]
        cols = slice(g * 512, (g + 1) * 512)
        P.dma("sp", h[:, :, :], Hap[:, cols].rearrange("(kc p) t -> p kc t", p=128), reads=[Hb], writes=[hb], owner=hb)
        for kc in range(KC):
            P.op("dve", lambda e, h=h, kc=kc, cols=cols: e.scalar_tensor_tensor(
                out=h[:, kc, :], in0=h[:, kc, :], scalar=gcol[:, kc:kc + 1], in1=env.rstd[:, cols],
                op0=ALU.mult, op1=ALU.mult), reads=[hb, gcolb, env.rstd_b], writes=[hb])
        for tbi in range(GB):
            tb = g * GB + tbi
            o, ob = ost[tb % 2], ostb[tb % 2]
            for hc in range(0, KC, HC):
                pi = n % 2
                n += 1

                def fn(e, h=h, pi=pi, hc=hc, tbi=tbi):
                    inst = None
                    for kc in range(hc, hc + HC):
                        inst = e.transpose(out=pst[pi][:, (kc - hc) * 128:(kc - hc + 1) * 128],
                                           in_=h[:, kc, tbi * 128:(tbi + 1) * 128], identity=env.identf[:, :])
                    return inst
                P.op("pe", fn, reads=[hb, env.identf_b], writes=[pstb[pi]])
                P.op("act", lambda e, o=o, pi=pi, hc=hc: e.activation(
                    out=o[:, hc * 128:(hc + HC) * 128], in_=pst[pi][:, :], func=AF.Copy),
                    reads=[pstb[pi]], writes=[ob], partial=True)
            P.dma("sp", out[tb * 128:(tb + 1) * 128, :], o[:, :], reads=[ob], writes=[outb], owner=ob, partial=True)


PARAMS = [("a_attn_norm", "D"), ("a_w_qkv", "D,3D"), ("a_w_o", "D,D"), ("a_ffn_norm", "D"), ("a_w_up", "D,2F"),
          ("a_w_conv", "3,F"), ("a_b_conv", "F"), ("a_w_down", "F,D"), ("kv_norm", "D"), ("w_kv", "D,KV"),
          ("w_f", "D,H"), ("b_f", "H"), ("b_attn_norm", "D"), ("b_w_q", "D,D"), ("b_w_o", "D,D"),
          ("b_ffn_norm", "D"), ("b_w_up", "D,2F"), ("b_w_conv", "3,F"), ("b_b_conv", "F"), ("b_w_down", "F,D"),
          ("final_norm", "D")]


def param_shape(cfg, spec):
    m = {"D": cfg.D, "3D": 3 * cfg.D, "2F": 2 * cfg.F, "F": cfg.F, "KV": 2 * cfg.KVH * 128, "H": cfg.H, "3": 3}
    return [m[s] for s in spec.split(",")]


def build(cfg, dbg=(), stop_after=None, dbg_att=None):
    nc = bass.Bass("TRN2", target_bir_lowering=False)
    D, S, H, KVH, G, F, KC, FC = cfg.D, cfg.S, cfg.H, cfg.KVH, cfg.G, cfg.F, cfg.KC, cfg.FC
    x = nc.dram_tensor("x", [S, D], F32, kind="ExternalInput").ap()
    prm = {}
    for name, spec in PARAMS:
        prm[name] = nc.dram_tensor(name, param_shape(cfg, spec), F32, kind="ExternalInput").ap()
    out = nc.dram_tensor("out", [S, D], F32, kind="ExternalOutput").ap()
    sbufs = {}

    def scratch(name, shape, dt):
        kind = "ExternalOutput" if name in dbg else "Internal"
        sbufs[name] = Buf(name)
        return nc.dram_tensor(name, shape, dt, kind=kind).ap()
    Hs = [scratch("H%d" % i, [D, S], F32) for i in range(5)]
    Hb = [sbufs["H%d" % i] for i in range(5)]
    QKVT = scratch("QKVT", [3 * D, S], BF16)
    OTA = scratch("OTA", [D, S], BF16)
    ACTA = scratch("ACTA", [S // 512, 128, FC * 512], BF16)
    KVT = scratch("KVT", [2 * KVH * 128, S], BF16)
    NEGC = scratch("NEGC", [H, S], F32)
    QT = scratch("QT", [D, S], BF16)
    OTB = scratch("OTB", [D, S], BF16)
    ACTB = scratch("ACTB", [S // 512, 128, FC * 512], BF16)
    KHh = (FC + 1) // 2
    WCA = scratch("WCA", [2 * (D // 256), 128, KHh * 256], BF16)
    WCB = scratch("WCB", [2 * (D // 256), 128, KHh * 256], BF16)

    class Stop(Exception):
        pass

    def chk(tag):
        if stop_after == tag:
            P.dead = True

    with ExitStack() as top:
        P = Prog(nc, top)
        env = Env()
        env.dbg_att = dbg_att
        env.rstd = P.sb("rstd", [128, S], F32)
        env.rstd_b = P.buf("rstd")
        env.acc = P.sb("acc", [128, S], F32)
        env.acc_b = P.buf("acc")
        try:
            make_consts(P, env)
            with P.phase("init"):
                phase_in(P, env, cfg, x, Hs[0], Hb[0])
            chk("in")

            def units():
                pus = [P.ps("pu%d" % i, [128, 1024], F32) for i in range(4)]
                pubs = [P.buf("pu%d" % i) for i in range(4)]
                return pus, pubs

            def normed(Hi, gname, stats=True):
                with P.phase("norm"):
                    pus, pubs = units()
                    if stats:
                        norm_stats_acc(P, env, cfg, D, pus[0:2], pubs[0:2])
                    norm_apply(P, env, cfg, Hs[Hi], Hb[Hi], prm[gname], KC, xn, xnb, pus[2], pubs[2])

            def ffn(Hi, pre, ACT, ACTb):
                WC = WCA if pre == "a" else WCB
                WCb = sbufs["WCA" if pre == "a" else "WCB"]
                normed(Hi, pre + "_ffn_norm")
                with P.phase("up"):
                    pus, pubs = units()
                    up_proj(P, env, cfg, xn, xnb, prm[pre + "_w_up"], prm[pre + "_w_conv"], prm[pre + "_b_conv"],
                            ACT, ACTb, pus, pubs, conv_jobs=wd_convert_jobs(cfg, prm[pre + "_w_down"], WC), WCb=WCb)
                chk(pre + "_up")
                with P.phase("down"):
                    pus, pubs = units()
                    down_proj(P, env, cfg, xflat, xnb, ACT, ACTb, prm[pre + "_w_down"], Hs[Hi], Hb[Hi],
                              Hs[Hi + 1], Hb[Hi + 1], pus, pubs, WC=WC, WCb=WCb)
                chk(pre + "_down")

            def wo(OT, OTb, wname, Hi):
                with P.phase("wo"):
                    pus, pubs = units()
                    load_at(P, xn, xnb, OT, KC, [OTb])
                    pre_, epi = resid_epilogue(P, env, cfg, Hs[Hi], Hb[Hi], Hs[Hi + 1], Hb[Hi + 1], "wo")
                    gemm_plain(P, cfg, xn, xnb, KC, prm[wname], D, epi, pus, pubs, pre=pre_)

            with P.scope():
                xflat = P.sb("xn", [128, max(KC * S, FC * 512)], BF16)
                xn = xflat[:, 0:KC * S].rearrange("p (kc t) -> p kc t", t=S)
                xnb = P.buf("xn")
                normed(0, "a_attn_norm")
                with P.phase("qkv"):
                    pus, pubs = units()
                    epi = store_epilogue(P, cfg, QKVT, sbufs["QKVT"], "qkv")
                    gemm_plain(P, cfg, xn, xnb, KC, prm["a_w_qkv"], 3 * D, epi, pus, pubs)
            chk("qkv")
            with P.phase("attA"):
                attention(P, env, cfg, "sb", QKVT[0:D, :], QKVT[D:2 * D, :], QKVT[2 * D:3 * D, :], [sbufs["QKVT"]],
                          H, 1, OTA, sbufs["OTA"])
            chk("attA")
            with P.scope():
                xflat = P.sb("xn", [128, max(KC * S, FC * 512)], BF16)
                xn = xflat[:, 0:KC * S].rearrange("p (kc t) -> p kc t", t=S)
                xnb = P.buf("xn")
                wo(OTA, sbufs["OTA"], "a_w_o", 0)
                chk("woA")
                ffn(1, "a", ACTA, sbufs["ACTA"])
            with P.scope():
                xflat = P.sb("xn", [128, max(KC * S, FC * 512)], BF16)
                xn = xflat[:, 0:KC * S].rearrange("p (kc t) -> p kc t", t=S)
                xnb = P.buf("xn")
                normed(2, "kv_norm")
                with P.phase("kv"):
                    pus, pubs = units()
                    epi = store_epilogue(P, cfg, KVT, sbufs["KVT"], "kv")
                    gemm_plain(P, cfg, xn, xnb, KC, prm["w_kv"], 2 * KVH * 128, epi, pus, pubs)
                    forget_gates(P, env, cfg, xn, xnb, prm["w_f"], prm["b_f"], NEGC, sbufs["NEGC"], pus[0], pubs[0])
                chk("kv")
                normed(2, "b_attn_norm", stats=False)
                with P.phase("q"):
                    pus, pubs = units()
                    epi = store_epilogue(P, cfg, QT, sbufs["QT"], "q")
                    gemm_plain(P, cfg, xn, xnb, KC, prm["b_w_q"], D, epi, pus, pubs)
            chk("q")
            with P.phase("attB"):
                attention(P, env, cfg, "fox", QT, KVT[0:KVH * 128, :], KVT[KVH * 128:2 * KVH * 128, :],
                          [sbufs["QT"], sbufs["KVT"]], KVH, G, OTB, sbufs["OTB"], NEGC, sbufs["NEGC"])
            chk("attB")
            with P.scope():
                xflat = P.sb("xn", [128, max(KC * S, FC * 512)], BF16)
                xn = xflat[:, 0:KC * S].rearrange("p (kc t) -> p kc t", t=S)
                xnb = P.buf("xn")
                wo(OTB, sbufs["OTB"], "b_w_o", 2)
                chk("woB")
                ffn(3, "b", ACTB, sbufs["ACTB"])
            with P.phase("final"):
                phase_out(P, env, cfg, Hs[4], Hb[4], prm["final_norm"], out)
        except Stop:
            pass
        P.dead = False
        with P.phase("end"):
            P.finish()
    nc._prog_ninst = P.n_inst
    return nc


def load_at(P, at, atb, src_ap, KC, srcbufs):
    src = src_ap.rearrange("(kc p) t -> p kc t", p=128)
    for c in range(0, KC, 8):
        ce = min(KC, c + 8)
        P.dma("sp", at[:, c:ce, :], src[:, c:ce, :], reads=srcbufs, writes=[atb], owner=atb,
              partial=c > 0, group_cont=c > 0)


_CFG = Cfg()
_NC_CACHE = {}


def kernel(**inputs):
    cfg = _CFG
    B = inputs["x"].shape[0]
    if "nc" not in _NC_CACHE:
        _NC_CACHE["nc"] = build(cfg)
    nc = _NC_CACHE["nc"]
    shared = {}
    for name, spec in PARAMS:
        shared[name] = np.ascontiguousarray(np.asarray(inputs[name], dtype=np.float32).reshape(param_shape(cfg, spec)))
    x = np.asarray(inputs["x"], dtype=np.float32)
    in_maps = []
    for b in range(B):
        m = dict(shared)
        m["x"] = np.ascontiguousarray(x[b])
        in_maps.append(m)
    res = run_bass_kernel_spmd(nc, in_maps, core_ids=list(range(B)))
    return np.stack([np.asarray(r["out"]) for r in res.results], axis=0).astype(np.float32)
```
